# Optimizing a Trainium2 kernel written in Bass

```python
import math
import jax, jax.numpy as jnp
from jax import lax
import numpy as np

D_MODEL = 1024
BATCH = 4
SEQ = 4096
DEPTH = 4

HEAD_DIM = 64
ROPE_THETA = 10000.0
Q_BLOCK = 128
NORM_EPS = 1e-6
NEG_INF = -1e30
BIG = 1e30

NSA_HEADS = 8
NSA_GROUPS = 2
NSA_HPG = NSA_HEADS // NSA_GROUPS
CMP_BLOCK = 32
CMP_STRIDE = 16
SEL_BLOCK = 64
SEL_TOPK = 16
WINDOW = 512
NSA_Q_CHUNK = 64

DIFF_HEADS = 4
DIFF_VDIM = 2 * HEAD_DIM

SB_HEADS = D_MODEL // HEAD_DIM
SB_WIDTH = SB_HEADS * HEAD_DIM

FFN_HIDDEN = -(-(8 * D_MODEL) // (3 * 256)) * 256

NSA_Q_W = NSA_HEADS * HEAD_DIM
NSA_KV_W = NSA_GROUPS * HEAD_DIM
NSA_GATE_W = 3 * NSA_HEADS
DIFF_QK_W = DIFF_HEADS * 2 * HEAD_DIM
DIFF_V_W = DIFF_HEADS * DIFF_VDIM
EVEN_SIZES = (NSA_Q_W, NSA_KV_W, NSA_KV_W, NSA_KV_W, NSA_KV_W, NSA_KV_W, NSA_KV_W,
              NSA_GATE_W, DIFF_QK_W, DIFF_QK_W, DIFF_V_W)
EVEN_IN = sum(EVEN_SIZES)
EVEN_OUT = NSA_HEADS * HEAD_DIM + DIFF_HEADS * DIFF_VDIM

N_EVEN = (DEPTH + 1) // 2
N_ODD = DEPTH // 2

kernel_name = "nsa_diff_stickbreaking_hybrid"


def rmsnorm(x, g):
    xf = x.astype(jnp.float32)
    y = xf * lax.rsqrt(jnp.mean(xf * xf, axis=-1, keepdims=True) + NORM_EPS)
    return (y * g.astype(jnp.float32)).astype(x.dtype)


def rope_tables(seq, dim):
    inv = 1.0 / (ROPE_THETA ** (jnp.arange(0, dim, 2, dtype=jnp.float32) / dim))
    ang = jnp.arange(seq, dtype=jnp.float32)[:, None] * inv[None, :]
    return jnp.cos(ang), jnp.sin(ang)


def apply_rope(x, cos, sin):
    x1, x2 = jnp.split(x, 2, axis=-1)
    c = cos[None, :, None, :].astype(x.dtype)
    s = sin[None, :, None, :].astype(x.dtype)
    return jnp.concatenate([x1 * c - x2 * s, x1 * s + x2 * c], axis=-1)


def masked_softmax(s, mask):
    s = jnp.where(mask, s.astype(jnp.float32), NEG_INF)
    return jnp.where(mask, jax.nn.softmax(s, axis=-1), 0.0)


def nsa_mixer(q, kc, vc, ks, vs, kw, vw, gate_logits,
              cmp_pos_k, cmp_w_k, cmp_pos_v, cmp_w_v, cos, sin):
    B, S = q.shape[0], q.shape[1]
    G, HPG, D = NSA_GROUPS, NSA_HPG, HEAD_DIM
    scale = D ** -0.5
    q = apply_rope(q, cos, sin)
    kc, ks, kw = apply_rope(kc, cos, sin), apply_rope(ks, cos, sin), apply_rope(kw, cos, sin)
    qg = q.reshape(B, S, G, HPG, D).transpose(0, 2, 3, 1, 4)
    kc, vc, ks, vs, kw, vw = [a.transpose(0, 2, 1, 3) for a in (kc, vc, ks, vs, kw, vw)]
    pos = jnp.arange(S)

    n_cmp = (S - CMP_BLOCK) // CMP_STRIDE + 1
    cmp_start = jnp.arange(n_cmp) * CMP_STRIDE
    cmp_idx = cmp_start[:, None] + jnp.arange(CMP_BLOCK)[None, :]

    def compress(a, p, w):
        blocks = a[:, :, cmp_idx] + p
        return blocks.reshape(B, G, n_cmp, CMP_BLOCK * D) @ w

    k_cmp = compress(kc, cmp_pos_k, cmp_w_k)
    v_cmp = compress(vc, cmp_pos_v, cmp_w_v)
    s_cmp = jnp.einsum('bghsd,bgnd->bghsn', qg, k_cmp) * scale
    cmp_mask = (cmp_start + CMP_BLOCK - 1)[None, :] <= pos[:, None]
    p_cmp = masked_softmax(s_cmp, cmp_mask)
    o_cmp = jnp.einsum('bghsn,bgnd->bghsd', p_cmp.astype(v_cmp.dtype), v_cmp)

    n_slc = S // SEL_BLOCK
    sel_start = jnp.arange(n_slc) * SEL_BLOCK
    cmp_to_sel = ((cmp_start[:, None] < sel_start[None, :] + SEL_BLOCK) &
                  (cmp_start[:, None] + CMP_BLOCK > sel_start[None, :])).astype(jnp.float32)
    imp = jnp.einsum('bghsn,nj->bgsj', p_cmp, cmp_to_sel)
    qblk = pos // SEL_BLOCK
    jb = jnp.arange(n_slc)
    forced = (jb[None, :] == 0) | (jb[None, :] == qblk[:, None]) | (jb[None, :] == qblk[:, None] - 1)
    future = jb[None, :] > qblk[:, None]
    imp = jnp.where(forced, BIG, jnp.where(future, -BIG, imp))
    n_top = min(SEL_TOPK, n_slc)
    top_val, top_idx = lax.top_k(imp, n_top)
    top_ok = top_val >= 0.0

    Qc = NSA_Q_CHUNK
    nc = S // Qc
    ks_blk = ks.reshape(B, G, n_slc, SEL_BLOCK, D)
    vs_blk = vs.reshape(B, G, n_slc, SEL_BLOCK, D)
    kw_pad = jnp.pad(kw, ((0, 0), (0, 0), (WINDOW, 0), (0, 0)))
    vw_pad = jnp.pad(vw, ((0, 0), (0, 0), (WINDOW, 0), (0, 0)))
    q_ch = qg.reshape(B, G, HPG, nc, Qc, D).transpose(3, 0, 1, 2, 4, 5)
    idx_ch = top_idx.reshape(B, G, nc, Qc, n_top).transpose(2, 0, 1, 3, 4)
    ok_ch = top_ok.reshape(B, G, nc, Qc, n_top).transpose(2, 0, 1, 3, 4)
    bi = jnp.arange(B)[:, None, None, None]
    gi = jnp.arange(G)[None, :, None, None]
    offs = jnp.arange(SEL_BLOCK)
    n_sel_keys = n_top * SEL_BLOCK

    def chunk(args):
        c, q_c, idx_c, ok_c = args
        t = c * Qc + jnp.arange(Qc)
        k_sel = ks_blk[bi, gi, idx_c].reshape(B, G, Qc, n_sel_keys, D)
        v_sel = vs_blk[bi, gi, idx_c].reshape(B, G, Qc, n_sel_keys, D)
        kpos = idx_c[..., None] * SEL_BLOCK + offs
        m_sel = (ok_c[..., None] & (kpos <= t[None, None, :, None, None])).reshape(B, G, Qc, n_sel_keys)
        s_sel = jnp.einsum('bghqd,bgqnd->bghqn', q_c, k_sel) * scale
        p_sel = masked_softmax(s_sel, m_sel[:, :, None])
        o_sel = jnp.einsum('bghqn,bgqnd->bghqd', p_sel.astype(v_sel.dtype), v_sel)
        k_w = lax.dynamic_slice_in_dim(kw_pad, c * Qc, Qc + WINDOW, axis=2)
        v_w = lax.dynamic_slice_in_dim(vw_pad, c * Qc, Qc + WINDOW, axis=2)
        wpos = c * Qc - WINDOW + jnp.arange(Qc + WINDOW)
        m_w = ((wpos[None, :] <= t[:, None]) & (wpos[None, :] > t[:, None] - WINDOW)
               & (wpos[None, :] >= 0))
        s_w = jnp.einsum('bghqd,bgkd->bghqk', q_c, k_w) * scale
        p_w = masked_softmax(s_w, m_w)
        o_w = jnp.einsum('bghqk,bgkd->bghqd', p_w.astype(v_w.dtype), v_w)
        return o_sel, o_w

    o_sel, o_win = lax.map(chunk, (jnp.arange(nc), q_ch, idx_ch, ok_ch))
    o_sel = o_sel.transpose(1, 0, 4, 2, 3, 5).reshape(B, S, NSA_HEADS, D)
    o_win = o_win.transpose(1, 0, 4, 2, 3, 5).reshape(B, S, NSA_HEADS, D)
    o_cmp = o_cmp.transpose(0, 3, 1, 2, 4).reshape(B, S, NSA_HEADS, D)

    g = jax.nn.sigmoid(gate_logits)
    o = g[..., 0:1] * o_cmp + g[..., 1:2] * o_sel + g[..., 2:3] * o_win
    return o.reshape(B, S, NSA_HEADS * D)


def diff_mixer(q, k, v, lq1, lk1, lq2, lk2, subln, lam_init, cos, sin):
    B, S = q.shape[0], q.shape[1]
    H, D = DIFF_HEADS, HEAD_DIM
    scale = D ** -0.5
    heads = lambda a: a.transpose(0, 2, 1, 3)
    q1 = heads(apply_rope(q[:, :, :, 0], cos, sin))
    q2 = heads(apply_rope(q[:, :, :, 1], cos, sin))
    k1 = heads(apply_rope(k[:, :, :, 0], cos, sin))
    k2 = heads(apply_rope(k[:, :, :, 1], cos, sin))
    vh = heads(v)
    lam = (jnp.exp(jnp.sum(lq1.astype(jnp.float32) * lk1.astype(jnp.float32)))
           - jnp.exp(jnp.sum(lq2.astype(jnp.float32) * lk2.astype(jnp.float32))) + lam_init)
    nb = S // Q_BLOCK
    q1b = q1.reshape(B, H, nb, Q_BLOCK, D).transpose(2, 0, 1, 3, 4)
    q2b = q2.reshape(B, H, nb, Q_BLOCK, D).transpose(2, 0, 1, 3, 4)
    kpos = jnp.arange(S)

    def blk(args):
        i, a1, a2 = args
        t = i * Q_BLOCK + jnp.arange(Q_BLOCK)
        m = kpos[None, :] <= t[:, None]
        p1 = masked_softmax(jnp.einsum('bhqd,bhkd->bhqk', a1, k1) * scale, m)
        p2 = masked_softmax(jnp.einsum('bhqd,bhkd->bhqk', a2, k2) * scale, m)
        attn = p1 - lam * p2
        return jnp.einsum('bhqk,bhkd->bhqd', attn.astype(vh.dtype), vh)

    o = lax.map(blk, (jnp.arange(nb), q1b, q2b))
    o = o.transpose(1, 0, 3, 2, 4).reshape(B, S, H, DIFF_VDIM)
    o = rmsnorm(o, subln) * (1.0 - lam_init)
    return o.reshape(B, S, H * DIFF_VDIM)


def even_mixer(h, w_in, w_out, cmp_pos_k, cmp_w_k, cmp_pos_v, cmp_w_v,
               lq1, lk1, lq2, lk2, subln, lam_init, cos, sin):
    B, S, _ = h.shape
    splits = np.cumsum(EVEN_SIZES)[:-1].tolist()
    parts = jnp.split(h @ w_in, splits, axis=-1)
    nq, kc, vc, ks, vs, kw, vw, gl, dq, dk, dv = parts
    kvs = [a.reshape(B, S, NSA_GROUPS, HEAD_DIM) for a in (kc, vc, ks, vs, kw, vw)]
    o_nsa = nsa_mixer(nq.reshape(B, S, NSA_HEADS, HEAD_DIM), *kvs,
                      gl.reshape(B, S, NSA_HEADS, 3),
                      cmp_pos_k, cmp_w_k, cmp_pos_v, cmp_w_v, cos, sin)
    o_diff = diff_mixer(dq.reshape(B, S, DIFF_HEADS, 2, HEAD_DIM),
                        dk.reshape(B, S, DIFF_HEADS, 2, HEAD_DIM),
                        dv.reshape(B, S, DIFF_HEADS, DIFF_VDIM),
                        lq1, lk1, lq2, lk2, subln, lam_init, cos, sin)
    return jnp.concatenate([o_nsa, o_diff], axis=-1) @ w_out


def sb_mixer(h, w_in, w_out):
    B, S, _ = h.shape
    scale = HEAD_DIM ** -0.5
    q, k, v = jnp.split(h @ w_in, 3, axis=-1)
    heads = lambda a: a.reshape(B, S, SB_HEADS, HEAD_DIM).transpose(0, 2, 1, 3)
    q, k, v = heads(q), heads(k), heads(v)
    nb = S // Q_BLOCK
    qb = q.reshape(B, SB_HEADS, nb, Q_BLOCK, HEAD_DIM).transpose(2, 0, 1, 3, 4)
    kpos = jnp.arange(S)

    def blk(args):
        i, q_c = args
        t = i * Q_BLOCK + jnp.arange(Q_BLOCK)
        strict = kpos[None, :] < t[:, None]
        z = jnp.einsum('bhqd,bhkd->bhqk', q_c, k).astype(jnp.float32) * scale
        log_rest = jnp.where(strict, jax.nn.log_sigmoid(-z), 0.0)
        suffix = lax.cumsum(log_rest, axis=3, reverse=True) - log_rest
        a = jnp.where(strict, jnp.exp(jax.nn.log_sigmoid(z) + suffix), 0.0)
        return jnp.einsum('bhqk,bhkd->bhqd', a.astype(v.dtype), v)

    o = lax.map(blk, (jnp.arange(nb), qb))
    o = o.transpose(1, 0, 3, 2, 4).reshape(B, S, SB_WIDTH)
    return o @ w_out


def swiglu(h, w_gate, w_up, w_down):
    return (jax.nn.silu(h @ w_gate) * (h @ w_up)) @ w_down


def _nrm(key, shape, scale):
    return jax.random.normal(key, shape, jnp.float32) * scale


def setup_inputs(seed: int = 0) -> dict:
    key = jax.random.key(seed)
    ks = jax.random.split(key, 24)
    d = D_MODEL
    return {
        "x": _nrm(ks[0], (BATCH, SEQ, d), 1.0),
        "norm_mix": 1.0 + _nrm(ks[1], (DEPTH, d), 0.02),
        "norm_ffn": 1.0 + _nrm(ks[2], (DEPTH, d), 0.02),
        "norm_final": 1.0 + _nrm(ks[3], (d,), 0.02),
        "even_w_in": _nrm(ks[4], (N_EVEN, d, EVEN_IN), d ** -0.5),
        "even_w_out": _nrm(ks[5], (N_EVEN, EVEN_OUT, d), EVEN_OUT ** -0.5),
        "cmp_pos_k": _nrm(ks[6], (N_EVEN, CMP_BLOCK, HEAD_DIM), 0.3),
        "cmp_w_k": _nrm(ks[7], (N_EVEN, CMP_BLOCK * HEAD_DIM, HEAD_DIM), (CMP_BLOCK * HEAD_DIM) ** -0.5),
        "cmp_pos_v": _nrm(ks[8], (N_EVEN, CMP_BLOCK, HEAD_DIM), 0.3),
        "cmp_w_v": _nrm(ks[9], (N_EVEN, CMP_BLOCK * HEAD_DIM, HEAD_DIM), (CMP_BLOCK * HEAD_DIM) ** -0.5),
        "diff_lq1": _nrm(ks[10], (N_EVEN, HEAD_DIM), 0.1),
        "diff_lk1": _nrm(ks[11], (N_EVEN, HEAD_DIM), 0.1),
        "diff_lq2": _nrm(ks[12], (N_EVEN, HEAD_DIM), 0.1),
        "diff_lk2": _nrm(ks[13], (N_EVEN, HEAD_DIM), 0.1),
        "diff_subln": 1.0 + _nrm(ks[14], (N_EVEN, DIFF_VDIM), 0.02),
        "odd_w_in": _nrm(ks[15], (N_ODD, d, 3 * SB_WIDTH), d ** -0.5),
        "odd_w_out": _nrm(ks[16], (N_ODD, SB_WIDTH, d), SB_WIDTH ** -0.5),
        "ffn_w_gate": _nrm(ks[17], (DEPTH, d, FFN_HIDDEN), d ** -0.5),
        "ffn_w_up": _nrm(ks[18], (DEPTH, d, FFN_HIDDEN), d ** -0.5),
        "ffn_w_down": _nrm(ks[19], (DEPTH, FFN_HIDDEN, d), FFN_HIDDEN ** -0.5),
    }


def reference(x, norm_mix, norm_ffn, norm_final, even_w_in, even_w_out,
              cmp_pos_k, cmp_w_k, cmp_pos_v, cmp_w_v,
              diff_lq1, diff_lk1, diff_lq2, diff_lk2, diff_subln,
              odd_w_in, odd_w_out, ffn_w_gate, ffn_w_up, ffn_w_down):
    S = x.shape[1]
    cos, sin = rope_tables(S, HEAD_DIM)
    for layer in range(DEPTH):
        h = rmsnorm(x, norm_mix[layer])
        if layer % 2 == 0:
            e = layer // 2
            lam_init = 0.8 - 0.6 * math.exp(-0.3 * layer)
            y = even_mixer(h, even_w_in[e], even_w_out[e],
                           cmp_pos_k[e], cmp_w_k[e], cmp_pos_v[e], cmp_w_v[e],
                           diff_lq1[e], diff_lk1[e], diff_lq2[e], diff_lk2[e], diff_subln[e],
                           lam_init, cos, sin)
        else:
            o = layer // 2
            y = sb_mixer(h, odd_w_in[o], odd_w_out[o])
        x = x + y
        h = rmsnorm(x, norm_ffn[layer])
        x = x + swiglu(h, ffn_w_gate[layer], ffn_w_up[layer], ffn_w_down[layer])
    return rmsnorm(x, norm_final)
```

```python
import math
import numpy as np
import ml_dtypes
from contextlib import ExitStack
import concourse.bass as bass
import concourse.mybir as mybir
from concourse.bass_utils import run_bass_kernel_spmd

F32 = mybir.dt.float32
BF16 = mybir.dt.bfloat16
AF = mybir.ActivationFunctionType
ALU = mybir.AluOpType
AX = mybir.AxisListType
NPBF16 = ml_dtypes.bfloat16

SEM_LIMIT = 30000
N_DMA_SEMS = {"sp": 14, "pool": 6, "act": 4}

D_MODEL = 1024
SEQ = 4096
BATCH = 4
DEPTH = 4
HD = 64
FFN_H = 2816
NFC = FFN_H // 128
EVEN_IN = 2840
NORM_EPS = 1e-6
NT = 2048


class Dep:
    __slots__ = ("last_w", "readers")

    def __init__(self):
        self.last_w = None
        self.readers = []


class V:
    __slots__ = ("tl", "ap")

    def __init__(self, tl, ap):
        self.tl = tl
        self.ap = ap

    def __getitem__(self, idx):
        return V(self.tl, self.ap[idx])

    def rearrange(self, pattern, **kw):
        return V(self.tl, self.ap.rearrange(pattern, **kw))

    def bitcast(self, dt):
        return V(self.tl, self.ap.bitcast(dt))

    def to_broadcast(self, shape):
        return V(self.tl, self.ap.to_broadcast(list(shape)))

    def unsqueeze(self, ax):
        return V(self.tl, self.ap.unsqueeze(ax))

    @property
    def shape(self):
        return self.ap.shape


class Tl:
    def __init__(self, h, name, is_ap=False):
        self.h = h
        self.name = name
        self.dep = Dep()
        self.is_ap = is_ap

    def __getitem__(self, idx):
        return V(self, self.h[idx])

    def all(self):
        return V(self, self.h[:] if not self.is_ap else self.h)

    def __hash__(self):
        return id(self)


class Op:
    __slots__ = ("eng", "fn", "deps", "signal", "event", "is_dma", "dma_prev", "idx", "inc")


def _ap(v):
    return v.ap if isinstance(v, V) else v


class Prog:
    ENGS = ("pe", "act", "dve", "pool", "sp")

    def __init__(self):
        self.nc = bass.Bass("TRN2", target_bir_lowering=False)
        self.es = ExitStack()
        self.ops = {e: [] for e in self.ENGS}
        self.nops = 0
        self.same_engine_sync = True
        self._uid = 0
        self.arena = None
        self.arena_off = 0
        self.arena_base = 0
        self.ps_pool = None
        self.ps_i = 0
        self.dram_override = {}
        self.dma_pending = []
        self.bar = None

    def dram(self, name, shape, dt, kind):
        if name in self.dram_override:
            return self.dram_override[name]
        return self.dram_real(name, shape, dt, kind)

    def dram_real(self, name, shape, dt, kind):
        h = self.nc.dram_tensor(name, list(shape), dt, kind=kind)
        return Tl(h.ap(), name, is_ap=True)

    def sbuf(self, name, shape, dt):
        if self.arena is not None:
            esz = {F32: 4, BF16: 2}[dt]
            free = 1
            for d in shape[1:]:
                free *= d
            nb = (free * esz + 63) // 64 * 64
            off = self.arena_off
            assert off + nb <= self.arena_size, ("arena overflow", name, off, nb)
            self.arena_off = off + nb
            self.arena_peak = max(getattr(self, "arena_peak", 0), self.arena_off)
            ap = self.arena[0:shape[0], off:off + free * esz].bitcast(dt)
            if len(shape) == 3:
                ap = ap.rearrange("p (a b) -> p a b", a=shape[1])
            elif len(shape) == 4:
                ap = ap.rearrange("p (a b c) -> p a b c", a=shape[1], b=shape[2])
            return Tl(ap, name, is_ap=True)
        h = self.es.enter_context(self.nc.sbuf_tensor(name, list(shape), dt))
        return Tl(h, name)

    def psum(self, name, shape, dt):
        if self.ps_pool is not None:
            assert list(shape) == [128, 512] and dt == F32
            t = self.ps_pool[self.ps_i % 8]
            self.ps_i += 1
            return t
        h = self.es.enter_context(self.nc.psum_tensor(name, list(shape), dt))
        return Tl(h, name)

    def enable_arena(self, nbytes):
        U8 = mybir.dt.uint8
        h = self.es.enter_context(self.nc.sbuf_tensor("arena", [128, nbytes], U8))
        self.arena_size = nbytes
        self.bar = {e: Tl(self.es.enter_context(self.nc.sbuf_tensor(f"bar_{e}", [128, 8], F32)), f"bar_{e}")
                    for e in ("act", "dve", "pool", "sp", "src", "w_act", "w_dve", "w_pool", "w_sp")}
        self.bar_lhs = Tl(self.es.enter_context(self.nc.sbuf_tensor("bar_lhs", [128, 8], BF16)), "bar_lhs")
        self.ps_pool = []
        self.ps_pair = {}
        for j in range(4):
            hh = self.es.enter_context(self.nc.psum_tensor(f"psd{j}", [128, 1024], F32))
            a = Tl(hh[:, 0:512], f"psb{2 * j}", is_ap=True)
            b_ = Tl(hh[:, 512:1024], f"psb{2 * j + 1}", is_ap=True)
            self.ps_pool += [a, b_]
            self.ps_pair[id(a)] = (hh[:, :], b_)
        self.arena = h[:]
        self.memset("pool", self.bar["src"][:], 0.0)
        self.memset("pool", self.bar_lhs[:], 0.0)

    def phase_reset(self):
        b = self.bar
        pend = list(self.dma_pending)
        self.dma_pending = []
        pst = self.ps_pool[7]
        m = {}
        m["act"] = self.copy("act", b["act"][:], b["src"][:])
        m["dve"] = self.copy("dve", b["dve"][:], b["src"][:])
        m["pool"] = self.copy("pool", b["pool"][:], b["src"][:])
        m["sp"] = self.dma(b["sp"][:], b["src"][:])
        m["pe"] = self.mm(pst[0:8, 0:8], self.bar_lhs[:], self.bar_lhs[:], True, True)
        allr = [b["act"][:], b["dve"][:], b["pool"][:], b["sp"][:], pst[0:8, 0:8]]
        self.add("act", lambda e, o=b["w_act"][:].ap, i=b["src"][:].ap: e.copy(o, i), allr, [b["w_act"][:]], extra=pend)
        self.add("dve", lambda e, o=b["w_dve"][:].ap, i=b["src"][:].ap: e.tensor_copy(o, i), allr, [b["w_dve"][:]], extra=pend)
        self.add("pool", lambda e, o=b["w_pool"][:].ap, i=b["src"][:].ap: e.tensor_copy(o, i), allr, [b["w_pool"][:]], extra=pend)
        self.add("sp", lambda e, o=b["w_sp"][:].ap, i=b["src"][:].ap: e.dma_start(out=o, in_=i), allr, [b["w_sp"][:]],
                 dma=True, extra=pend)
        self.add("pe", lambda e, o=pst[0:8, 8:16].ap, l=self.bar_lhs[:].ap: e.matmul(o, l, l, start=True, stop=True),
                 allr, [pst[0:8, 8:16]], extra=pend)
        self.arena_off = self.arena_base
        self.ps_i = 0

    def ring(self, name, n, shape, dt, space="sbuf"):
        f = self.sbuf if space == "sbuf" else self.psum
        return Ring([f(f"{name}{i}", shape, dt) for i in range(n)])

    def add(self, eng, fn, reads=(), writes=(), dma=False, extra=()):
        op = Op()
        op.eng = eng
        op.fn = fn
        op.is_dma = dma
        op.signal = False
        op.event = None
        op.dma_prev = None
        op.inc = 16
        op.idx = self.nops
        self.nops += 1
        deps = {}
        rt = []
        wt = []
        for r in reads:
            if isinstance(r, V):
                rt.append(r.tl)
        for w in writes:
            if isinstance(w, V):
                wt.append(w.tl)
        for r in rt:
            lw = r.dep.last_w
            if lw is not None:
                deps[lw.idx] = lw
        for w in wt:
            lw = w.dep.last_w
            if lw is not None:
                deps[lw.idx] = lw
            for rd in w.dep.readers:
                deps[rd.idx] = rd
        for d in extra:
            deps[d.idx] = d
        out = []
        for d in deps.values():
            if d.eng == eng and not d.is_dma and not dma:
                if eng == "pe" or not self.same_engine_sync:
                    continue
            d.signal = True
            out.append(d)
        op.deps = out
        for r in rt:
            r.dep.readers.append(op)
        for w in wt:
            w.dep.last_w = op
            w.dep.readers = []
        self.ops[eng].append(op)
        if dma:
            self.dma_pending.append(op)
        return op

    def dma(self, out, in_, eng="sp"):
        o, i = out.ap, in_.ap
        return self.add(eng, lambda e: e.dma_start(out=o, in_=i), [in_], [out], dma=True)

    def collective(self, kind, out, in_, groups):
        o, i = out.ap, in_.ap
        op = self.add("pool", lambda e: e.collective_compute(kind, ALU.bypass, replica_groups=groups,
                                                             ins=[i], outs=[o]), [in_], [out], dma=True)
        op.inc = 1
        return op

    def mm(self, out, lhsT, rhs, start, stop):
        o, l, r = out.ap, lhsT.ap, rhs.ap
        return self.add("pe", lambda e: e.matmul(o, l, r, start=start, stop=stop), [lhsT, rhs], [out])

    def tr(self, out, in_, ident):
        o, i, d = out.ap, in_.ap, ident.ap
        return self.add("pe", lambda e: e.transpose(o, i, d), [in_, ident], [out])

    def pair(self, tl):
        full, other = self.ps_pair[id(tl)]
        return V(tl, full), other.all()

    def act(self, out, in_, func, bias=None, scale=None, accum_out=None, extra_r=(), extra_w=()):
        kw = {}
        rd = [in_] + list(extra_r)
        wr = [out] + list(extra_w)
        if bias is not None:
            kw["bias"] = _ap(bias)
            rd.append(bias)
        if scale is not None:
            kw["scale"] = _ap(scale)
            rd.append(scale)
        if accum_out is not None:
            kw["accum_out"] = accum_out.ap
            wr.append(accum_out)
        o, i = out.ap, in_.ap
        return self.add("act", lambda e: e.activation(out=o, in_=i, func=func, **kw), rd, wr)

    def tt(self, eng, out, in0, in1, op):
        o, a, b = out.ap, in0.ap, in1.ap
        return self.add(eng, lambda e: e.tensor_tensor(o, a, b, op), [in0, in1], [out])

    def ts(self, eng, out, in0, s1, s2, op0, op1=None, accum_out=None):
        o, a = out.ap, in0.ap
        a1, a2 = _ap(s1), _ap(s2)
        wr = [out]
        kw = {}
        if accum_out is not None:
            kw["accum_out"] = accum_out.ap
            wr.append(accum_out)
        if op1 is None:
            return self.add(eng, lambda e: e.tensor_scalar(o, a, a1, None, op0, **kw), [in0, s1], wr)
        return self.add(eng, lambda e: e.tensor_scalar(o, a, a1, a2, op0, op1, **kw), [in0, s1, s2], wr)

    def stt(self, eng, out, in0, scalar, in1, op0, op1):
        o, a, s, b = out.ap, in0.ap, _ap(scalar), in1.ap
        return self.add(eng, lambda e: e.scalar_tensor_tensor(o, a, s, b, op0, op1), [in0, scalar, in1], [out])

    def copy(self, eng, out, in_):
        o, i = out.ap, in_.ap
        if eng == "act":
            return self.add(eng, lambda e: e.copy(o, i), [in_], [out])
        return self.add(eng, lambda e: e.tensor_copy(o, i), [in_], [out])

    def memset(self, eng, out, val):
        o = out.ap
        return self.add(eng, lambda e: e.memset(o, val), [], [out])

    def asel(self, out, in_, pattern, cmp, fill, base, cm):
        o, i = out.ap, in_.ap
        return self.add("pool", lambda e: e.affine_select(out=o, in_=i, pattern=pattern, compare_op=cmp,
                                                          fill=fill, base=base, channel_multiplier=cm),
                        [in_], [out])

    def recip(self, out, in_):
        o, i = out.ap, in_.ap
        return self.add("dve", lambda e: e.reciprocal(o, i), [in_], [out])

    def sqrt(self, out, in_):
        o, i = out.ap, in_.ap
        return self.add("act", lambda e: e.sqrt(o, i), [in_], [out])

    def max8(self, out, in_):
        o, i = out.ap, in_.ap
        return self.add("dve", lambda e: e.max(out=o, in_=i), [in_], [out])

    def match_replace(self, out, vals, in_, imm):
        o, v, i = out.ap, vals.ap, in_.ap
        return self.add("dve", lambda e: e.match_replace(out=o, in_to_replace=v, in_values=i, imm_value=imm),
                        [vals, in_], [out])

    def reduce(self, eng, out, in_, op, axis=AX.X):
        o, i = out.ap, in_.ap
        return self.add(eng, lambda e: e.tensor_reduce(o, i, axis, op), [in_], [out])

    def build(self):
        nc = self.nc
        es = self.es
        eng_sems = {}
        for e in self.ENGS:
            nsig = sum(1 for o in self.ops[e] if o.signal and not o.is_dma)
            nep = max(1, -(-nsig // SEM_LIMIT))
            eng_sems[e] = [es.enter_context(nc.semaphore(f"s_{e}_{i}")) for i in range(nep)]
        dma_sems = {}
        for e in self.ENGS:
            nd = sum(1 for o in self.ops[e] if o.is_dma)
            if nd:
                n = min(N_DMA_SEMS.get(e, 4), nd)
                dma_sems[e] = [es.enter_context(nc.semaphore(f"d_{e}_{i}")) for i in range(n)]
        cc_sem = es.enter_context(nc.semaphore("s_cc"))
        final_dma = {}
        for e in self.ENGS:
            cnt = 0
            dcnt = 0
            uses = {}
            for o in self.ops[e]:
                if o.is_dma:
                    pool = dma_sems[e]
                    if o.inc == 1:
                        s = cc_sem
                    else:
                        s = pool[dcnt % len(pool)]
                        dcnt += 1
                    k = uses.get(id(s), 0)
                    o.dma_prev = (s, k)
                    uses[id(s)] = k + o.inc
                    o.event = (s, k + o.inc)
                    final_dma[(e, id(s))] = (s, k + o.inc)
                elif o.signal:
                    ep = cnt // SEM_LIMIT
                    o.event = (eng_sems[e][ep], cnt % SEM_LIMIT + 1)
                    cnt += 1
        ops = self.ops
        nwaits = {e: 0 for e in self.ENGS}

        def emit(e, eng):
            seen = {}

            def wait(s, v):
                if v > 0 and seen.get(id(s), 0) < v:
                    eng.wait_ge(s, v)
                    seen[id(s)] = v
                    nwaits[e] += 1

            for o in ops[e]:
                need = {}
                for d in o.deps:
                    s_, v_ = d.event
                    if need.get(id(s_), (None, 0))[1] < v_:
                        need[id(s_)] = (s_, v_)
                for s_, v_ in need.values():
                    wait(s_, v_)
                if o.is_dma:
                    wait(*o.dma_prev)
                ins = o.fn(eng)
                if o.is_dma:
                    ins.then_inc(o.event[0], o.inc)
                elif o.signal:
                    ins.then_inc(o.event[0], 1)
            for (ee, _), (s, v) in final_dma.items():
                if ee == e:
                    wait(s, v)

        with nc.Block() as block:
            @block.tensor
            def _(eng):
                emit("pe", eng)

            @block.scalar
            def _(eng):
                emit("act", eng)

            @block.vector
            def _(eng):
                emit("dve", eng)

            @block.gpsimd
            def _(eng):
                emit("pool", eng)

            @block.sync
            def _(eng):
                emit("sp", eng)

        self.stats = {e: (len(ops[e]), nwaits[e]) for e in self.ENGS}
        es.close()
        return nc


class Ring:
    def __init__(self, tiles):
        self.tiles = tiles
        self.i = 0

    def next(self):
        t = self.tiles[self.i % len(self.tiles)]
        self.i += 1
        return t


def run_pipeline(units, depth=1):
    n = len(units)

    def doA(u):
        if "pre" in u:
            u["pre"]()
        if "A" in u:
            u["A"]()
    for j in range(min(depth, n)):
        doA(units[j])
    for k in range(n):
        if k + depth < n:
            doA(units[k + depth])
        u = units[k]
        for key in ("B", "C", "post"):
            if key in u:
                u[key]()


def make_ident(p, dt=BF16):
    ident = p.sbuf("ident", [128, 128], dt)
    p.memset("pool", ident[:], 1.0)
    p.asel(ident[:], ident[:], [[-1, 128]], ALU.is_equal, 0.0, 0, 1)
    return ident


def even_chunks():
    ch = []
    for j in range(4):
        ch.append(("rope", 0.125, "nqT", j))
    ch.append(("rope", 1.0, "kcT", 0))
    ch.append(("fm", 1.0, "vcT", 0))
    ch.append(("rope", 1.0, "ksT", 0))
    ch.append(("tm", 1.0, "vs", 0))
    ch.append(("rope", 1.0, "kwT", 0))
    ch.append(("tm", 1.0, "vw", 0))
    ch.append(("tmf", 1.0, "gl", 0))
    for j in range(4):
        ch.append(("rope", 0.125, "dqT", j))
    for j in range(4):
        ch.append(("rope", 1.0, "dkT", j))
    for j in range(4):
        ch.append(("tm", 1.0, "dv", j))
    return ch


def odd_chunks():
    ch = []
    for j in range(8):
        ch.append(("fm", 0.125, "qT", j))
    for j in range(8):
        ch.append(("fm", 1.0, "kT", j))
    for j in range(8):
        ch.append(("tm", 1.0, "v", j))
    return ch


def inproj_outputs(nxt):
    if nxt == "even":
        return {
            "nqT": ([8, 64, NT], BF16), "kcT": ([2, 64, NT], BF16), "vcT": ([2, 64, NT], BF16),
            "ksT": ([2, 64, NT], BF16), "kwT": ([2, 64, NT], BF16),
            "vs": ([NT, 128], BF16), "vw": ([NT, 128], BF16), "gl": ([NT, 128], F32),
            "dqT": ([8, 64, NT], BF16), "dkT": ([8, 64, NT], BF16), "dv": ([NT, 512], BF16),
        }
    return {"qT": ([16, 64, NT], BF16), "kT": ([16, 64, NT], BF16), "v": ([NT, 1024], BF16)}


def build_tok(has_prev, nxt, p=None, fused=False):
    p = p or Prog()
    GT = 512
    NG = NT // GT
    TPG = GT // 128
    x_d = p.dram("x", [NT, D_MODEL], F32, "ExternalInput")
    if has_prev:
        oT_d = p.dram("oT", [512 if fused else D_MODEL, 2 * NT if fused else NT], BF16, "ExternalInput")
        oTb_d = p.dram("oTb", [512, 2 * NT], BF16, "ExternalInput") if fused else None
        par = (p.nc.partition_id() % 2) if fused else None
        wo_d = p.dram("wo", [8, 128, D_MODEL], F32, "ExternalInput")
        g2_d = p.dram("g2", [1, D_MODEL], F32, "ExternalInput")
        wgu_d = p.dram("wgu", [NFC, 128, 2, 8, 128], F32, "ExternalInput")
        wd_d = p.dram("wd", [NFC, 128, D_MODEL], F32, "ExternalInput")
    g1_d = p.dram("g1", [1, D_MODEL], F32, "ExternalInput")
    if nxt is not None and fused:
        xo_d = p.dram("xo", [NT, D_MODEL], F32, "ExternalOutput")
        hs_d = p.dram("hsend", [128, 4, NT], BF16, "Internal")
        hsb_d = p.dram("hsendb", [128, 4, NT], BF16, "Internal")
    elif nxt is not None:
        chunks = even_chunks() if nxt == "even" else odd_chunks()
        NCW = len(chunks)
        win_d = p.dram("win", [NCW, 128, 8, 128], F32, "ExternalInput")
        outs = {k: p.dram(k, sh, dt, "ExternalOutput") for k, (sh, dt) in inproj_outputs(nxt).items()}
        xo_d = p.dram("xo", [NT, D_MODEL], F32, "ExternalOutput")
        if nxt == "even":
            cos_d = p.dram("cosT", [128, NT], F32, "ExternalInput")
            sin_d = p.dram("sinT", [128, NT], F32, "ExternalInput")
    else:
        out_d = p.dram("out", [NT, D_MODEL], F32, "ExternalOutput")

    ident = make_ident(p)
    xg = p.sbuf("xg", [128, TPG, D_MODEL], F32)
    hT = p.sbuf("hT", [128, 8, GT], BF16)
    hs_r = p.ring("hs", 2, [128, D_MODEL], BF16)
    junk = p.sbuf("junk", [128, D_MODEL], F32)
    ss_r = p.ring("ss", 2, [128, 1], F32)
    rs_r = p.ring("rs", 2, [128, 1], F32)
    gB1 = p.sbuf("gB1", [128, D_MODEL], F32)
    p.dma(gB1[:], g1_d[0:1, :].to_broadcast([128, D_MODEL]))
    ps = [p.psum(f"ps{i}", [128, 512], F32) for i in range(8)]
    st_r = p.ring("st", 3, [128, 2, 8, 128], F32)
    wb_r = p.ring("wb", 3, [128, 2, 8, 128], BF16)
    if has_prev:
        gB2 = p.sbuf("gB2", [128, D_MODEL], F32)
        p.dma(gB2[:], g2_d[0:1, :].to_broadcast([128, D_MODEL]))
        wo = p.sbuf("wo_sb", [128, 8, D_MODEL], BF16)
        for c in range(8):
            st = st_r.next()
            stv = st[:].rearrange("p a b c -> p (a b c)")[:, 0:D_MODEL]
            p.dma(stv, wo_d[c])
            p.copy("pool", wo[:, c, :], stv)
        oTg = p.sbuf("oTg", [128, 8, GT], BF16)
        aT = p.sbuf("aT", [128, NFC, GT], BF16)
        sg_r = p.ring("sg", 2, [128, GT], F32)
        wdb_r = p.ring("wdb", 3, [128, D_MODEL], BF16)
    if nxt == "even" and not fused:
        cosg = p.sbuf("cosg", [128, GT], F32)
        sing = p.sbuf("sing", [128, GT], F32)
        wrot_r = p.ring("wrot", 2, [128, 8, 128], BF16)
        t1_r = p.ring("t1", 2, [128, GT], F32)
        t2_r = p.ring("t2", 2, [128, GT], F32)
    if nxt is not None and fused:
        pass
    elif nxt is not None:
        fo_r = p.ring("fo", 3, [128, GT], BF16)
        tmo_r = p.ring("tmo", 2, [128, TPG, 128], BF16)
        tmf_r = p.ring("tmf", 2, [128, TPG, 128], F32)
    else:
        og_r = p.ring("og", 2, [128, D_MODEL], F32)
    psi = [0]

    def next_ps():
        t = ps[psi[0] % 8]
        psi[0] += 1
        return t

    def rmsnorm_to_hT(gB):
        for i in range(TPG):
            ss = ss_r.next()
            rs = rs_r.next()
            hs = hs_r.next()
            p.act(junk[:], xg[:, i, :], AF.Square, accum_out=ss[:])
            p.ts("dve", rs[:], ss[:], 1.0 / D_MODEL, NORM_EPS, ALU.mult, ALU.add)
            p.sqrt(rs[:], rs[:])
            p.recip(rs[:], rs[:])
            p.stt("dve", hs[:], xg[:, i, :], rs[:], gB[:], ALU.mult, ALU.mult)
            pt = next_ps()
            ptv = pt[:].bitcast(BF16).rearrange("p (c t) -> p c t", c=8)
            for c in range(8):
                p.tr(ptv[:, c, :], hs[:, c * 128:(c + 1) * 128], ident[:])
            p.copy("dve", hT[:, :, i * 128:(i + 1) * 128], ptv)

    for g in range(NG):
        t0 = g * GT
        p.dma(xg[:], x_d[t0:t0 + GT, :].rearrange("(i p) d -> p i d", p=128))
        if has_prev:
            if fused:
                p.dma(oTg[:, 0:4, :], oT_d[:, bass.ds(par * NT + t0, GT)].rearrange("(c f) t -> f c t", f=128))
                p.dma(oTg[:, 4:8, :], oTb_d[:, bass.ds(par * NT + t0, GT)].rearrange("(c f) t -> f c t", f=128))
            else:
                p.dma(oTg[:], oT_d[:, t0:t0 + GT].rearrange("(c f) t -> f c t", f=128))
            for i in range(TPG):
                for hf in range(2):
                    acc = next_ps()
                    for c in range(8):
                        p.mm(acc[:], oTg[:, c, i * 128:(i + 1) * 128], wo[:, c, hf * 512:(hf + 1) * 512],
                             c == 0, c == 7)
                    xs = xg[:, i, hf * 512:(hf + 1) * 512]
                    p.tt("dve", xs, xs, acc[:], ALU.add)
            rmsnorm_to_hT(gB2)
            for n in range(NFC):
                st = st_r.next()
                wb = wb_r.next()
                p.dma(st[:], wgu_d[n])
                p.copy("pool", wb[:], st[:])
                G = next_ps()
                U = next_ps()
                for c in range(8):
                    p.mm(G[:], wb[:, 0, c, :], hT[:, c, :], c == 0, c == 7)
                for c in range(8):
                    p.mm(U[:], wb[:, 1, c, :], hT[:, c, :], c == 0, c == 7)
                sg = sg_r.next()
                p.act(sg[:], G[:], AF.Silu)
                p.tt("dve", aT[:, n, :], sg[:], U[:], ALU.mult)
            accs = [next_ps() for _ in range(8)]
            for n in range(NFC):
                st = st_r.next()
                wdb = wdb_r.next()
                stv = st[:].rearrange("p a b c -> p (a b c)")[:, 0:D_MODEL]
                p.dma(stv, wd_d[n])
                p.copy("pool", wdb[:], stv)
                for i in range(TPG):
                    for hf in range(2):
                        p.mm(accs[i * 2 + hf][:], aT[:, n, i * 128:(i + 1) * 128],
                             wdb[:, hf * 512:(hf + 1) * 512], n == 0, n == NFC - 1)
            for i in range(TPG):
                for hf in range(2):
                    xs = xg[:, i, hf * 512:(hf + 1) * 512]
                    p.tt("dve", xs, xs, accs[i * 2 + hf][:], ALU.add)
        if nxt is None:
            for i in range(TPG):
                ss = ss_r.next()
                rs = rs_r.next()
                og = og_r.next()
                p.act(junk[:], xg[:, i, :], AF.Square, accum_out=ss[:])
                p.ts("dve", rs[:], ss[:], 1.0 / D_MODEL, NORM_EPS, ALU.mult, ALU.add)
                p.sqrt(rs[:], rs[:])
                p.recip(rs[:], rs[:])
                p.stt("dve", og[:], xg[:, i, :], rs[:], gB1[:], ALU.mult, ALU.mult)
                p.dma(out_d[t0 + i * 128:t0 + (i + 1) * 128, :], og[:], eng="pool")
            continue
        p.dma(xo_d[t0:t0 + GT, :].rearrange("(i p) d -> p i d", p=128), xg[:], eng="pool")
        rmsnorm_to_hT(gB1)
        if fused:
            p.dma(hs_d[:, :, t0:t0 + GT], hT[:, 0:4, :], eng="pool")
            p.dma(hsb_d[:, :, t0:t0 + GT], hT[:, 4:8, :], eng="pool")
            continue
        if nxt == "even":
            p.dma(cosg[:], cos_d[:, t0:t0 + GT])
            p.dma(sing[:], sin_d[:, t0:t0 + GT])
        for ci, (kind, scale, dname, dj) in enumerate(chunks):
            st = st_r.next()
            wb = wb_r.next()
            stv = st[:, 0]
            wbv = wb[:, 0]
            p.dma(stv, win_d[ci])
            if scale == 1.0:
                p.copy("pool", wbv, stv)
            else:
                p.ts("pool", wbv, stv, scale, None, ALU.mult)
            dst = outs[dname]
            if kind in ("fm", "rope"):
                A = next_ps()
                for c in range(8):
                    p.mm(A[:], wbv[:, c, :], hT[:, c, :], c == 0, c == 7)
                fo = fo_r.next()
                if kind == "rope":
                    wr = wrot_r.next()
                    sv = stv.rearrange("p c (h f d) -> p (c h) f d", h=2, f=2, d=32)
                    wv = wr[:].rearrange("p c (h f d) -> p (c h) f d", h=2, f=2, d=32)
                    p.ts("pool", wv[:, :, 0, :], sv[:, :, 1, :], -scale, None, ALU.mult)
                    p.ts("pool", wv[:, :, 1, :], sv[:, :, 0, :], scale, None, ALU.mult)
                    Bp = next_ps()
                    for c in range(8):
                        p.mm(Bp[:], wr[:, c, :], hT[:, c, :], c == 0, c == 7)
                    t1 = t1_r.next()
                    t2 = t2_r.next()
                    p.tt("dve", t1[:], A[:], cosg[:], ALU.mult)
                    p.tt("dve", t2[:], Bp[:], sing[:], ALU.mult)
                    p.tt("pool", fo[:], t1[:], t2[:], ALU.add)
                else:
                    p.copy("act", fo[:], A[:])
                dv_ = dst[2 * dj:2 * dj + 2, :, t0:t0 + GT].rearrange("h d t -> (h d) t")
                p.dma(dv_, fo[:], eng="pool")
            else:
                A = next_ps()
                Av = A[:].rearrange("p (i n) -> p i n", i=TPG)
                for i in range(TPG):
                    for c in range(8):
                        p.mm(Av[:, i, :], hT[:, c, i * 128:(i + 1) * 128], wbv[:, c, :], c == 0, c == 7)
                if kind == "tm":
                    to = tmo_r.next()
                else:
                    to = tmf_r.next()
                p.copy("act", to[:], Av)
                dv_ = dst[t0:t0 + GT, dj * 128:(dj + 1) * 128].rearrange("(i p) n -> p i n", p=128)
                p.dma(dv_, to[:], eng="pool")
    return p


def lay_wo(w):
    return np.ascontiguousarray(w.reshape(8, 128, D_MODEL))


def lay_wgu(wg, wu):
    a = np.stack([wg, wu], 0).reshape(2, 8, 128, NFC, 128)
    return np.ascontiguousarray(a.transpose(3, 2, 0, 1, 4))


def lay_wd(wd):
    return np.ascontiguousarray(wd.reshape(NFC, 128, D_MODEL))


def lay_win_even(w):
    cols = [w[:, 0:512], w[:, 512:640], w[:, 640:768], w[:, 768:896], w[:, 896:1024], w[:, 1024:1152],
            w[:, 1152:1280], np.pad(w[:, 1280:1304], ((0, 0), (0, 104))), w[:, 1304:1816], w[:, 1816:2328],
            w[:, 2328:2840]]
    wp = np.concatenate(cols, 1)
    n = wp.shape[1] // 128
    return np.ascontiguousarray(wp.reshape(8, 128, n, 128).transpose(2, 1, 0, 3))


def lay_win_odd(w):
    return np.ascontiguousarray(w.reshape(8, 128, 24, 128).transpose(2, 1, 0, 3))


def rope_tables_T(pos):
    inv = 1.0 / (10000.0 ** (np.arange(0, HD, 2, dtype=np.float32) / HD))
    ang = pos.astype(np.float32)[None, :] * inv.astype(np.float32)[:, None]
    c = np.cos(ang).astype(np.float32)
    s = np.sin(ang).astype(np.float32)
    return np.ascontiguousarray(np.tile(c, (4, 1))), np.ascontiguousarray(np.tile(s, (4, 1)))


def build_att_odd(S=SEQ, n_hg=2, p=None, mid_hook=None):
    if p is None:
        p = Prog()
        p.enable_arena(200 * 1024)
    NH = n_hg * 4
    NI = S // 512
    NKB = S // 128
    qT_d = p.dram("qT", [NH, 64, S], BF16, "ExternalInput")
    kT_d = p.dram("kT", [NH, 64, S], BF16, "ExternalInput")
    v_d = p.dram("v", [128, NKB, NH * 64], BF16, "ExternalInput")
    oT_d = p.dram("oT", [NH * 64, S], BF16, "ExternalOutput")
    ident = make_ident(p)
    U = p.sbuf("Uincl", [128, 128], BF16)
    L = p.sbuf("Lstr", [128, 128], BF16)
    p.memset("pool", U[:], 1.0)
    p.asel(U[:], U[:], [[-1, 128]], ALU.is_ge, 0.0, 0, 1)
    p.memset("pool", L[:], 1.0)
    p.asel(L[:], L[:], [[1, 128]], ALU.is_ge, 0.0, -1, -1)
    qT = p.sbuf("qT_sb", [64, 4, S], BF16)
    kT = p.sbuf("kT_sb", [64, 4, S], BF16)
    v_sb = p.sbuf("v_sb", [128, NKB, NH * 64], BF16)
    for q4 in range(4):
        a, b = q4 * NKB // 4, (q4 + 1) * NKB // 4
        p.dma(v_sb[:, a:b, :], v_d[:, a:b, :])
    Z_r = p.ring("Z", 4, [128, 512], F32, "psum")
    A_r = p.ring("A", 2, [128, 512], F32, "psum")
    O_r = p.ring("O", 2, [128, 512], F32, "psum")
    E_r = p.ring("E", 3, [128, 1024], F32)
    SP_r = p.ring("SP", 3, [128, 1024], BF16)
    G_r = p.ring("G", 3, [128, 1024], F32)
    P_r = p.ring("P", 3, [128, 1024], BF16)
    oT_r = p.ring("oTs", 4, [64, 512], BF16)
    units = []
    for hg in range(n_hg):
        for I in range(NI):
            kb_hi = 4 * I + 3
            for hp in range(2):
                st = {}
                for kb in range(kb_hi, -1, -1):
                    units.append({"st": st, "hg": hg, "I": I, "hp": hp, "kb": kb, "kb_hi": kb_hi,
                                  "heads": (2 * hp, 2 * hp + 1),
                                  "load": (I == 0 and hp == 0 and kb == kb_hi)})

    def c0_of(u):
        return max(0, 128 * (u["kb"] - 4 * u["I"]))

    def cv(v, c0):
        return v.rearrange("p (h q) -> p h q", h=2)[:, :, c0:512]

    def fA(u):
        if u["load"]:
            for j in range(4):
                p.dma(qT[:, j, :], qT_d[u["hg"] * 4 + j])
                p.dma(kT[:, j, :], kT_d[u["hg"] * 4 + j])
        kb, I = u["kb"], u["I"]
        c0 = c0_of(u)
        u["Z"] = {}
        for h in u["heads"]:
            u["Z"][h] = Z_r.next()
            p.mm(u["Z"][h][:, c0:512], kT[:, h, kb * 128:(kb + 1) * 128], qT[:, h, I * 512 + c0:(I + 1) * 512],
                 True, True)

    def fE(u):
        c0 = c0_of(u)
        u["E2"] = E_r.next()
        Zp, Zo = p.pair(u["Z"][u["heads"][0]])
        p.act(cv(u["E2"][:], c0), cv(Zp, c0), AF.Exp, extra_r=[Zo])

    def fSPU(u):
        kb, I, kb_hi, heads = u["kb"], u["I"], u["kb_hi"], u["heads"]
        c0 = c0_of(u)
        if kb == kb_hi:
            u["st"]["A"] = {h: A_r.next() for h in heads}
        Ah = u["st"]["A"]
        u["SP2"] = SP_r.next()
        SP2 = u["SP2"]
        p.act(cv(SP2[:], c0), cv(u["E2"][:], c0), AF.Ln, bias=1.0)
        if kb >= 4 * I:
            p.asel(cv(SP2[:], c0), cv(SP2[:], c0), [[0, 2], [1, 512 - c0]], ALU.is_ge, 0.0, -1, -1)
        for j, h in enumerate(heads):
            p.mm(Ah[h][:, c0:512], U[:], SP2[:, j * 512 + c0:(j + 1) * 512], kb == kb_hi, False)

    def fRest(u):
        kb, I, kb_hi, heads, hg, hp = u["kb"], u["I"], u["kb_hi"], u["heads"], u["hg"], u["hp"]
        c0 = c0_of(u)
        Ah = u["st"]["A"]
        SP2, E2 = u["SP2"], u["E2"]
        G2, P2 = G_r.next(), P_r.next()
        Ap, Ao = p.pair(Ah[heads[0]])
        p.act(cv(G2[:], c0), cv(Ap, c0), AF.Exp, scale=-1.0, extra_r=[Ao])
        for j, h in enumerate(heads):
            p.mm(Ah[h][:, c0:512], L[:], SP2[:, j * 512 + c0:(j + 1) * 512], False, kb == 0)
        p.tt("dve", cv(P2[:], c0), cv(E2[:], c0), cv(G2[:], c0), ALU.mult)
        if kb >= 4 * I:
            p.asel(cv(P2[:], c0), cv(P2[:], c0), [[0, 2], [1, 512 - c0]], ALU.is_ge, 0.0, -1, -1)
        u["P2"] = P2

    def fPV(u):
        kb, I, kb_hi, heads, hg, hp = u["kb"], u["I"], u["kb_hi"], u["heads"], u["hg"], u["hp"]
        c0 = c0_of(u)
        P2 = u["P2"]
        if kb == kb_hi:
            u["st"]["O"] = {h: O_r.next() for h in heads}
        Oh = u["st"]["O"]
        for j, h in enumerate(heads):
            p.mm(Oh[h][0:64, c0:512], v_sb[:, kb, (hg * 4 + h) * 64:(hg * 4 + h + 1) * 64],
                 P2[:, j * 512 + c0:(j + 1) * 512], kb == kb_hi, kb == 0)
        if kb == 0:
            for h in heads:
                ot = oT_r.next()
                p.copy("dve", ot[:], Oh[h][0:64, :])
                r0 = (hg * 4 + h) * 64
                p.dma(oT_d[r0:r0 + 64, I * 512:(I + 1) * 512], ot[:], eng="pool")
            if mid_hook is not None and hg == 0 and hp == 1 and I == NI - 1 and n_hg > 1:
                mid_hook()

    n = len(units)
    fA(units[0])
    if n > 1:
        fA(units[1])
    fE(units[0])
    for k in range(n):
        fSPU(units[k])
        if k + 2 < n:
            fA(units[k + 2])
        if k + 1 < n:
            fE(units[k + 1])
        fRest(units[k])
        if k >= 1:
            fPV(units[k - 1])
    fPV(units[n - 1])
    return p


NEGM = 30000.0


def build_att_even(S=SEQ, do_nsa=True, do_diff=True, p=None, mid_hook=None):
    if p is None:
        p = Prog()
        p.enable_arena(200 * 1024)
    NQ = S // 128
    NI = S // 512
    NSL = S // 64
    NCMP = (S - 32) // 16 + 1
    NCC = -(-NCMP // 128)
    nqT_d = p.dram("nqT", [4, 64, S], BF16, "ExternalInput")
    kcT_d = p.dram("kcT", [64, S], BF16, "ExternalInput")
    vcT_d = p.dram("vcT", [64, S], BF16, "ExternalInput")
    ksT_d = p.dram("ksT", [64, S], BF16, "ExternalInput")
    kwT_d = p.dram("kwT", [64, S], BF16, "ExternalInput")
    vs_d = p.dram("vs", [128, NQ, 64], BF16, "ExternalInput")
    vw_d = p.dram("vw", [128, NQ, 64], BF16, "ExternalInput")
    gl_d = p.dram("gl", [128, NQ, 12], F32, "ExternalInput")
    wck_d = p.dram("wck", [64, 32, 64], F32, "ExternalInput")
    wcv_d = p.dram("wcv", [64, 32, 64], F32, "ExternalInput")
    pk_d = p.dram("pkT", [64, 32], F32, "ExternalInput")
    pv_d = p.dram("pvT", [64, 32], F32, "ExternalInput")
    ftab_d = p.dram("ftab", [128, NQ, NSL], F32, "ExternalInput")
    dqT_d = p.dram("dqT", [4, 64, S], BF16, "ExternalInput")
    dkT_d = p.dram("dkT", [4, 64, S], BF16, "ExternalInput")
    dv_d = p.dram("dv", [128, NQ, 256], BF16, "ExternalInput")
    lqk_d = p.dram("lqk", [1, 256], F32, "ExternalInput")
    li_d = p.dram("lami", [1, 2], F32, "ExternalInput")
    sub_d = p.dram("subln", [1, 128], F32, "ExternalInput")
    oT_d = p.dram("oT", [512, S], BF16, "ExternalOutput")

    ident = make_ident(p)
    ps = [p.psum(f"ps{i}", [128, 512], F32) for i in range(8)]
    bigA = p.sbuf("bigA", [64, 4, S], BF16)
    bigB = p.sbuf("bigB", [64, 4, S], BF16)
    E_r = p.ring("E", 4, [128, 512], BF16)
    oTs_r = p.ring("oTs", 2, [128, 2, 512], BF16)
    vstg = p.sbuf("vstg", [128, NQ, 256], BF16)

    if do_nsa:
        Zr = Ring(ps[0:4])
        OC, OS, OW, TP = ps[4], ps[5], ps[6], ps[7]
        OCv = OC[:].rearrange("p (h c) -> p h c", h=4)
        OSv = OS[:, 0:260].rearrange("p (h c) -> p h c", h=4)
        OWv = OW[:, 0:260].rearrange("p (h c) -> p h c", h=4)
        for j in range(4):
            p.dma(bigA[:, j, :], nqT_d[j])
        p.dma(bigB[:, 0, :], kcT_d[:, :])
        p.dma(bigB[:, 1, :], vcT_d[:, :])
        p.dma(bigB[:, 2, :], ksT_d[:, :])
        p.dma(bigB[:, 3, :], kwT_d[:, :])
        kcT, vcT, ksT, kwT = (bigB[:, j, :] for j in range(4))
        vs1 = p.sbuf("vs1", [128, NQ, 65], BF16)
        vw1 = p.sbuf("vw1", [128, NQ, 65], BF16)
        p.memset("pool", vs1[:, :, 64:65], 1.0)
        p.memset("pool", vw1[:, :, 64:65], 1.0)
        p.dma(vstg[:, :, 0:64], vs_d[:, :, :])
        p.dma(vstg[:, :, 64:128], vw_d[:, :, :])
        p.copy("act", vs1[:, :, 0:64], vstg[:, :, 0:64])
        p.copy("act", vw1[:, :, 0:64], vstg[:, :, 64:128])
        gates = p.sbuf("gates", [128, NQ, 12], F32)
        p.dma(gates[:], gl_d[:, :, :])
        p.act(gates[:], gates[:], AF.Exp, scale=-1.0)
        p.ts("dve", gates[:], gates[:], 1.0, None, ALU.add)
        p.recip(gates[:], gates[:])
        ftab = p.sbuf("ftab_sb", [128, NQ, NSL], F32)
        p.dma(ftab[:], ftab_d[:, :, :])
        EE = p.sbuf("EE", [64, S], BF16)
        p.memset("pool", EE[:], 1.0)
        p.asel(EE[:], EE[:], [[1, S]], ALU.is_ge, 0.0, 0, -64)
        p.asel(EE[:], EE[:], [[-1, S]], ALU.is_ge, 0.0, 63, 64)
        wst = p.sbuf("wst", [64, 32, 64], F32)
        wck = p.sbuf("wck_sb", [64, 32, 64], BF16)
        wcv = p.sbuf("wcv_sb", [64, 32, 64], BF16)
        pst = p.sbuf("pst", [64, 32], F32)
        pkT = p.sbuf("pkT_sb", [64, 32], BF16)
        pvTb = p.sbuf("pvTb", [64, 32, 128], BF16)
        p.dma(wst[:], wck_d[:, :, :])
        p.copy("dve", wck[:], wst[:])
        p.dma(wst[:], wcv_d[:, :, :])
        p.copy("dve", wcv[:], wst[:])
        p.dma(pst[:], pk_d[:, :])
        p.copy("dve", pkT[:], pst[:])
        p.dma(pst[:], pv_d[:, :])
        p.copy("dve", pvTb[:], pst[:].unsqueeze(2).to_broadcast([64, 32, 128]))
        kcmpT = p.sbuf("kcmpT", [64, NCC * 128], BF16)
        VC = p.sbuf("VC", [128, NCC, 128], BF16)
        ck = p.sbuf("ck", [64, 1], F32)
        p.memset("pool", kcmpT[:], 0.0)
        zk = ps[0]
        zc = ps[1]
        for j in range(32):
            p.mm(zk[0:64, 0:NCMP], wck[:, j, :], kcT[:, j:j + 16 * (NCMP - 1) + 1:16], j == 0, j == 31)
        for j in range(32):
            p.mm(zc[0:64, 0:1], wck[:, j, :], pkT[:, j:j + 1], j == 0, j == 31)
        p.copy("dve", ck[:], zc[0:64, 0:1])
        p.ts("dve", kcmpT[:, 0:NCMP], zk[0:64, 0:NCMP], ck[:], None, ALU.add)
        cmask = p.sbuf("cmask", [128, NCC, 64], F32)
        p.memset("pool", VC[:], 0.0)
        p.memset("pool", cmask[:], 1.0)
        for c in range(NCC):
            m = min(128, NCMP - 128 * c)
            zv = ps[2 + c % 2]
            for j in range(32):
                n0 = 16 * 128 * c + j
                p.mm(zv[0:m, 0:64], vcT[:, n0:n0 + 16 * (m - 1) + 1:16], wcv[:, j, :], j == 0, False)
            for j in range(32):
                p.mm(zv[0:m, 0:64], pvTb[:, j, 0:m], wcv[:, j, :], False, j == 31)
            p.copy("dve", VC[0:m, c, 0:64], zv[0:m, 0:64])
            p.asel(cmask[:, c, :], cmask[:, c, :], [[-4, 64]], ALU.is_ge, 0.0, 128 * c + 1, 1)
            p.asel(cmask[:, c, :], cmask[:, c, :], [[4, 64]], ALU.is_ge, 0.0, 3 - 128 * c, -1)
            p.copy("pool", VC[0:m, c, 64:128], cmask[0:m, c, :])
            p.memset("pool", VC[0:m, c, 64:65], 1.0)
        onsa_r = p.ring("onsa", 2, [128, 4, 64], F32)
        onb_r = p.ring("onb", 2, [128, 256], BF16)
        rz_r = p.ring("rz", 3, [128, 4], F32)
        gco_r = p.ring("gco", 3, [128, 4], F32)
        tmp_r = p.ring("tmpn", 2, [128, 4, 64], F32)
        imp_r = p.ring("imp", 2, [128, 64], F32)
        wk_r = p.ring("impw", 2, [128, 64], F32)
        m8_r = p.ring("m8", 2, [128, 16], F32)
        thr_r = p.ring("thr", 2, [128, 1], F32)
        neg_r = p.ring("neg", 2, [128, 64], BF16)
        negT_r = p.ring("negT", 2, [64, 128], BF16)
        units = []
        tile_state = {}

        def pre_sel_fn(stt_):
            negT = negT_r.next()
            stt_["negT"] = negT
            TPb = TP[:].bitcast(BF16)
            p.tr(TPb[0:64, 0:128], stt_["neg"][:], ident[:])
            p.copy("dve", negT[:], TPb[0:64, 0:128])

        for i in range(NQ):
            qv = bigA[:, :, i * 128:(i + 1) * 128]
            stt_ = {"onsa": None}
            tile_state[i] = stt_
            if i % 4 == 0:
                oTs_cur = oTs_r.next()
            stt_["oTs"] = oTs_cur

            def mk_unit(kind, kb, i=i, qv=qv, stt_=stt_, first=False, last=False, c=None):
                u = {}

                def A(u=u):
                    Z = Zr.next()
                    u["Z"] = Z
                    if kind == "cmp":
                        p.mm(Z[:], kcmpT[:, c * 128:(c + 1) * 128], qv, True, True)
                    elif kind == "win":
                        p.mm(Z[:], kwT[:, kb * 128:(kb + 1) * 128], qv, True, True)
                    else:
                        if first:
                            pre_sel_fn(stt_)
                        p.mm(Z[:], ksT[:, kb * 128:(kb + 1) * 128], qv, True, False)
                        p.mm(Z[:], EE[:, kb * 128:(kb + 1) * 128],
                             stt_["negT"][:].unsqueeze(1).to_broadcast([64, 4, 128]), False, True)

                def B(u=u):
                    E = E_r.next()
                    u["E"] = E
                    p.act(E[:], u["Z"][:], AF.Exp)
                    if kind == "cmp":
                        if not (128 * c + 127 <= 8 * i - 2):
                            p.asel(E[:], E[:], [[0, 4], [1, 128]], ALU.is_ge, 0.0, 128 * i - 31 - 2048 * c, -16)
                    else:
                        if kb == i:
                            p.asel(E[:], E[:], [[0, 4], [1, 128]], ALU.is_ge, 0.0, 0, -1)
                        if kind == "win" and kb == i - 4:
                            p.asel(E[:], E[:], [[0, 4], [-1, 128]], ALU.is_ge, 0.0, -1, 1)

                def C(u=u):
                    E = u["E"]
                    for h in range(4):
                        if kind == "cmp":
                            p.mm(OCv[:, h, :], E[:, h * 128:(h + 1) * 128], VC[:, c, :], first and h == 0,
                                 last and h == 3)
                        elif kind == "win":
                            p.mm(OWv[:, h, :], E[:, h * 128:(h + 1) * 128], vw1[:, kb, :], first and h == 0,
                                 last and h == 3)
                        else:
                            p.mm(OSv[:, h, :], E[:, h * 128:(h + 1) * 128], vs1[:, kb, :], first and h == 0,
                                 last and h == 3)
                u["A"], u["B"], u["C"] = A, B, C
                return u

            ccs = [c for c in range(NCC) if 2048 * c + 31 <= 128 * i + 127]
            cu = [mk_unit("cmp", None, first=(c == ccs[0]), last=(c == ccs[-1]), c=c) for c in ccs]

            def post_cmp(i=i, stt_=stt_):
                prev = tile_state.get(i - 1)
                if prev is not None and "flush" in prev:
                    prev.pop("flush")()
                onsa = onsa_r.next()
                stt_["onsa"] = onsa
                rz = rz_r.next()
                gco = gco_r.next()
                p.ts("dve", rz[:], OCv[:, :, 64], 1e-30, None, ALU.max)
                p.recip(rz[:], rz[:])
                p.tt("dve", gco[:], rz[:], gates[:, i, 0:12:3], ALU.mult)
                p.tt("dve", onsa[:], OCv[:, :, 0:64], gco[:].unsqueeze(2).to_broadcast([128, 4, 64]), ALU.mult)
                tmp = tmp_r.next()
                imp = imp_r.next()
                wk = wk_r.next()
                m8 = m8_r.next()
                thr = thr_r.next()
                neg = neg_r.next()
                stt_["neg"] = neg
                p.tt("dve", tmp[:], OCv[:, :, 64:128], rz[:].unsqueeze(2).to_broadcast([128, 4, 64]), ALU.mult)
                p.reduce("dve", imp[:, 0:64], tmp[:].rearrange("p h j -> p j h"), ALU.add)
                p.tt("dve", imp[:, 0:NSL], imp[:, 0:NSL], ftab[:, i, :], ALU.add)
                p.max8(m8[:, 0:8], imp[:, 0:NSL])
                p.match_replace(wk[:, 0:NSL], m8[:, 0:8], imp[:, 0:NSL], -3.0e38)
                p.max8(m8[:, 8:16], wk[:, 0:NSL])
                p.reduce("dve", thr[:], m8[:, 8:16], ALU.min)
                p.memset("pool", neg[:], 0.0)
                p.ts("dve", wk[:, 0:NSL], imp[:, 0:NSL], thr[:], None, ALU.is_ge)
                p.ts("dve", neg[:, 0:NSL], wk[:, 0:NSL], NEGM, -NEGM, ALU.mult, ALU.add)
            cu[-1]["post"] = post_cmp
            kbs = list(range(max(0, i - 4), i + 1))
            wu = [mk_unit("win", kb, first=(kb == kbs[0]), last=(kb == i)) for kb in kbs]
            su = [mk_unit("sel", kb, first=(kb == 0), last=(kb == i)) for kb in range(i + 1)]


            def post_sel(i=i, stt_=stt_):
                onsa = stt_["onsa"]
                for (Ov, gi) in ((OWv, 2), (OSv, 1)):
                    rz2 = rz_r.next()
                    gc2 = gco_r.next()
                    tmp2 = tmp_r.next()
                    p.recip(rz2[:], Ov[:, :, 64])
                    p.tt("dve", gc2[:], rz2[:], gates[:, i, gi:12:3], ALU.mult)
                    p.tt("dve", tmp2[:], Ov[:, :, 0:64], gc2[:].unsqueeze(2).to_broadcast([128, 4, 64]), ALU.mult)
                    p.tt("pool", onsa[:], onsa[:], tmp2[:], ALU.add)
                onb = onb_r.next()
                p.copy("act", onb[:], onsa[:].rearrange("p h d -> p (h d)"))

                def flush(i=i, onb=onb, oTs=stt_["oTs"]):
                    TPb = TP[:].bitcast(BF16)
                    for c2 in range(2):
                        p.tr(TPb[:, 256 + c2 * 128:256 + (c2 + 1) * 128], onb[:, c2 * 128:(c2 + 1) * 128], ident[:])
                    p.copy("dve", oTs[:, :, (i % 4) * 128:(i % 4 + 1) * 128],
                           TPb[:, 256:512].rearrange("p (c q) -> p c q", c=2))
                    if i % 4 == 3:
                        i0 = (i // 4) * 512
                        p.dma(oT_d[0:256, i0:i0 + 512].rearrange("(c f) q -> f c q", c=2), oTs[:], eng="pool")
                stt_["flush"] = flush
            su[-1]["post"] = post_sel
            units += cu + wu + su
        run_pipeline(units, 1)
        tile_state[NQ - 1].pop("flush")()

    if mid_hook is not None:
        mid_hook()
    if do_diff:
        Zr = Ring(ps[0:4])
        TP = ps[3]
        E2_r = p.ring("E2d", 3, [128, 1024], BF16)
        TPb = TP[:].bitcast(BF16)
        OD = [[ps[4], ps[5]], [ps[6], ps[7]]]

        def odv(m, sub):
            return OD[m][sub // 2][:, 0:258].rearrange("p (s c) -> p s c", s=2)[:, sub % 2, :]

        for j in range(4):
            p.dma(bigA[:, j, :], dqT_d[j])
            p.dma(bigB[:, j, :], dkT_d[j])
        dv1 = p.sbuf("dv1", [128, NQ, 2, 129], BF16)
        p.memset("pool", dv1[:, :, :, 128:129], 1.0)
        for q4 in range(4):
            a, b = q4 * NQ // 4, (q4 + 1) * NQ // 4
            p.dma(vstg[:, a:b, :], dv_d[:, a:b, :])
        p.copy("dve", dv1[:, :, :, 0:128], vstg[:].rearrange("p k (h d) -> p k h d", h=2))
        lqk = p.sbuf("lqk_sb", [128, 4, 64], F32)
        li = p.sbuf("li_sb", [128, 2], F32)
        sB = p.sbuf("sublnB", [128, 128], F32)
        p.dma(lqk[:].rearrange("p a d -> p (a d)"), lqk_d[0:1, :].to_broadcast([128, 256]))
        p.dma(li[:], li_d[0:1, :].to_broadcast([128, 2]))
        p.dma(sB[:], sub_d[0:1, :].to_broadcast([128, 128]))
        lpr = p.sbuf("lpr", [128, 2, 64], F32)
        lsum = p.sbuf("lsum", [128, 2], F32)
        nlam = p.sbuf("nlam", [128, 1], F32)
        p.tt("dve", lpr[:], lqk[:, 0:4:2, :], lqk[:, 1:4:2, :], ALU.mult)
        p.reduce("dve", lsum[:], lpr[:], ALU.add)
        p.act(lsum[:], lsum[:], AF.Exp)
        p.tt("dve", nlam[:], lsum[:, 1:2], lsum[:, 0:1], ALU.subtract)
        p.tt("dve", nlam[:], nlam[:], li[:, 0:1], ALU.subtract)
        p.ts("dve", sB[:], sB[:], li[:, 1:2], None, ALU.mult)
        od_r = p.ring("od", 2, [128, 128], F32)
        od2_r = p.ring("od2", 2, [128, 128], F32)
        odb_r = p.ring("odb", 2, [128, 128], BF16)
        rzd_r = p.ring("rzd", 4, [128, 1], F32)
        ssd_r = p.ring("ssd", 2, [128, 1], F32)
        jk = p.sbuf("jkd", [128, 128], F32)
        oTd_r = p.ring("oTd", 2, [128, 512], BF16)
        units = []
        for hd in range(2):
            for I in range(NI):
                nkb = 4 * I + 4
                for kb in range(nkb):
                    if True:
                        m = 1
                        u = {}
                        c0 = max(0, 128 * (kb - 4 * I))

                        def cvd(v, c0=c0):
                            return v.rearrange("p (h q) -> p h q", h=2)[:, :, c0:512]

                        def A(u=u, hd=hd, I=I, kb=kb, c0=c0):
                            u["Z"] = [Zr.next(), Zr.next()]
                            for m_ in range(2):
                                p.mm(u["Z"][m_][:, c0:512], bigB[:, hd * 2 + m_, kb * 128:(kb + 1) * 128],
                                     bigA[:, hd * 2 + m_, I * 512 + c0:(I + 1) * 512], True, True)

                        def B(u=u, I=I, kb=kb, c0=c0, cvd=cvd):
                            E2 = E2_r.next()
                            u["E"] = E2
                            Zp, Zo = p.pair(u["Z"][0])
                            p.act(cvd(E2[:]), cvd(Zp), AF.Exp, extra_r=[Zo])
                            if kb >= 4 * I:
                                p.asel(cvd(E2[:]), cvd(E2[:]), [[0, 2], [1, 512 - c0]], ALU.is_ge, 0.0, 0, -1)

                        def C(u=u, hd=hd, kb=kb, nkb=nkb, c0=c0):
                            for m_ in range(2):
                                for sub in range(c0 // 128, 4):
                                    p.mm(odv(m_, sub), u["E"][:, m_ * 512 + sub * 128:m_ * 512 + (sub + 1) * 128],
                                         dv1[:, kb, hd, :], kb == 0 and sub % 2 == 0,
                                         (sub == 1 and kb == nkb - 3) or (sub == 3 and kb == nkb - 1))
                        u["A"], u["B"], u["C"] = A, B, C

                        def post(hd=hd, I=I):
                            oTd = oTd_r.next()
                            for sub in range(4):
                                r1 = rzd_r.next()
                                r2 = rzd_r.next()
                                od = od_r.next()
                                od2 = od2_r.next()
                                odb = odb_r.next()
                                ss = ssd_r.next()
                                p.recip(r1[:], odv(0, sub)[:, 128:129])
                                p.recip(r2[:], odv(1, sub)[:, 128:129])
                                p.tt("dve", r2[:], r2[:], nlam[:], ALU.mult)
                                p.ts("dve", od[:], odv(0, sub)[:, 0:128], r1[:], None, ALU.mult)
                                p.stt("dve", od2[:], odv(1, sub)[:, 0:128], r2[:], od[:], ALU.mult, ALU.add)
                                p.act(jk[:], od2[:], AF.Square, accum_out=ss[:])
                                p.ts("dve", ss[:], ss[:], 1.0 / 128, NORM_EPS, ALU.mult, ALU.add)
                                p.act(ss[:], ss[:], AF.Ln)
                                p.act(ss[:], ss[:], AF.Exp, scale=-0.5)
                                p.stt("dve", odb[:], od2[:], ss[:], sB[:], ALU.mult, ALU.mult)
                                p.tr(TPb[:, sub * 128:(sub + 1) * 128], odb[:], ident[:])
                            p.copy("dve", oTd[:], TPb[:, 0:512])
                            p.dma(oT_d[256 + hd * 128:256 + (hd + 1) * 128, I * 512:(I + 1) * 512], oTd[:],
                                  eng="pool")
                        if kb == nkb - 1 and m == 1:
                            u["post"] = post
                        units.append(u)
        run_pipeline(units, 1)
    return p


_PROGS = {}


def _prog(key, fn):
    if key not in _PROGS:
        _PROGS[key] = fn().build()
    return _PROGS[key]


def make_ftab(S):
    NQ = S // 128
    NSL = S // 64
    t = np.arange(S)
    qblk = t // 64
    jb = np.arange(NSL)
    F = np.zeros((S, NSL), np.float32)
    F[jb[None, :] > qblk[:, None]] = -1e30
    F[np.arange(S), qblk] = 3e30
    m = qblk >= 1
    F[np.arange(S)[m], qblk[m] - 1] = 2e30
    F[:, 0] = 1e30
    return np.ascontiguousarray(F.reshape(NQ, 128, NSL).transpose(1, 0, 2))


def _pm(a):
    S = a.shape[0]
    return np.ascontiguousarray(a.reshape(S // 128, 128, -1).transpose(1, 0, 2))


def _run(nc, in_maps):
    res = run_bass_kernel_spmd(nc, in_maps, core_ids=list(range(8)))
    return res.results


def kernel_unfused(x, norm_mix, norm_ffn, norm_final, even_w_in, even_w_out,
           cmp_pos_k, cmp_w_k, cmp_pos_v, cmp_w_v,
           diff_lq1, diff_lk1, diff_lq2, diff_lk2, diff_subln,
           odd_w_in, odd_w_out, ffn_w_gate, ffn_w_up, ffn_w_down):
    f32 = lambda a: np.ascontiguousarray(np.asarray(a, dtype=np.float32))
    x = f32(x)
    xs = [np.ascontiguousarray(x[c // 2, (c % 2) * NT:(c % 2 + 1) * NT]) for c in range(8)]
    ropes = [rope_tables_T(np.arange((c % 2) * NT, (c % 2 + 1) * NT)) for c in range(8)]
    ftab = make_ftab(SEQ)
    oT_full = None
    out = None
    for layer in range(DEPTH + 1):
        has_prev = layer > 0
        nxt = None if layer == DEPTH else ("even" if layer % 2 == 0 else "odd")
        nc = _prog(("tok", has_prev, nxt), lambda: build_tok(has_prev, nxt))
        common = {}
        if has_prev:
            pl = layer - 1
            w_out = f32(even_w_out[pl // 2]) if pl % 2 == 0 else f32(odd_w_out[pl // 2])
            common["wo"] = lay_wo(w_out)
            common["g2"] = f32(norm_ffn[pl])[None, :]
            common["wgu"] = lay_wgu(f32(ffn_w_gate[pl]), f32(ffn_w_up[pl]))
            common["wd"] = lay_wd(f32(ffn_w_down[pl]))
        if nxt is None:
            common["g1"] = f32(norm_final)[None, :]
        else:
            common["g1"] = f32(norm_mix[layer])[None, :]
            common["win"] = lay_win_even(f32(even_w_in[layer // 2])) if nxt == "even" else lay_win_odd(f32(odd_w_in[layer // 2]))
        in_maps = []
        for c in range(8):
            m = dict(common)
            m["x"] = xs[c]
            if has_prev:
                b, hf = c // 2, c % 2
                m["oT"] = np.ascontiguousarray(oT_full[b][:, hf * NT:(hf + 1) * NT])
            if nxt == "even":
                m["cosT"], m["sinT"] = ropes[c]
            in_maps.append(m)
        r = _run(nc, in_maps)
        if nxt is None:
            out = np.stack([np.concatenate([r[2 * b]["out"], r[2 * b + 1]["out"]], 0) for b in range(BATCH)], 0)
            break
        xs = [np.ascontiguousarray(r[c]["xo"]) for c in range(8)]

        def cat(name, b, axis):
            return np.concatenate([r[2 * b][name], r[2 * b + 1][name]], axis)

        in_maps = []
        if nxt == "even":
            e = layer // 2
            lam_init = 0.8 - 0.6 * math.exp(-0.3 * layer)
            cw = {
                "wck": np.ascontiguousarray(f32(cmp_w_k[e]).reshape(32, 64, 64).transpose(1, 0, 2)),
                "wcv": np.ascontiguousarray(f32(cmp_w_v[e]).reshape(32, 64, 64).transpose(1, 0, 2)),
                "pkT": np.ascontiguousarray(f32(cmp_pos_k[e]).T),
                "pvT": np.ascontiguousarray(f32(cmp_pos_v[e]).T),
                "ftab": ftab,
                "lqk": np.ascontiguousarray(np.stack([f32(diff_lq1[e]), f32(diff_lk1[e]), f32(diff_lq2[e]),
                                                      f32(diff_lk2[e])], 0).reshape(1, 256)),
                "lami": np.array([[lam_init, 1.0 - lam_init]], np.float32),
                "subln": f32(diff_subln[e])[None, :],
            }
            nca = _prog(("att_even",), lambda: build_att_even(SEQ))
            for b in range(BATCH):
                nqT, kcT, vcT = cat("nqT", b, 2), cat("kcT", b, 2), cat("vcT", b, 2)
                ksT, kwT = cat("ksT", b, 2), cat("kwT", b, 2)
                vs, vw, gl = cat("vs", b, 0), cat("vw", b, 0), cat("gl", b, 0)
                dqT, dkT, dv = cat("dqT", b, 2), cat("dkT", b, 2), cat("dv", b, 0)
                for g in range(2):
                    m = dict(cw)
                    m["nqT"] = np.ascontiguousarray(nqT[4 * g:4 * g + 4])
                    m["kcT"] = np.ascontiguousarray(kcT[g])
                    m["vcT"] = np.ascontiguousarray(vcT[g])
                    m["ksT"] = np.ascontiguousarray(ksT[g])
                    m["kwT"] = np.ascontiguousarray(kwT[g])
                    m["vs"] = _pm(vs[:, 64 * g:64 * g + 64])
                    m["vw"] = _pm(vw[:, 64 * g:64 * g + 64])
                    m["gl"] = _pm(gl[:, 12 * g:12 * g + 12])
                    m["dqT"] = np.ascontiguousarray(dqT[4 * g:4 * g + 4])
                    m["dkT"] = np.ascontiguousarray(dkT[4 * g:4 * g + 4])
                    m["dv"] = _pm(dv[:, 256 * g:256 * g + 256])
                    in_maps.append(m)
            ra = _run(nca, in_maps)
            oT_full = []
            for b in range(BATCH):
                o0, o1 = ra[2 * b]["oT"], ra[2 * b + 1]["oT"]
                oT_full.append(np.concatenate([o0[0:256], o1[0:256], o0[256:512], o1[256:512]], 0))
        else:
            nca = _prog(("att_odd",), lambda: build_att_odd(SEQ, 2))
            for b in range(BATCH):
                qT, kT, v = cat("qT", b, 2), cat("kT", b, 2), cat("v", b, 0)
                for hh in range(2):
                    in_maps.append({
                        "qT": np.ascontiguousarray(qT[8 * hh:8 * hh + 8]),
                        "kT": np.ascontiguousarray(kT[8 * hh:8 * hh + 8]),
                        "v": _pm(v[:, 512 * hh:512 * hh + 512]),
                    })
            ra = _run(nca, in_maps)
            oT_full = [np.concatenate([ra[2 * b]["oT"], ra[2 * b + 1]["oT"]], 0) for b in range(BATCH)]
    return out.astype(np.float32)


def build_inproj(p, nxt, S=SEQ):
    GT = 512
    NGRP = S // GT
    NQ = S // 128
    halls = [p.dram(f"hall{i}", [256, 8 * 1024], BF16, "Internal") for i in range(NT // 1024)]
    hvs = [h_[:, :].rearrange("(r k) (c t) -> r k c t", r=2, c=8) for h_ in halls]
    if nxt == "even":
        NCW = 13
        T = {
            "nqT": p.dram("nqT", [4, 64, S], BF16, "Internal"), "kcT": p.dram("kcT", [64, S], BF16, "Internal"),
            "vcT": p.dram("vcT", [64, S], BF16, "Internal"), "ksT": p.dram("ksT", [64, S], BF16, "Internal"),
            "kwT": p.dram("kwT", [64, S], BF16, "Internal"), "vs": p.dram("vs", [128, NQ, 64], BF16, "Internal"),
            "vw": p.dram("vw", [128, NQ, 64], BF16, "Internal"), "gl": p.dram("gl", [128, NQ, 12], F32, "Internal"),
            "dqT": p.dram("dqT", [4, 64, S], BF16, "Internal"), "dkT": p.dram("dkT", [4, 64, S], BF16, "Internal"),
            "dv": p.dram("dv", [128, NQ, 256], BF16, "Internal"),
        }
        cos_d = p.dram("cosT", [128, S], F32, "ExternalInput")
        sin_d = p.dram("sinT", [128, S], F32, "ExternalInput")
        kinds = [("rope", 0.125), ("rope", 0.125), ("rope", 1.0), ("rope", 1.0), ("fm", 1.0), ("tm", 1.0),
                 ("tmf", 1.0), ("rope", 0.125), ("rope", 0.125), ("rope", 1.0), ("rope", 1.0), ("tm", 1.0),
                 ("tm", 1.0)]
    else:
        NCW = 12
        T = {"qT": p.dram("qT", [8, 64, S], BF16, "Internal"), "kT": p.dram("kT", [8, 64, S], BF16, "Internal"),
             "v": p.dram("v", [128, NQ, 512], BF16, "Internal")}
        kinds = [("fm", 0.125)] * 4 + [("fm", 1.0)] * 4 + [("tm", 1.0)] * 4
    win_d = p.dram("win", [NCW, 128, 8, 128], F32, "ExternalInput")
    ps = [p.psum(f"ps{i}", [128, 512], F32) for i in range(8)]
    psi = [0]

    def next_ps():
        t = ps[psi[0] % 8]
        psi[0] += 1
        return t

    st_r = p.ring("ipst", 2, [128, 8, 128], F32)
    wb = [p.sbuf(f"ipw{i}", [128, 8, 128], BF16) for i in range(NCW)]
    wr = {}
    for ci, (kind, scale) in enumerate(kinds):
        st = st_r.next()
        p.dma(st[:], win_d[ci])
        if scale == 1.0:
            p.copy("act" if ci % 2 == 0 else "dve", wb[ci][:], st[:])
        else:
            p.ts("dve", wb[ci][:], st[:], scale, None, ALU.mult)
        if kind == "rope":
            wr[ci] = p.sbuf(f"ipr{ci}", [128, 8, 128], BF16)
            sv = st[:].rearrange("p c (h f d) -> p (c h) f d", h=2, f=2, d=32)
            wv = wr[ci][:].rearrange("p c (h f d) -> p (c h) f d", h=2, f=2, d=32)
            p.ts("pool", wv[:, :, 0, :], sv[:, :, 1, :], -scale, None, ALU.mult)
            p.ts("pool", wv[:, :, 1, :], sv[:, :, 0, :], scale, None, ALU.mult)
    hT_r = p.ring("iphT", 2, [128, 8, GT], BF16)
    fo_r = p.ring("ipfo", 3, [128, GT], BF16)
    tmo_r = p.ring("iptmo", 2, [128, 4, 128], BF16)
    tmf_r = p.ring("iptmf", 2, [128, 4, 128], F32)
    if nxt == "even":
        cos_r = p.ring("ipcos", 2, [128, GT], F32)
        sin_r = p.ring("ipsin", 2, [128, GT], F32)
        t1_r = p.ring("ipt1", 2, [128, GT], F32)
        t2_r = p.ring("ipt2", 2, [128, GT], F32)
    for tg in range(NGRP):
        r, tl0 = divmod(tg * GT, NT)
        t0 = tg * GT
        kb0 = tg * 4
        hT = hT_r.next()
        pi, off = divmod(tl0, 1024)
        p.dma(hT[:], hvs[pi][r, :, :, off:off + GT])
        if nxt == "even":
            cosg = cos_r.next()
            sing = sin_r.next()
            p.dma(cosg[:], cos_d[:, t0:t0 + GT])
            p.dma(sing[:], sin_d[:, t0:t0 + GT])
        for ci, (kind, scale) in enumerate(kinds):
            w = wb[ci]
            if kind in ("fm", "rope"):
                A = next_ps()
                for c in range(8):
                    p.mm(A[:], w[:, c, :], hT[:, c, :], c == 0, c == 7)
                fo = fo_r.next()
                if kind == "rope":
                    Bp = next_ps()
                    for c in range(8):
                        p.mm(Bp[:], wr[ci][:, c, :], hT[:, c, :], c == 0, c == 7)
                    t1 = t1_r.next()
                    t2 = t2_r.next()
                    p.tt("dve", t1[:], A[:], cosg[:], ALU.mult)
                    p.tt("dve", t2[:], Bp[:], sing[:], ALU.mult)
                    p.tt("pool", fo[:], t1[:], t2[:], ALU.add)
                else:
                    p.copy("act", fo[:], A[:])
                ts_ = slice(t0, t0 + GT)
                if nxt == "even":
                    if ci in (0, 1):
                        dsts = [(T["nqT"][2 * ci:2 * ci + 2, :, ts_].rearrange("h d t -> (h d) t"), fo[:])]
                    elif ci == 2:
                        dsts = [(T["kcT"][:, ts_], fo[0:64, :]), (T["ksT"][:, ts_], fo[64:128, :])]
                    elif ci == 3:
                        dsts = [(T["kwT"][:, ts_], fo[0:64, :])]
                    elif ci == 4:
                        dsts = [(T["vcT"][:, ts_], fo[0:64, :])]
                    elif ci in (7, 8):
                        j = ci - 7
                        dsts = [(T["dqT"][2 * j:2 * j + 2, :, ts_].rearrange("h d t -> (h d) t"), fo[:])]
                    else:
                        j = ci - 9
                        dsts = [(T["dkT"][2 * j:2 * j + 2, :, ts_].rearrange("h d t -> (h d) t"), fo[:])]
                else:
                    nm = "qT" if ci < 4 else "kT"
                    j = ci % 4
                    dsts = [(T[nm][2 * j:2 * j + 2, :, ts_].rearrange("h d t -> (h d) t"), fo[:])]
                for (dv_, sv_) in dsts:
                    p.dma(dv_, sv_, eng="pool")
            else:
                A = next_ps()
                Av = A[:].rearrange("p (i n) -> p i n", i=4)
                for i in range(4):
                    for c in range(8):
                        p.mm(Av[:, i, :], hT[:, c, i * 128:(i + 1) * 128], w[:, c, :], c == 0, c == 7)
                to = tmo_r.next() if kind == "tm" else tmf_r.next()
                p.copy("act", to[:], Av)
                ks_ = slice(kb0, kb0 + 4)
                if nxt == "even":
                    if ci == 5:
                        dsts = [(T["vs"][:, ks_, :], to[:, :, 0:64]), (T["vw"][:, ks_, :], to[:, :, 64:128])]
                    elif ci == 6:
                        dsts = [(T["gl"][:, ks_, :], to[:, :, 0:12])]
                    else:
                        j = ci - 11
                        dsts = [(T["dv"][:, ks_, j * 128:(j + 1) * 128], to[:])]
                else:
                    j = ci - 8
                    dsts = [(T["v"][:, ks_, j * 128:(j + 1) * 128], to[:])]
                for (dv_, sv_) in dsts:
                    p.dma(dv_, sv_, eng="pool")
    return T


PAIRS = [[0, 1], [2, 3], [4, 5], [6, 7]]


class RowSplit:
    def __init__(self, a, b, n):
        self.a, self.b, self.n = a, b, n

    def __getitem__(self, idx):
        rs, cs = idx
        if rs.start >= self.n:
            return self.b[rs.start - self.n:rs.stop - self.n, cs]
        return self.a[rs, cs]


def build_fused():
    p = Prog()
    p.enable_arena(200 * 1024)
    ext = lambda name, shape, dt=F32: p.dram_real(name, shape, dt, "ExternalInput")
    x_in = ext("x", [NT, D_MODEL])
    out_d = p.dram_real("out", [NT, D_MODEL], F32, "ExternalOutput")
    xbuf = p.dram_real("xbuf", [NT, D_MODEL], F32, "Internal")
    hsend = [p.dram_real(f"hsend{i}", [128, 8, 1024], BF16, "Internal") for i in range(NT // 1024)]
    hall = [p.dram_real(f"hall{i}", [256, 8 * 1024], BF16, "Internal") for i in range(NT // 1024)]
    osend = p.dram_real("osend", [256, SEQ], BF16, "Internal")
    osendb = p.dram_real("osendb", [256, SEQ], BF16, "Internal")
    oall = p.dram_real("oall", [512, SEQ], BF16, "Internal")
    oallb = p.dram_real("oallb", [512, SEQ], BF16, "Internal")
    cosT = ext("cosT", [128, SEQ])
    sinT = ext("sinT", [128, SEQ])
    ftab = ext("ftab", [128, SEQ // 128, SEQ // 64])
    g1 = [ext(f"g1_{l}", [1, D_MODEL]) for l in range(DEPTH + 1)]
    win = [ext(f"win_{l}", [13 if l % 2 == 0 else 12, 128, 8, 128]) for l in range(DEPTH)]
    wo = [ext(f"wo_{l}", [8, 128, D_MODEL]) for l in range(DEPTH)]
    g2 = [ext(f"g2_{l}", [1, D_MODEL]) for l in range(DEPTH)]
    wgu = [ext(f"wgu_{l}", [NFC, 128, 2, 8, 128]) for l in range(DEPTH)]
    wd = [ext(f"wd_{l}", [NFC, 128, D_MODEL]) for l in range(DEPTH)]
    ev = {}
    for e in range(2):
        ev[e] = {"wck": ext(f"wck_{e}", [64, 32, 64]), "wcv": ext(f"wcv_{e}", [64, 32, 64]),
                 "pkT": ext(f"pkT_{e}", [64, 32]), "pvT": ext(f"pvT_{e}", [64, 32]),
                 "lqk": ext(f"lqk_{e}", [1, 256]), "lami": ext(f"lami_{e}", [1, 2]),
                 "subln": ext(f"subln_{e}", [1, 128])}
    scratch = {}
    for layer in range(DEPTH + 1):
        has_prev = layer > 0
        nxt = None if layer == DEPTH else ("even" if layer % 2 == 0 else "odd")
        ov = {"x": x_in if layer == 0 else xbuf, "xo": xbuf, "out": out_d, "g1": g1[layer],
              "hsend0": hsend[0], "hsend1": hsend[1], "oT": oall, "oTb": oallb}
        if has_prev:
            ov.update({"wo": wo[layer - 1], "g2": g2[layer - 1], "wgu": wgu[layer - 1], "wd": wd[layer - 1]})
        p.dram_override = ov
        gather_h = lambda i: p.collective("AllGather", hall[i].all(),
                                          hsend[i][:, :, :].rearrange("k c t -> k (c t)"), PAIRS)
        build_tok2(p, has_prev, nxt, pass_hook=gather_h if nxt is not None else None)
        if nxt is None:
            break
        p.phase_reset()
        ov = dict(scratch)
        ov.update({"hall0": hall[0], "hall1": hall[1], "win": win[layer], "cosT": cosT, "sinT": sinT})
        p.dram_override = ov
        T = build_inproj(p, nxt)
        scratch.update(T)
        p.phase_reset()
        ov = dict(scratch)
        ov["oT"] = RowSplit(osend, osendb, 256)
        if nxt == "even":
            ov.update(ev[layer // 2])
            ov["ftab"] = ftab
        first_ag = lambda: p.collective("AllGather", oall.all(), osend.all(), PAIRS)
        if nxt == "even":
            p.dram_override = ov
            build_att_even(SEQ, p=p, mid_hook=first_ag)
        else:
            p.dram_override = ov
            build_att_odd(SEQ, 2, p=p, mid_hook=first_ag)
        p.collective("AllGather", oallb.all(), osendb.all(), PAIRS)
        p.phase_reset()
    return p


def lay_win_even_core(w, g):
    z = lambda a: np.pad(a, ((0, 0), (0, 128 - a.shape[1])))
    c = [w[:, 256 * g:256 * g + 128], w[:, 256 * g + 128:256 * g + 256],
         np.concatenate([w[:, 512 + 64 * g:576 + 64 * g], w[:, 768 + 64 * g:832 + 64 * g]], 1),
         z(w[:, 1024 + 64 * g:1088 + 64 * g]), z(w[:, 640 + 64 * g:704 + 64 * g]),
         np.concatenate([w[:, 896 + 64 * g:960 + 64 * g], w[:, 1152 + 64 * g:1216 + 64 * g]], 1),
         z(w[:, 1280 + 12 * g:1292 + 12 * g]),
         w[:, 1304 + 256 * g:1304 + 256 * g + 128], w[:, 1304 + 256 * g + 128:1304 + 256 * g + 256],
         w[:, 1816 + 256 * g:1816 + 256 * g + 128], w[:, 1816 + 256 * g + 128:1816 + 256 * g + 256],
         w[:, 2328 + 256 * g:2328 + 256 * g + 128], w[:, 2328 + 256 * g + 128:2328 + 256 * g + 256]]
    wp = np.stack(c, 0)
    return np.ascontiguousarray(wp.reshape(13, 8, 128, 128).transpose(0, 2, 1, 3))


def lay_win_odd_core(w, hh):
    c = []
    for base in (0, 1024, 2048):
        for j in range(4):
            a = base + 512 * hh + 128 * j
            c.append(w[:, a:a + 128])
    wp = np.stack(c, 0)
    return np.ascontiguousarray(wp.reshape(12, 8, 128, 128).transpose(0, 2, 1, 3))


def kernel_fused(x, norm_mix, norm_ffn, norm_final, even_w_in, even_w_out,
                 cmp_pos_k, cmp_w_k, cmp_pos_v, cmp_w_v,
                 diff_lq1, diff_lk1, diff_lq2, diff_lk2, diff_subln,
                 odd_w_in, odd_w_out, ffn_w_gate, ffn_w_up, ffn_w_down):
    f32 = lambda a: np.ascontiguousarray(np.asarray(a, dtype=np.float32))
    x = f32(x)
    nc = _prog(("fused",), build_fused)
    cosT, sinT = rope_tables_T(np.arange(SEQ))
    common = {"cosT": cosT, "sinT": sinT, "ftab": make_ftab(SEQ)}
    for l in range(DEPTH):
        common[f"g1_{l}"] = f32(norm_mix[l])[None, :]
        w_out = f32(even_w_out[l // 2]) if l % 2 == 0 else f32(odd_w_out[l // 2])
        if l % 2 == 1:
            w_out = np.concatenate([w_out[0:256], w_out[512:768], w_out[256:512], w_out[768:1024]], 0)
        common[f"wo_{l}"] = lay_wo(w_out)
        common[f"g2_{l}"] = f32(norm_ffn[l])[None, :]
        common[f"wgu_{l}"] = lay_wgu(f32(ffn_w_gate[l]), f32(ffn_w_up[l]))
        common[f"wd_{l}"] = lay_wd(f32(ffn_w_down[l]))
    common[f"g1_{DEPTH}"] = f32(norm_final)[None, :]
    for e in range(2):
        lam_init = 0.8 - 0.6 * math.exp(-0.3 * (2 * e))
        common[f"wck_{e}"] = np.ascontiguousarray(f32(cmp_w_k[e]).reshape(32, 64, 64).transpose(1, 0, 2))
        common[f"wcv_{e}"] = np.ascontiguousarray(f32(cmp_w_v[e]).reshape(32, 64, 64).transpose(1, 0, 2))
        common[f"pkT_{e}"] = np.ascontiguousarray(f32(cmp_pos_k[e]).T)
        common[f"pvT_{e}"] = np.ascontiguousarray(f32(cmp_pos_v[e]).T)
        common[f"lqk_{e}"] = np.ascontiguousarray(np.stack([f32(diff_lq1[e]), f32(diff_lk1[e]), f32(diff_lq2[e]),
                                                            f32(diff_lk2[e])], 0).reshape(1, 256))
        common[f"lami_{e}"] = np.array([[lam_init, 1.0 - lam_init]], np.float32)
        common[f"subln_{e}"] = f32(diff_subln[e])[None, :]
    wins = {}
    for par in range(2):
        for l in range(DEPTH):
            wins[(par, l)] = (lay_win_even_core(f32(even_w_in[l // 2]), par) if l % 2 == 0
                              else lay_win_odd_core(f32(odd_w_in[l // 2]), par))
    in_maps = []
    for c in range(8):
        m = dict(common)
        m["x"] = np.ascontiguousarray(x[c // 2, (c % 2) * NT:(c % 2 + 1) * NT])
        for l in range(DEPTH):
            m[f"win_{l}"] = wins[(c % 2, l)]
        in_maps.append(m)
    r = _run(nc, in_maps)
    out = np.stack([np.concatenate([r[2 * b]["out"], r[2 * b + 1]["out"]], 0) for b in range(BATCH)], 0)
    return out.astype(np.float32)


def kernel(**inputs):
    return kernel_fused(**inputs)


def build_tok2(p, has_prev, nxt, pass_hook=None):
    PT = 1024
    NPASS = NT // PT
    TPP = PT // 128
    NH2 = NFC // 2
    x_d = p.dram("x", [NT, D_MODEL], F32, "ExternalInput")
    g1_d = p.dram("g1", [1, D_MODEL], F32, "ExternalInput")
    if getattr(p, "par", None) is None:
        p.par = p.nc.partition_id() % 2
    par = p.par
    if has_prev:
        oT_d = p.dram("oT", [512, 2 * NT], BF16, "Internal")
        oTb_d = p.dram("oTb", [512, 2 * NT], BF16, "Internal")
        wo_d = p.dram("wo", [8, 128, D_MODEL], F32, "ExternalInput")
        g2_d = p.dram("g2", [1, D_MODEL], F32, "ExternalInput")
        wgu_d = p.dram("wgu", [NFC, 128, 2, 8, 128], F32, "ExternalInput")
        wd_d = p.dram("wd", [NFC, 128, D_MODEL], F32, "ExternalInput")
    if nxt is not None:
        xo_d = p.dram("xo", [NT, D_MODEL], F32, "Internal")
        hs_d = [p.dram(f"hsend{i}", [128, 8, PT], BF16, "Internal") for i in range(NPASS)]
    else:
        out_d = p.dram("out", [NT, D_MODEL], F32, "ExternalOutput")
    ident = make_ident(p)
    xg = p.sbuf("xg", [128, TPP, D_MODEL], F32)
    hT = p.sbuf("hT", [128, 8, PT], BF16)
    hs_r = p.ring("hs", 2, [128, D_MODEL], BF16)
    junk = p.sbuf("junk", [128, D_MODEL], F32)
    ss_r = p.ring("ss", 2, [128, 1], F32)
    gB1 = p.sbuf("gB1", [128, D_MODEL], F32)
    p.dma(gB1[:], g1_d[0:1, :].to_broadcast([128, D_MODEL]))
    ps = [p.psum(f"ps{i}", [128, 512], F32) for i in range(8)]
    psi = [0]

    def next_ps():
        t = ps[psi[0] % 8]
        psi[0] += 1
        return t

    if has_prev:
        st_r = p.ring("st", 3, [128, 2, 8, 128], F32)
        wb_r = p.ring("wb", 3, [128, 2, 8, 128], BF16)
        gB2 = p.sbuf("gB2", [128, D_MODEL], F32)
        p.dma(gB2[:], g2_d[0:1, :].to_broadcast([128, D_MODEL]))
        wo = p.sbuf("wo_sb", [128, 8, D_MODEL], BF16)
        for c in range(8):
            st = st_r.next()
            stv = st[:].rearrange("p a b c -> p (a b c)")[:, 0:D_MODEL]
            p.dma(stv, wo_d[c])
            p.copy("dve", wo[:, c, :], stv)
        oTg_r = p.ring("oTg", 2, [128, 8, 512], BF16)
        aT = p.sbuf("aT", [128, NH2, PT], BF16)
        wdr = p.sbuf("wdr", [128, NH2, D_MODEL], BF16)
        sg_r = p.ring("sg", 2, [128, 512], F32)
    else:
        og_r = None
    og_r = p.ring("og", 2, [128, D_MODEL], F32) if nxt is None else None

    def rstd_of(i):
        ss = ss_r.next()
        p.act(junk[:], xg[:, i, :], AF.Square, accum_out=ss[:])
        p.ts("dve", ss[:], ss[:], 1.0 / D_MODEL, NORM_EPS, ALU.mult, ALU.add)
        p.act(ss[:], ss[:], AF.Ln)
        p.act(ss[:], ss[:], AF.Exp, scale=-0.5)
        return ss

    def rmsnorm_to_hT(gB):
        for i in range(TPP):
            rs = rstd_of(i)
            hs = hs_r.next()
            p.stt("dve", hs[:], xg[:, i, :], rs[:], gB[:], ALU.mult, ALU.mult)
            pt = next_ps()
            ptv = pt[:].bitcast(BF16).rearrange("p (c t) -> p c t", c=8)
            for c in range(8):
                p.tr(ptv[:, c, :], hs[:, c * 128:(c + 1) * 128], ident[:])
            p.copy("dve", hT[:, :, i * 128:(i + 1) * 128], ptv)

    for ps_ in range(NPASS):
        t0 = ps_ * PT
        p.dma(xg[:], x_d[t0:t0 + PT, :].rearrange("(i p) d -> p i d", p=128))
        if has_prev:
            for sub in range(PT // 512):
                oTg = oTg_r.next()
                tt0 = t0 + sub * 512
                p.dma(oTg[:, 0:4, :], oT_d[:, bass.ds(par * NT + tt0, 512)].rearrange("(c f) t -> f c t", f=128))
                p.dma(oTg[:, 4:8, :], oTb_d[:, bass.ds(par * NT + tt0, 512)].rearrange("(c f) t -> f c t", f=128))
                for i4 in range(4):
                    i = sub * 4 + i4
                    for hf in range(2):
                        acc = next_ps()
                        for c in range(8):
                            p.mm(acc[:], oTg[:, c, i4 * 128:(i4 + 1) * 128], wo[:, c, hf * 512:(hf + 1) * 512],
                                 c == 0, c == 7)
                        xs = xg[:, i, hf * 512:(hf + 1) * 512]
                        p.tt("dve", xs, xs, acc[:], ALU.add)
            rmsnorm_to_hT(gB2)
            for hh in range(2):
                for n in range(NH2):
                    nn = hh * NH2 + n
                    st = st_r.next()
                    wb = wb_r.next()
                    p.dma(st[:], wgu_d[nn])
                    p.copy("act", wb[:], st[:])
                    st2 = st_r.next()
                    stv = st2[:].rearrange("p a b c -> p (a b c)")[:, 0:D_MODEL]
                    p.dma(stv, wd_d[nn])
                    p.copy("dve", wdr[:, n, :], stv)
                    for sub in range(PT // 512):
                        G = next_ps()
                        Uu = next_ps()
                        for c in range(8):
                            p.mm(G[:], wb[:, 0, c, :], hT[:, c, sub * 512:(sub + 1) * 512], c == 0, c == 7)
                        for c in range(8):
                            p.mm(Uu[:], wb[:, 1, c, :], hT[:, c, sub * 512:(sub + 1) * 512], c == 0, c == 7)
                        sg = sg_r.next()
                        p.act(sg[:], G[:], AF.Silu)
                        p.tt("dve", aT[:, n, sub * 512:(sub + 1) * 512], sg[:], Uu[:], ALU.mult)
                for i in range(TPP):
                    for hf in range(2):
                        acc = next_ps()
                        for n in range(NH2):
                            p.mm(acc[:], aT[:, n, i * 128:(i + 1) * 128], wdr[:, n, hf * 512:(hf + 1) * 512],
                                 n == 0, n == NH2 - 1)
                        xs = xg[:, i, hf * 512:(hf + 1) * 512]
                        p.tt("dve", xs, xs, acc[:], ALU.add)
        if nxt is None:
            for i in range(TPP):
                rs = rstd_of(i)
                og = og_r.next()
                p.stt("dve", og[:], xg[:, i, :], rs[:], gB1[:], ALU.mult, ALU.mult)
                p.dma(out_d[t0 + i * 128:t0 + (i + 1) * 128, :], og[:], eng="pool")
            continue
        p.dma(xo_d[t0:t0 + PT, :].rearrange("(i p) d -> p i d", p=128), xg[:], eng="pool")
        rmsnorm_to_hT(gB1)
        p.dma(hs_d[ps_][:, :, :], hT[:], eng="pool")
        if pass_hook is not None:
            pass_hook(ps_)
    return p
```

```python
import math
import numpy as np
import ml_dtypes
from contextlib import ExitStack
import concourse.bass as bass
import concourse.mybir as mybir
from concourse.bass_utils import run_bass_kernel_spmd

F32 = mybir.dt.float32
BF16 = mybir.dt.bfloat16
AF = mybir.ActivationFunctionType
ALU = mybir.AluOpType
AX = mybir.AxisListType
NPBF16 = ml_dtypes.bfloat16

SEM_LIMIT = 30000
N_DMA_SEMS = {"sp": 14, "pool": 6, "act": 4}

D_MODEL = 1024
SEQ = 4096
BATCH = 4
DEPTH = 4
HD = 64
FFN_H = 2816
NFC = FFN_H // 128
EVEN_IN = 2840
NORM_EPS = 1e-6
NT = 2048


class Dep:
    __slots__ = ("last_w", "readers")

    def __init__(self):
        self.last_w = None
        self.readers = []


class V:
    __slots__ = ("tl", "ap")

    def __init__(self, tl, ap):
        self.tl = tl
        self.ap = ap

    def __getitem__(self, idx):
        return V(self.tl, self.ap[idx])

    def rearrange(self, pattern, **kw):
        return V(self.tl, self.ap.rearrange(pattern, **kw))

    def bitcast(self, dt):
        return V(self.tl, self.ap.bitcast(dt))

    def to_broadcast(self, shape):
        return V(self.tl, self.ap.to_broadcast(list(shape)))

    def unsqueeze(self, ax):
        return V(self.tl, self.ap.unsqueeze(ax))

    @property
    def shape(self):
        return self.ap.shape


class Tl:
    def __init__(self, h, name, is_ap=False):
        self.h = h
        self.name = name
        self.dep = Dep()
        self.is_ap = is_ap

    def __getitem__(self, idx):
        return V(self, self.h[idx])

    def all(self):
        return V(self, self.h[:] if not self.is_ap else self.h)

    def __hash__(self):
        return id(self)


class Op:
    __slots__ = ("eng", "fn", "deps", "signal", "event", "is_dma", "dma_prev", "idx", "inc")


def _ap(v):
    return v.ap if isinstance(v, V) else v


class Prog:
    ENGS = ("pe", "act", "dve", "pool", "sp")

    def __init__(self):
        self.nc = bass.Bass("TRN2", target_bir_lowering=False)
        self.es = ExitStack()
        self.ops = {e: [] for e in self.ENGS}
        self.nops = 0
        self.same_engine_sync = True
        self._uid = 0
        self.arena = None
        self.arena_off = 0
        self.arena_base = 0
        self.ps_pool = None
        self.ps_i = 0
        self.dram_override = {}
        self.dma_pending = []
        self.bar = None

    def dram(self, name, shape, dt, kind):
        if name in self.dram_override:
            return self.dram_override[name]
        return self.dram_real(name, shape, dt, kind)

    def dram_real(self, name, shape, dt, kind):
        h = self.nc.dram_tensor(name, list(shape), dt, kind=kind)
        return Tl(h.ap(), name, is_ap=True)

    def sbuf(self, name, shape, dt):
        if self.arena is not None:
            esz = {F32: 4, BF16: 2}[dt]
            free = 1
            for d in shape[1:]:
                free *= d
            nb = (free * esz + 63) // 64 * 64
            off = self.arena_off
            assert off + nb <= self.arena_size, ("arena overflow", name, off, nb)
            self.arena_off = off + nb
            self.arena_peak = max(getattr(self, "arena_peak", 0), self.arena_off)
            ap = self.arena[0:shape[0], off:off + free * esz].bitcast(dt)
            if len(shape) == 3:
                ap = ap.rearrange("p (a b) -> p a b", a=shape[1])
            elif len(shape) == 4:
                ap = ap.rearrange("p (a b c) -> p a b c", a=shape[1], b=shape[2])
            return Tl(ap, name, is_ap=True)
        h = self.es.enter_context(self.nc.sbuf_tensor(name, list(shape), dt))
        return Tl(h, name)

    def psum(self, name, shape, dt):
        if self.ps_pool is not None:
            assert list(shape) == [128, 512] and dt == F32
            t = self.ps_pool[self.ps_i % 8]
            self.ps_i += 1
            return t
        h = self.es.enter_context(self.nc.psum_tensor(name, list(shape), dt))
        return Tl(h, name)

    def enable_arena(self, nbytes):
        U8 = mybir.dt.uint8
        h = self.es.enter_context(self.nc.sbuf_tensor("arena", [128, nbytes], U8))
        self.arena_size = nbytes
        self.bar = {e: Tl(self.es.enter_context(self.nc.sbuf_tensor(f"bar_{e}", [128, 8], F32)), f"bar_{e}")
                    for e in ("act", "dve", "pool", "sp", "src", "w_act", "w_dve", "w_pool", "w_sp")}
        self.bar_lhs = Tl(self.es.enter_context(self.nc.sbuf_tensor("bar_lhs", [128, 8], BF16)), "bar_lhs")
        self.ps_pool = []
        self.ps_pair = {}
        for j in range(4):
            hh = self.es.enter_context(self.nc.psum_tensor(f"psd{j}", [128, 1024], F32))
            a = Tl(hh[:, 0:512], f"psb{2 * j}", is_ap=True)
            b_ = Tl(hh[:, 512:1024], f"psb{2 * j + 1}", is_ap=True)
            self.ps_pool += [a, b_]
            self.ps_pair[id(a)] = (hh[:, :], b_)
        self.arena = h[:]
        self.memset("pool", self.bar["src"][:], 0.0)
        self.memset("pool", self.bar_lhs[:], 0.0)

    def phase_reset(self):
        b = self.bar
        pend = list(self.dma_pending)
        self.dma_pending = []
        pst = self.ps_pool[7]
        m = {}
        m["act"] = self.copy("act", b["act"][:], b["src"][:])
        m["dve"] = self.copy("dve", b["dve"][:], b["src"][:])
        m["pool"] = self.copy("pool", b["pool"][:], b["src"][:])
        m["sp"] = self.dma(b["sp"][:], b["src"][:])
        m["pe"] = self.mm(pst[0:8, 0:8], self.bar_lhs[:], self.bar_lhs[:], True, True)
        allr = [b["act"][:], b["dve"][:], b["pool"][:], b["sp"][:], pst[0:8, 0:8]]
        self.add("act", lambda e, o=b["w_act"][:].ap, i=b["src"][:].ap: e.copy(o, i), allr, [b["w_act"][:]], extra=pend)
        self.add("dve", lambda e, o=b["w_dve"][:].ap, i=b["src"][:].ap: e.tensor_copy(o, i), allr, [b["w_dve"][:]], extra=pend)
        self.add("pool", lambda e, o=b["w_pool"][:].ap, i=b["src"][:].ap: e.tensor_copy(o, i), allr, [b["w_pool"][:]], extra=pend)
        self.add("sp", lambda e, o=b["w_sp"][:].ap, i=b["src"][:].ap: e.dma_start(out=o, in_=i), allr, [b["w_sp"][:]],
                 dma=True, extra=pend)
        self.add("pe", lambda e, o=pst[0:8, 8:16].ap, l=self.bar_lhs[:].ap: e.matmul(o, l, l, start=True, stop=True),
                 allr, [pst[0:8, 8:16]], extra=pend)
        self.arena_off = self.arena_base
        self.ps_i = 0

    def ring(self, name, n, shape, dt, space="sbuf"):
        f = self.sbuf if space == "sbuf" else self.psum
        return Ring([f(f"{name}{i}", shape, dt) for i in range(n)])

    def add(self, eng, fn, reads=(), writes=(), dma=False, extra=()):
        op = Op()
        op.eng = eng
        op.fn = fn
        op.is_dma = dma
        op.signal = False
        op.event = None
        op.dma_prev = None
        op.inc = 16
        op.idx = self.nops
        self.nops += 1
        deps = {}
        rt = []
        wt = []
        for r in reads:
            if isinstance(r, V):
                rt.append(r.tl)
        for w in writes:
            if isinstance(w, V):
                wt.append(w.tl)
        for r in rt:
            lw = r.dep.last_w
            if lw is not None:
                deps[lw.idx] = lw
        for w in wt:
            lw = w.dep.last_w
            if lw is not None:
                deps[lw.idx] = lw
            for rd in w.dep.readers:
                deps[rd.idx] = rd
        for d in extra:
            deps[d.idx] = d
        out = []
        for d in deps.values():
            if d.eng == eng and not d.is_dma and not dma:
                if eng == "pe" or not self.same_engine_sync:
                    continue
            d.signal = True
            out.append(d)
        op.deps = out
        for r in rt:
            r.dep.readers.append(op)
        for w in wt:
            w.dep.last_w = op
            w.dep.readers = []
        self.ops[eng].append(op)
        if dma:
            self.dma_pending.append(op)
        return op

    def dma(self, out, in_, eng="sp"):
        o, i = out.ap, in_.ap
        return self.add(eng, lambda e: e.dma_start(out=o, in_=i), [in_], [out], dma=True)

    def collective(self, kind, out, in_, groups):
        o, i = out.ap, in_.ap
        op = self.add("pool", lambda e: e.collective_compute(kind, ALU.bypass, replica_groups=groups,
                                                             ins=[i], outs=[o]), [in_], [out], dma=True)
        op.inc = 1
        return op

    def mm(self, out, lhsT, rhs, start, stop):
        o, l, r = out.ap, lhsT.ap, rhs.ap
        return self.add("pe", lambda e: e.matmul(o, l, r, start=start, stop=stop), [lhsT, rhs], [out])

    def tr(self, out, in_, ident):
        o, i, d = out.ap, in_.ap, ident.ap
        return self.add("pe", lambda e: e.transpose(o, i, d), [in_, ident], [out])

    def pair(self, tl):
        full, other = self.ps_pair[id(tl)]
        return V(tl, full), other.all()

    def act(self, out, in_, func, bias=None, scale=None, accum_out=None, extra_r=(), extra_w=()):
        kw = {}
        rd = [in_] + list(extra_r)
        wr = [out] + list(extra_w)
        if bias is not None:
            kw["bias"] = _ap(bias)
            rd.append(bias)
        if scale is not None:
            kw["scale"] = _ap(scale)
            rd.append(scale)
        if accum_out is not None:
            kw["accum_out"] = accum_out.ap
            wr.append(accum_out)
        o, i = out.ap, in_.ap
        return self.add("act", lambda e: e.activation(out=o, in_=i, func=func, **kw), rd, wr)

    def tt(self, eng, out, in0, in1, op):
        o, a, b = out.ap, in0.ap, in1.ap
        return self.add(eng, lambda e: e.tensor_tensor(o, a, b, op), [in0, in1], [out])

    def ts(self, eng, out, in0, s1, s2, op0, op1=None, accum_out=None):
        o, a = out.ap, in0.ap
        a1, a2 = _ap(s1), _ap(s2)
        wr = [out]
        kw = {}
        if accum_out is not None:
            kw["accum_out"] = accum_out.ap
            wr.append(accum_out)
        if op1 is None:
            return self.add(eng, lambda e: e.tensor_scalar(o, a, a1, None, op0, **kw), [in0, s1], wr)
        return self.add(eng, lambda e: e.tensor_scalar(o, a, a1, a2, op0, op1, **kw), [in0, s1, s2], wr)

    def stt(self, eng, out, in0, scalar, in1, op0, op1):
        o, a, s, b = out.ap, in0.ap, _ap(scalar), in1.ap
        return self.add(eng, lambda e: e.scalar_tensor_tensor(o, a, s, b, op0, op1), [in0, scalar, in1], [out])

    def copy(self, eng, out, in_):
        o, i = out.ap, in_.ap
        if eng == "act":
            return self.add(eng, lambda e: e.copy(o, i), [in_], [out])
        return self.add(eng, lambda e: e.tensor_copy(o, i), [in_], [out])

    def memset(self, eng, out, val):
        o = out.ap
        return self.add(eng, lambda e: e.memset(o, val), [], [out])

    def asel(self, out, in_, pattern, cmp, fill, base, cm):
        o, i = out.ap, in_.ap
        return self.add("pool", lambda e: e.affine_select(out=o, in_=i, pattern=pattern, compare_op=cmp,
                                                          fill=fill, base=base, channel_multiplier=cm),
                        [in_], [out])

    def recip(self, out, in_):
        o, i = out.ap, in_.ap
        return self.add("dve", lambda e: e.reciprocal(o, i), [in_], [out])

    def sqrt(self, out, in_):
        o, i = out.ap, in_.ap
        return self.add("act", lambda e: e.sqrt(o, i), [in_], [out])

    def max8(self, out, in_):
        o, i = out.ap, in_.ap
        return self.add("dve", lambda e: e.max(out=o, in_=i), [in_], [out])

    def match_replace(self, out, vals, in_, imm):
        o, v, i = out.ap, vals.ap, in_.ap
        return self.add("dve", lambda e: e.match_replace(out=o, in_to_replace=v, in_values=i, imm_value=imm),
                        [vals, in_], [out])

    def reduce(self, eng, out, in_, op, axis=AX.X):
        o, i = out.ap, in_.ap
        return self.add(eng, lambda e: e.tensor_reduce(o, i, axis, op), [in_], [out])

    def build(self):
        nc = self.nc
        es = self.es
        eng_sems = {}
        for e in self.ENGS:
            nsig = sum(1 for o in self.ops[e] if o.signal and not o.is_dma)
            nep = max(1, -(-nsig // SEM_LIMIT))
            eng_sems[e] = [es.enter_context(nc.semaphore(f"s_{e}_{i}")) for i in range(nep)]
        dma_sems = {}
        for e in self.ENGS:
            nd = sum(1 for o in self.ops[e] if o.is_dma)
            if nd:
                n = min(N_DMA_SEMS.get(e, 4), nd)
                dma_sems[e] = [es.enter_context(nc.semaphore(f"d_{e}_{i}")) for i in range(n)]
        cc_sem = es.enter_context(nc.semaphore("s_cc"))
        final_dma = {}
        for e in self.ENGS:
            cnt = 0
            dcnt = 0
            uses = {}
            for o in self.ops[e]:
                if o.is_dma:
                    pool = dma_sems[e]
                    if o.inc == 1:
                        s = cc_sem
                    else:
                        s = pool[dcnt % len(pool)]
                        dcnt += 1
                    k = uses.get(id(s), 0)
                    o.dma_prev = (s, k)
                    uses[id(s)] = k + o.inc
                    o.event = (s, k + o.inc)
                    final_dma[(e, id(s))] = (s, k + o.inc)
                elif o.signal:
                    ep = cnt // SEM_LIMIT
                    o.event = (eng_sems[e][ep], cnt % SEM_LIMIT + 1)
                    cnt += 1
        ops = self.ops
        nwaits = {e: 0 for e in self.ENGS}

        def emit(e, eng):
            seen = {}

            def wait(s, v):
                if v > 0 and seen.get(id(s), 0) < v:
                    eng.wait_ge(s, v)
                    seen[id(s)] = v
                    nwaits[e] += 1

            for o in ops[e]:
                need = {}
                for d in o.deps:
                    s_, v_ = d.event
                    if need.get(id(s_), (None, 0))[1] < v_:
                        need[id(s_)] = (s_, v_)
                for s_, v_ in need.values():
                    wait(s_, v_)
                if o.is_dma:
                    wait(*o.dma_prev)
                ins = o.fn(eng)
                if o.is_dma:
                    ins.then_inc(o.event[0], o.inc)
                elif o.signal:
                    ins.then_inc(o.event[0], 1)
            for (ee, _), (s, v) in final_dma.items():
                if ee == e:
                    wait(s, v)

        with nc.Block() as block:
            @block.tensor
            def _(eng):
                emit("pe", eng)

            @block.scalar
            def _(eng):
                emit("act", eng)

            @block.vector
            def _(eng):
                emit("dve", eng)

            @block.gpsimd
            def _(eng):
                emit("pool", eng)

            @block.sync
            def _(eng):
                emit("sp", eng)

        self.stats = {e: (len(ops[e]), nwaits[e]) for e in self.ENGS}
        es.close()
        return nc


class Ring:
    def __init__(self, tiles):
        self.tiles = tiles
        self.i = 0

    def next(self):
        t = self.tiles[self.i % len(self.tiles)]
        self.i += 1
        return t


def run_pipeline(units, depth=1):
    n = len(units)

    def doA(u):
        if "pre" in u:
            u["pre"]()
        if "A" in u:
            u["A"]()
    for j in range(min(depth, n)):
        doA(units[j])
    for k in range(n):
        if k + depth < n:
            doA(units[k + depth])
        u = units[k]
        for key in ("B", "C", "post"):
            if key in u:
                u[key]()


def make_ident(p, dt=BF16):
    ident = p.sbuf("ident", [128, 128], dt)
    p.memset("pool", ident[:], 1.0)
    p.asel(ident[:], ident[:], [[-1, 128]], ALU.is_equal, 0.0, 0, 1)
    return ident


def even_chunks():
    ch = []
    for j in range(4):
        ch.append(("rope", 0.125, "nqT", j))
    ch.append(("rope", 1.0, "kcT", 0))
    ch.append(("fm", 1.0, "vcT", 0))
    ch.append(("rope", 1.0, "ksT", 0))
    ch.append(("tm", 1.0, "vs", 0))
    ch.append(("rope", 1.0, "kwT", 0))
    ch.append(("tm", 1.0, "vw", 0))
    ch.append(("tmf", 1.0, "gl", 0))
    for j in range(4):
        ch.append(("rope", 0.125, "dqT", j))
    for j in range(4):
        ch.append(("rope", 1.0, "dkT", j))
    for j in range(4):
        ch.append(("tm", 1.0, "dv", j))
    return ch


def odd_chunks():
    ch = []
    for j in range(8):
        ch.append(("fm", 0.125, "qT", j))
    for j in range(8):
        ch.append(("fm", 1.0, "kT", j))
    for j in range(8):
        ch.append(("tm", 1.0, "v", j))
    return ch


def inproj_outputs(nxt):
    if nxt == "even":
        return {
            "nqT": ([8, 64, NT], BF16), "kcT": ([2, 64, NT], BF16), "vcT": ([2, 64, NT], BF16),
            "ksT": ([2, 64, NT], BF16), "kwT": ([2, 64, NT], BF16),
            "vs": ([NT, 128], BF16), "vw": ([NT, 128], BF16), "gl": ([NT, 128], F32),
            "dqT": ([8, 64, NT], BF16), "dkT": ([8, 64, NT], BF16), "dv": ([NT, 512], BF16),
        }
    return {"qT": ([16, 64, NT], BF16), "kT": ([16, 64, NT], BF16), "v": ([NT, 1024], BF16)}


def build_tok(has_prev, nxt, p=None, fused=False):
    p = p or Prog()
    GT = 512
    NG = NT // GT
    TPG = GT // 128
    x_d = p.dram("x", [NT, D_MODEL], F32, "ExternalInput")
    if has_prev:
        oT_d = p.dram("oT", [512 if fused else D_MODEL, 2 * NT if fused else NT], BF16, "ExternalInput")
        oTb_d = p.dram("oTb", [512, 2 * NT], BF16, "ExternalInput") if fused else None
        par = (p.nc.partition_id() % 2) if fused else None
        wo_d = p.dram("wo", [8, 128, D_MODEL], F32, "ExternalInput")
        g2_d = p.dram("g2", [1, D_MODEL], F32, "ExternalInput")
        wgu_d = p.dram("wgu", [NFC, 128, 2, 8, 128], F32, "ExternalInput")
        wd_d = p.dram("wd", [NFC, 128, D_MODEL], F32, "ExternalInput")
    g1_d = p.dram("g1", [1, D_MODEL], F32, "ExternalInput")
    if nxt is not None and fused:
        xo_d = p.dram("xo", [NT, D_MODEL], F32, "ExternalOutput")
        hs_d = p.dram("hsend", [128, 4, NT], BF16, "Internal")
        hsb_d = p.dram("hsendb", [128, 4, NT], BF16, "Internal")
    elif nxt is not None:
        chunks = even_chunks() if nxt == "even" else odd_chunks()
        NCW = len(chunks)
        win_d = p.dram("win", [NCW, 128, 8, 128], F32, "ExternalInput")
        outs = {k: p.dram(k, sh, dt, "ExternalOutput") for k, (sh, dt) in inproj_outputs(nxt).items()}
        xo_d = p.dram("xo", [NT, D_MODEL], F32, "ExternalOutput")
        if nxt == "even":
            cos_d = p.dram("cosT", [128, NT], F32, "ExternalInput")
            sin_d = p.dram("sinT", [128, NT], F32, "ExternalInput")
    else:
        out_d = p.dram("out", [NT, D_MODEL], F32, "ExternalOutput")

    ident = make_ident(p)
    xg = p.sbuf("xg", [128, TPG, D_MODEL], F32)
    hT = p.sbuf("hT", [128, 8, GT], BF16)
    hs_r = p.ring("hs", 2, [128, D_MODEL], BF16)
    junk = p.sbuf("junk", [128, D_MODEL], F32)
    ss_r = p.ring("ss", 2, [128, 1], F32)
    rs_r = p.ring("rs", 2, [128, 1], F32)
    gB1 = p.sbuf("gB1", [128, D_MODEL], F32)
    p.dma(gB1[:], g1_d[0:1, :].to_broadcast([128, D_MODEL]))
    ps = [p.psum(f"ps{i}", [128, 512], F32) for i in range(8)]
    st_r = p.ring("st", 3, [128, 2, 8, 128], F32)
    wb_r = p.ring("wb", 3, [128, 2, 8, 128], BF16)
    if has_prev:
        gB2 = p.sbuf("gB2", [128, D_MODEL], F32)
        p.dma(gB2[:], g2_d[0:1, :].to_broadcast([128, D_MODEL]))
        wo = p.sbuf("wo_sb", [128, 8, D_MODEL], BF16)
        for c in range(8):
            st = st_r.next()
            stv = st[:].rearrange("p a b c -> p (a b c)")[:, 0:D_MODEL]
            p.dma(stv, wo_d[c])
            p.copy("pool", wo[:, c, :], stv)
        oTg = p.sbuf("oTg", [128, 8, GT], BF16)
        aT = p.sbuf("aT", [128, NFC, GT], BF16)
        sg_r = p.ring("sg", 2, [128, GT], F32)
        wdb_r = p.ring("wdb", 3, [128, D_MODEL], BF16)
    if nxt == "even" and not fused:
        cosg = p.sbuf("cosg", [128, GT], F32)
        sing = p.sbuf("sing", [128, GT], F32)
        wrot_r = p.ring("wrot", 2, [128, 8, 128], BF16)
        t1_r = p.ring("t1", 2, [128, GT], F32)
        t2_r = p.ring("t2", 2, [128, GT], F32)
    if nxt is not None and fused:
        pass
    elif nxt is not None:
        fo_r = p.ring("fo", 3, [128, GT], BF16)
        tmo_r = p.ring("tmo", 2, [128, TPG, 128], BF16)
        tmf_r = p.ring("tmf", 2, [128, TPG, 128], F32)
    else:
        og_r = p.ring("og", 2, [128, D_MODEL], F32)
    psi = [0]

    def next_ps():
        t = ps[psi[0] % 8]
        psi[0] += 1
        return t

    def rmsnorm_to_hT(gB):
        for i in range(TPG):
            ss = ss_r.next()
            rs = rs_r.next()
            hs = hs_r.next()
            p.act(junk[:], xg[:, i, :], AF.Square, accum_out=ss[:])
            p.ts("dve", rs[:], ss[:], 1.0 / D_MODEL, NORM_EPS, ALU.mult, ALU.add)
            p.sqrt(rs[:], rs[:])
            p.recip(rs[:], rs[:])
            p.stt("dve", hs[:], xg[:, i, :], rs[:], gB[:], ALU.mult, ALU.mult)
            pt = next_ps()
            ptv = pt[:].bitcast(BF16).rearrange("p (c t) -> p c t", c=8)
            for c in range(8):
                p.tr(ptv[:, c, :], hs[:, c * 128:(c + 1) * 128], ident[:])
            p.copy("dve", hT[:, :, i * 128:(i + 1) * 128], ptv)

    for g in range(NG):
        t0 = g * GT
        p.dma(xg[:], x_d[t0:t0 + GT, :].rearrange("(i p) d -> p i d", p=128))
        if has_prev:
            if fused:
                p.dma(oTg[:, 0:4, :], oT_d[:, bass.ds(par * NT + t0, GT)].rearrange("(c f) t -> f c t", f=128))
                p.dma(oTg[:, 4:8, :], oTb_d[:, bass.ds(par * NT + t0, GT)].rearrange("(c f) t -> f c t", f=128))
            else:
                p.dma(oTg[:], oT_d[:, t0:t0 + GT].rearrange("(c f) t -> f c t", f=128))
            for i in range(TPG):
                for hf in range(2):
                    acc = next_ps()
                    for c in range(8):
                        p.mm(acc[:], oTg[:, c, i * 128:(i + 1) * 128], wo[:, c, hf * 512:(hf + 1) * 512],
                             c == 0, c == 7)
                    xs = xg[:, i, hf * 512:(hf + 1) * 512]
                    p.tt("dve", xs, xs, acc[:], ALU.add)
            rmsnorm_to_hT(gB2)
            for n in range(NFC):
                st = st_r.next()
                wb = wb_r.next()
                p.dma(st[:], wgu_d[n])
                p.copy("pool", wb[:], st[:])
                G = next_ps()
                U = next_ps()
                for c in range(8):
                    p.mm(G[:], wb[:, 0, c, :], hT[:, c, :], c == 0, c == 7)
                for c in range(8):
                    p.mm(U[:], wb[:, 1, c, :], hT[:, c, :], c == 0, c == 7)
                sg = sg_r.next()
                p.act(sg[:], G[:], AF.Silu)
                p.tt("dve", aT[:, n, :], sg[:], U[:], ALU.mult)
            accs = [next_ps() for _ in range(8)]
            for n in range(NFC):
                st = st_r.next()
                wdb = wdb_r.next()
                stv = st[:].rearrange("p a b c -> p (a b c)")[:, 0:D_MODEL]
                p.dma(stv, wd_d[n])
                p.copy("pool", wdb[:], stv)
                for i in range(TPG):
                    for hf in range(2):
                        p.mm(accs[i * 2 + hf][:], aT[:, n, i * 128:(i + 1) * 128],
                             wdb[:, hf * 512:(hf + 1) * 512], n == 0, n == NFC - 1)
            for i in range(TPG):
                for hf in range(2):
                    xs = xg[:, i, hf * 512:(hf + 1) * 512]
                    p.tt("dve", xs, xs, accs[i * 2 + hf][:], ALU.add)
        if nxt is None:
            for i in range(TPG):
                ss = ss_r.next()
                rs = rs_r.next()
                og = og_r.next()
                p.act(junk[:], xg[:, i, :], AF.Square, accum_out=ss[:])
                p.ts("dve", rs[:], ss[:], 1.0 / D_MODEL, NORM_EPS, ALU.mult, ALU.add)
                p.sqrt(rs[:], rs[:])
                p.recip(rs[:], rs[:])
                p.stt("dve", og[:], xg[:, i, :], rs[:], gB1[:], ALU.mult, ALU.mult)
                p.dma(out_d[t0 + i * 128:t0 + (i + 1) * 128, :], og[:], eng="pool")
            continue
        p.dma(xo_d[t0:t0 + GT, :].rearrange("(i p) d -> p i d", p=128), xg[:], eng="pool")
        rmsnorm_to_hT(gB1)
        if fused:
            p.dma(hs_d[:, :, t0:t0 + GT], hT[:, 0:4, :], eng="pool")
            p.dma(hsb_d[:, :, t0:t0 + GT], hT[:, 4:8, :], eng="pool")
            continue
        if nxt == "even":
            p.dma(cosg[:], cos_d[:, t0:t0 + GT])
            p.dma(sing[:], sin_d[:, t0:t0 + GT])
        for ci, (kind, scale, dname, dj) in enumerate(chunks):
            st = st_r.next()
            wb = wb_r.next()
            stv = st[:, 0]
            wbv = wb[:, 0]
            p.dma(stv, win_d[ci])
            if scale == 1.0:
                p.copy("pool", wbv, stv)
            else:
                p.ts("pool", wbv, stv, scale, None, ALU.mult)
            dst = outs[dname]
            if kind in ("fm", "rope"):
                A = next_ps()
                for c in range(8):
                    p.mm(A[:], wbv[:, c, :], hT[:, c, :], c == 0, c == 7)
                fo = fo_r.next()
                if kind == "rope":
                    wr = wrot_r.next()
                    sv = stv.rearrange("p c (h f d) -> p (c h) f d", h=2, f=2, d=32)
                    wv = wr[:].rearrange("p c (h f d) -> p (c h) f d", h=2, f=2, d=32)
                    p.ts("pool", wv[:, :, 0, :], sv[:, :, 1, :], -scale, None, ALU.mult)
                    p.ts("pool", wv[:, :, 1, :], sv[:, :, 0, :], scale, None, ALU.mult)
                    Bp = next_ps()
                    for c in range(8):
                        p.mm(Bp[:], wr[:, c, :], hT[:, c, :], c == 0, c == 7)
                    t1 = t1_r.next()
                    t2 = t2_r.next()
                    p.tt("dve", t1[:], A[:], cosg[:], ALU.mult)
                    p.tt("dve", t2[:], Bp[:], sing[:], ALU.mult)
                    p.tt("pool", fo[:], t1[:], t2[:], ALU.add)
                else:
                    p.copy("act", fo[:], A[:])
                dv_ = dst[2 * dj:2 * dj + 2, :, t0:t0 + GT].rearrange("h d t -> (h d) t")
                p.dma(dv_, fo[:], eng="pool")
            else:
                A = next_ps()
                Av = A[:].rearrange("p (i n) -> p i n", i=TPG)
                for i in range(TPG):
                    for c in range(8):
                        p.mm(Av[:, i, :], hT[:, c, i * 128:(i + 1) * 128], wbv[:, c, :], c == 0, c == 7)
                if kind == "tm":
                    to = tmo_r.next()
                else:
                    to = tmf_r.next()
                p.copy("act", to[:], Av)
                dv_ = dst[t0:t0 + GT, dj * 128:(dj + 1) * 128].rearrange("(i p) n -> p i n", p=128)
                p.dma(dv_, to[:], eng="pool")
    return p


def lay_wo(w):
    return np.ascontiguousarray(w.reshape(8, 128, D_MODEL))


def lay_wgu(wg, wu):
    a = np.stack([wg, wu], 0).reshape(2, 8, 128, NFC, 128)
    return np.ascontiguousarray(a.transpose(3, 2, 0, 1, 4))


def lay_wd(wd):
    return np.ascontiguousarray(wd.reshape(NFC, 128, D_MODEL))


def lay_win_even(w):
    cols = [w[:, 0:512], w[:, 512:640], w[:, 640:768], w[:, 768:896], w[:, 896:1024], w[:, 1024:1152],
            w[:, 1152:1280], np.pad(w[:, 1280:1304], ((0, 0), (0, 104))), w[:, 1304:1816], w[:, 1816:2328],
            w[:, 2328:2840]]
    wp = np.concatenate(cols, 1)
    n = wp.shape[1] // 128
    return np.ascontiguousarray(wp.reshape(8, 128, n, 128).transpose(2, 1, 0, 3))


def lay_win_odd(w):
    return np.ascontiguousarray(w.reshape(8, 128, 24, 128).transpose(2, 1, 0, 3))


def rope_tables_T(pos):
    inv = 1.0 / (10000.0 ** (np.arange(0, HD, 2, dtype=np.float32) / HD))
    ang = pos.astype(np.float32)[None, :] * inv.astype(np.float32)[:, None]
    c = np.cos(ang).astype(np.float32)
    s = np.sin(ang).astype(np.float32)
    return np.ascontiguousarray(np.tile(c, (4, 1))), np.ascontiguousarray(np.tile(s, (4, 1)))


def build_att_odd(S=SEQ, n_hg=2, p=None, mid_hook=None):
    if p is None:
        p = Prog()
        p.enable_arena(200 * 1024)
    NH = n_hg * 4
    NI = S // 512
    NKB = S // 128
    qT_d = p.dram("qT", [NH, 64, S], BF16, "ExternalInput")
    kT_d = p.dram("kT", [NH, 64, S], BF16, "ExternalInput")
    v_d = p.dram("v", [128, NKB, NH * 64], BF16, "ExternalInput")
    oT_d = p.dram("oT", [NH * 64, S], BF16, "ExternalOutput")
    ident = make_ident(p)
    U = p.sbuf("Uincl", [128, 128], BF16)
    L = p.sbuf("Lstr", [128, 128], BF16)
    p.memset("pool", U[:], 1.0)
    p.asel(U[:], U[:], [[-1, 128]], ALU.is_ge, 0.0, 0, 1)
    p.memset("pool", L[:], 1.0)
    p.asel(L[:], L[:], [[1, 128]], ALU.is_ge, 0.0, -1, -1)
    qT = p.sbuf("qT_sb", [64, 4, S], BF16)
    kT = p.sbuf("kT_sb", [64, 4, S], BF16)
    v_sb = p.sbuf("v_sb", [128, NKB, NH * 64], BF16)
    for q4 in range(4):
        a, b = q4 * NKB // 4, (q4 + 1) * NKB // 4
        p.dma(v_sb[:, a:b, :], v_d[:, a:b, :])
    Z_r = p.ring("Z", 4, [128, 512], F32, "psum")
    A_r = p.ring("A", 2, [128, 512], F32, "psum")
    O_r = p.ring("O", 2, [128, 512], F32, "psum")
    E_r = p.ring("E", 3, [128, 1024], F32)
    SP_r = p.ring("SP", 3, [128, 1024], BF16)
    G_r = p.ring("G", 3, [128, 1024], F32)
    P_r = p.ring("P", 3, [128, 1024], BF16)
    oT_r = p.ring("oTs", 4, [64, 512], BF16)
    units = []
    for hg in range(n_hg):
        for I in range(NI):
            kb_hi = 4 * I + 3
            for hp in range(2):
                st = {}
                for kb in range(kb_hi, -1, -1):
                    units.append({"st": st, "hg": hg, "I": I, "hp": hp, "kb": kb, "kb_hi": kb_hi,
                                  "heads": (2 * hp, 2 * hp + 1),
                                  "load": (I == 0 and hp == 0 and kb == kb_hi)})

    def c0_of(u):
        return max(0, 128 * (u["kb"] - 4 * u["I"]))

    def cv(v, c0):
        return v.rearrange("p (h q) -> p h q", h=2)[:, :, c0:512]

    def fA(u):
        if u["load"]:
            for j in range(4):
                p.dma(qT[:, j, :], qT_d[u["hg"] * 4 + j])
                p.dma(kT[:, j, :], kT_d[u["hg"] * 4 + j])
        kb, I = u["kb"], u["I"]
        c0 = c0_of(u)
        u["Z"] = {}
        for h in u["heads"]:
            u["Z"][h] = Z_r.next()
            p.mm(u["Z"][h][:, c0:512], kT[:, h, kb * 128:(kb + 1) * 128], qT[:, h, I * 512 + c0:(I + 1) * 512],
                 True, True)

    def fE(u):
        c0 = c0_of(u)
        u["E2"] = E_r.next()
        Zp, Zo = p.pair(u["Z"][u["heads"][0]])
        p.act(cv(u["E2"][:], c0), cv(Zp, c0), AF.Exp, extra_r=[Zo])

    def fSPU(u):
        kb, I, kb_hi, heads = u["kb"], u["I"], u["kb_hi"], u["heads"]
        c0 = c0_of(u)
        if kb == kb_hi:
            u["st"]["A"] = {h: A_r.next() for h in heads}
        Ah = u["st"]["A"]
        u["SP2"] = SP_r.next()
        SP2 = u["SP2"]
        p.act(cv(SP2[:], c0), cv(u["E2"][:], c0), AF.Ln, bias=1.0)
        if kb >= 4 * I:
            p.asel(cv(SP2[:], c0), cv(SP2[:], c0), [[0, 2], [1, 512 - c0]], ALU.is_ge, 0.0, -1, -1)
        for j, h in enumerate(heads):
            p.mm(Ah[h][:, c0:512], U[:], SP2[:, j * 512 + c0:(j + 1) * 512], kb == kb_hi, False)

    def fRest(u):
        kb, I, kb_hi, heads, hg, hp = u["kb"], u["I"], u["kb_hi"], u["heads"], u["hg"], u["hp"]
        c0 = c0_of(u)
        Ah = u["st"]["A"]
        SP2, E2 = u["SP2"], u["E2"]
        G2, P2 = G_r.next(), P_r.next()
        Ap, Ao = p.pair(Ah[heads[0]])
        p.act(cv(G2[:], c0), cv(Ap, c0), AF.Exp, scale=-1.0, extra_r=[Ao])
        for j, h in enumerate(heads):
            p.mm(Ah[h][:, c0:512], L[:], SP2[:, j * 512 + c0:(j + 1) * 512], False, kb == 0)
        p.tt("dve", cv(P2[:], c0), cv(E2[:], c0), cv(G2[:], c0), ALU.mult)
        if kb >= 4 * I:
            p.asel(cv(P2[:], c0), cv(P2[:], c0), [[0, 2], [1, 512 - c0]], ALU.is_ge, 0.0, -1, -1)
        u["P2"] = P2

    def fPV(u):
        kb, I, kb_hi, heads, hg, hp = u["kb"], u["I"], u["kb_hi"], u["heads"], u["hg"], u["hp"]
        c0 = c0_of(u)
        P2 = u["P2"]
        if kb == kb_hi:
            u["st"]["O"] = {h: O_r.next() for h in heads}
        Oh = u["st"]["O"]
        for j, h in enumerate(heads):
            p.mm(Oh[h][0:64, c0:512], v_sb[:, kb, (hg * 4 + h) * 64:(hg * 4 + h + 1) * 64],
                 P2[:, j * 512 + c0:(j + 1) * 512], kb == kb_hi, kb == 0)
        if kb == 0:
            for h in heads:
                ot = oT_r.next()
                p.copy("dve", ot[:], Oh[h][0:64, :])
                r0 = (hg * 4 + h) * 64
                p.dma(oT_d[r0:r0 + 64, I * 512:(I + 1) * 512], ot[:], eng="pool")
            if mid_hook is not None and hg == 0 and hp == 1 and I == NI - 1 and n_hg > 1:
                mid_hook()

    n = len(units)
    fA(units[0])
    if n > 1:
        fA(units[1])
    fE(units[0])
    for k in range(n):
        fSPU(units[k])
        if k + 2 < n:
            fA(units[k + 2])
        if k + 1 < n:
            fE(units[k + 1])
        fRest(units[k])
        if k >= 1:
            fPV(units[k - 1])
    fPV(units[n - 1])
    return p


NEGM = 30000.0


def build_att_even(S=SEQ, do_nsa=True, do_diff=True, p=None, mid_hook=None):
    if p is None:
        p = Prog()
        p.enable_arena(200 * 1024)
    NQ = S // 128
    NI = S // 512
    NSL = S // 64
    NCMP = (S - 32) // 16 + 1
    NCC = -(-NCMP // 128)
    nqT_d = p.dram("nqT", [4, 64, S], BF16, "ExternalInput")
    kcT_d = p.dram("kcT", [64, S], BF16, "ExternalInput")
    vcT_d = p.dram("vcT", [64, S], BF16, "ExternalInput")
    ksT_d = p.dram("ksT", [64, S], BF16, "ExternalInput")
    kwT_d = p.dram("kwT", [64, S], BF16, "ExternalInput")
    vs_d = p.dram("vs", [128, NQ, 64], BF16, "ExternalInput")
    vw_d = p.dram("vw", [128, NQ, 64], BF16, "ExternalInput")
    gl_d = p.dram("gl", [128, NQ, 12], F32, "ExternalInput")
    wck_d = p.dram("wck", [64, 32, 64], F32, "ExternalInput")
    wcv_d = p.dram("wcv", [64, 32, 64], F32, "ExternalInput")
    pk_d = p.dram("pkT", [64, 32], F32, "ExternalInput")
    pv_d = p.dram("pvT", [64, 32], F32, "ExternalInput")
    ftab_d = p.dram("ftab", [128, NQ, NSL], F32, "ExternalInput")
    dqT_d = p.dram("dqT", [4, 64, S], BF16, "ExternalInput")
    dkT_d = p.dram("dkT", [4, 64, S], BF16, "ExternalInput")
    dv_d = p.dram("dv", [128, NQ, 256], BF16, "ExternalInput")
    lqk_d = p.dram("lqk", [1, 256], F32, "ExternalInput")
    li_d = p.dram("lami", [1, 2], F32, "ExternalInput")
    sub_d = p.dram("subln", [1, 128], F32, "ExternalInput")
    oT_d = p.dram("oT", [512, S], BF16, "ExternalOutput")

    ident = make_ident(p)
    ps = [p.psum(f"ps{i}", [128, 512], F32) for i in range(8)]
    bigA = p.sbuf("bigA", [64, 4, S], BF16)
    bigB = p.sbuf("bigB", [64, 4, S], BF16)
    E_r = p.ring("E", 4, [128, 512], BF16)
    oTs_r = p.ring("oTs", 2, [128, 2, 512], BF16)
    vstg = p.sbuf("vstg", [128, NQ, 256], BF16)

    if do_nsa:
        Zr = Ring(ps[0:4])
        OC, OS, OW, TP = ps[4], ps[5], ps[6], ps[7]
        OCv = OC[:].rearrange("p (h c) -> p h c", h=4)
        OSv = OS[:, 0:260].rearrange("p (h c) -> p h c", h=4)
        OWv = OW[:, 0:260].rearrange("p (h c) -> p h c", h=4)
        for j in range(4):
            p.dma(bigA[:, j, :], nqT_d[j])
        p.dma(bigB[:, 0, :], kcT_d[:, :])
        p.dma(bigB[:, 1, :], vcT_d[:, :])
        p.dma(bigB[:, 2, :], ksT_d[:, :])
        p.dma(bigB[:, 3, :], kwT_d[:, :])
        kcT, vcT, ksT, kwT = (bigB[:, j, :] for j in range(4))
        vs1 = p.sbuf("vs1", [128, NQ, 65], BF16)
        vw1 = p.sbuf("vw1", [128, NQ, 65], BF16)
        p.memset("pool", vs1[:, :, 64:65], 1.0)
        p.memset("pool", vw1[:, :, 64:65], 1.0)
        p.dma(vstg[:, :, 0:64], vs_d[:, :, :])
        p.dma(vstg[:, :, 64:128], vw_d[:, :, :])
        p.copy("act", vs1[:, :, 0:64], vstg[:, :, 0:64])
        p.copy("act", vw1[:, :, 0:64], vstg[:, :, 64:128])
        gates = p.sbuf("gates", [128, NQ, 12], F32)
        p.dma(gates[:], gl_d[:, :, :])
        p.act(gates[:], gates[:], AF.Exp, scale=-1.0)
        p.ts("dve", gates[:], gates[:], 1.0, None, ALU.add)
        p.recip(gates[:], gates[:])
        ftab = p.sbuf("ftab_sb", [128, NQ, NSL], F32)
        p.dma(ftab[:], ftab_d[:, :, :])
        EE = p.sbuf("EE", [64, S], BF16)
        p.memset("pool", EE[:], 1.0)
        p.asel(EE[:], EE[:], [[1, S]], ALU.is_ge, 0.0, 0, -64)
        p.asel(EE[:], EE[:], [[-1, S]], ALU.is_ge, 0.0, 63, 64)
        wst = p.sbuf("wst", [64, 32, 64], F32)
        wck = p.sbuf("wck_sb", [64, 32, 64], BF16)
        wcv = p.sbuf("wcv_sb", [64, 32, 64], BF16)
        pst = p.sbuf("pst", [64, 32], F32)
        pkT = p.sbuf("pkT_sb", [64, 32], BF16)
        pvTb = p.sbuf("pvTb", [64, 32, 128], BF16)
        p.dma(wst[:], wck_d[:, :, :])
        p.copy("dve", wck[:], wst[:])
        p.dma(wst[:], wcv_d[:, :, :])
        p.copy("dve", wcv[:], wst[:])
        p.dma(pst[:], pk_d[:, :])
        p.copy("dve", pkT[:], pst[:])
        p.dma(pst[:], pv_d[:, :])
        p.copy("dve", pvTb[:], pst[:].unsqueeze(2).to_broadcast([64, 32, 128]))
        kcmpT = p.sbuf("kcmpT", [64, NCC * 128], BF16)
        VC = p.sbuf("VC", [128, NCC, 128], BF16)
        ck = p.sbuf("ck", [64, 1], F32)
        p.memset("pool", kcmpT[:], 0.0)
        zk = ps[0]
        zc = ps[1]
        for j in range(32):
            p.mm(zk[0:64, 0:NCMP], wck[:, j, :], kcT[:, j:j + 16 * (NCMP - 1) + 1:16], j == 0, j == 31)
        for j in range(32):
            p.mm(zc[0:64, 0:1], wck[:, j, :], pkT[:, j:j + 1], j == 0, j == 31)
        p.copy("dve", ck[:], zc[0:64, 0:1])
        p.ts("dve", kcmpT[:, 0:NCMP], zk[0:64, 0:NCMP], ck[:], None, ALU.add)
        cmask = p.sbuf("cmask", [128, NCC, 64], F32)
        p.memset("pool", VC[:], 0.0)
        p.memset("pool", cmask[:], 1.0)
        for c in range(NCC):
            m = min(128, NCMP - 128 * c)
            zv = ps[2 + c % 2]
            for j in range(32):
                n0 = 16 * 128 * c + j
                p.mm(zv[0:m, 0:64], vcT[:, n0:n0 + 16 * (m - 1) + 1:16], wcv[:, j, :], j == 0, False)
            for j in range(32):
                p.mm(zv[0:m, 0:64], pvTb[:, j, 0:m], wcv[:, j, :], False, j == 31)
            p.copy("dve", VC[0:m, c, 0:64], zv[0:m, 0:64])
            p.asel(cmask[:, c, :], cmask[:, c, :], [[-4, 64]], ALU.is_ge, 0.0, 128 * c + 1, 1)
            p.asel(cmask[:, c, :], cmask[:, c, :], [[4, 64]], ALU.is_ge, 0.0, 3 - 128 * c, -1)
            p.copy("pool", VC[0:m, c, 64:128], cmask[0:m, c, :])
            p.memset("pool", VC[0:m, c, 64:65], 1.0)
        onsa_r = p.ring("onsa", 2, [128, 4, 64], F32)
        onb_r = p.ring("onb", 2, [128, 256], BF16)
        rz_r = p.ring("rz", 3, [128, 4], F32)
        gco_r = p.ring("gco", 3, [128, 4], F32)
        tmp_r = p.ring("tmpn", 2, [128, 4, 64], F32)
        imp_r = p.ring("imp", 2, [128, 64], F32)
        wk_r = p.ring("impw", 2, [128, 64], F32)
        m8_r = p.ring("m8", 2, [128, 16], F32)
        thr_r = p.ring("thr", 2, [128, 1], F32)
        neg_r = p.ring("neg", 2, [128, 64], BF16)
        negT_r = p.ring("negT", 2, [64, 128], BF16)
        units = []
        tile_state = {}
        E2n_r = p.ring("E2n", 3, [128, 1024], BF16)

        def pre_sel_fn(stt_):
            negT = negT_r.next()
            stt_["negT"] = negT
            TPb = TP[:].bitcast(BF16)
            p.tr(TPb[0:64, 0:128], stt_["neg"][:], ident[:])
            p.copy("dve", negT[:], TPb[0:64, 0:128])

        for i in range(NQ):
            qv = bigA[:, :, i * 128:(i + 1) * 128]
            stt_ = {"onsa": None}
            tile_state[i] = stt_
            if i % 4 == 0:
                oTs_cur = oTs_r.next()
            stt_["oTs"] = oTs_cur

            def mk_unit(kind, kb, i=i, qv=qv, stt_=stt_, first=False, last=False, c=None):
                u = {}

                def A(u=u):
                    Z = Zr.next()
                    u["Z"] = Z
                    if kind == "cmp":
                        p.mm(Z[:], kcmpT[:, c * 128:(c + 1) * 128], qv, True, True)
                    elif kind == "win":
                        p.mm(Z[:], kwT[:, kb * 128:(kb + 1) * 128], qv, True, True)
                    else:
                        if first:
                            pre_sel_fn(stt_)
                        p.mm(Z[:], ksT[:, kb * 128:(kb + 1) * 128], qv, True, False)
                        p.mm(Z[:], EE[:, kb * 128:(kb + 1) * 128],
                             stt_["negT"][:].unsqueeze(1).to_broadcast([64, 4, 128]), False, True)

                def B(u=u):
                    E = E_r.next()
                    u["E"] = E
                    p.act(E[:], u["Z"][:], AF.Exp)
                    if kind == "cmp":
                        if not (128 * c + 127 <= 8 * i - 2):
                            p.asel(E[:], E[:], [[0, 4], [1, 128]], ALU.is_ge, 0.0, 128 * i - 31 - 2048 * c, -16)
                    else:
                        if kb == i:
                            p.asel(E[:], E[:], [[0, 4], [1, 128]], ALU.is_ge, 0.0, 0, -1)
                        if kind == "win" and kb == i - 4:
                            p.asel(E[:], E[:], [[0, 4], [-1, 128]], ALU.is_ge, 0.0, -1, 1)

                def C(u=u):
                    E = u["E"]
                    for h in range(4):
                        if kind == "cmp":
                            p.mm(OCv[:, h, :], E[:, h * 128:(h + 1) * 128], VC[:, c, :], first and h == 0,
                                 last and h == 3)
                        elif kind == "win":
                            p.mm(OWv[:, h, :], E[:, h * 128:(h + 1) * 128], vw1[:, kb, :], first and h == 0,
                                 last and h == 3)
                        else:
                            p.mm(OSv[:, h, :], E[:, h * 128:(h + 1) * 128], vs1[:, kb, :], first and h == 0,
                                 last and h == 3)
                u["A"], u["B"], u["C"] = A, B, C
                return u

            ccs = [c for c in range(NCC) if 2048 * c + 31 <= 128 * i + 127]
            cu = [mk_unit("cmp", None, first=(c == ccs[0]), last=(c == ccs[-1]), c=c) for c in ccs]

            def post_cmp(i=i, stt_=stt_):
                prev = tile_state.get(i - 1)
                if prev is not None and "flush" in prev:
                    prev.pop("flush")()
                onsa = onsa_r.next()
                stt_["onsa"] = onsa
                rz = rz_r.next()
                gco = gco_r.next()
                p.ts("dve", rz[:], OCv[:, :, 64], 1e-30, None, ALU.max)
                p.recip(rz[:], rz[:])
                p.tt("dve", gco[:], rz[:], gates[:, i, 0:12:3], ALU.mult)
                p.tt("dve", onsa[:], OCv[:, :, 0:64], gco[:].unsqueeze(2).to_broadcast([128, 4, 64]), ALU.mult)
                tmp = tmp_r.next()
                imp = imp_r.next()
                wk = wk_r.next()
                m8 = m8_r.next()
                thr = thr_r.next()
                neg = neg_r.next()
                stt_["neg"] = neg
                p.tt("dve", tmp[:], OCv[:, :, 64:128], rz[:].unsqueeze(2).to_broadcast([128, 4, 64]), ALU.mult)
                p.reduce("dve", imp[:, 0:64], tmp[:].rearrange("p h j -> p j h"), ALU.add)
                p.tt("dve", imp[:, 0:NSL], imp[:, 0:NSL], ftab[:, i, :], ALU.add)
                p.max8(m8[:, 0:8], imp[:, 0:NSL])
                p.match_replace(wk[:, 0:NSL], m8[:, 0:8], imp[:, 0:NSL], -3.0e38)
                p.max8(m8[:, 8:16], wk[:, 0:NSL])
                p.reduce("dve", thr[:], m8[:, 8:16], ALU.min)
                p.memset("pool", neg[:], 0.0)
                p.ts("dve", wk[:, 0:NSL], imp[:, 0:NSL], thr[:], None, ALU.is_ge)
                p.ts("dve", neg[:, 0:NSL], wk[:, 0:NSL], NEGM, -NEGM, ALU.mult, ALU.add)
            cu[-1]["post"] = post_cmp
            kbs = list(range(max(0, i - 4), i + 1))
            wu = [mk_unit("win", kb, first=(kb == kbs[0]), last=(kb == i)) for kb in kbs]
            def mk_sel_pair(kb0, i=i, qv=qv, stt_=stt_):
                u = {}
                first, last = (kb0 == 0), (kb0 + 1 == i)

                def A(u=u):
                    if first:
                        pre_sel_fn(stt_)
                    if Zr.i % 2:
                        Zr.next()
                    u["Z"] = [Zr.next(), Zr.next()]
                    for b_ in range(2):
                        kb = kb0 + b_
                        p.mm(u["Z"][b_][:], ksT[:, kb * 128:(kb + 1) * 128], qv, True, False)
                        p.mm(u["Z"][b_][:], EE[:, kb * 128:(kb + 1) * 128],
                             stt_["negT"][:].unsqueeze(1).to_broadcast([64, 4, 128]), False, True)

                def B(u=u):
                    E2 = E2n_r.next()
                    u["E"] = E2
                    Zp, Zo = p.pair(u["Z"][0])
                    p.act(E2[:], Zp, AF.Exp, extra_r=[Zo])
                    if last:
                        p.asel(E2[:, 512:1024], E2[:, 512:1024], [[0, 4], [1, 128]], ALU.is_ge, 0.0, 0, -1)

                def C(u=u):
                    for b_ in range(2):
                        kb = kb0 + b_
                        for h in range(4):
                            p.mm(OSv[:, h, :], u["E"][:, b_ * 512 + h * 128:b_ * 512 + (h + 1) * 128],
                                 vs1[:, kb, :], first and b_ == 0 and h == 0, last and b_ == 1 and h == 3)
                u["A"], u["B"], u["C"] = A, B, C
                return u

            su = [mk_sel_pair(kb0) for kb0 in range(0, i, 2)]
            if (i + 1) % 2 == 1:
                su.append(mk_unit("sel", i, first=(i == 0), last=True))


            def post_sel(i=i, stt_=stt_):
                onsa = stt_["onsa"]
                for (Ov, gi) in ((OWv, 2), (OSv, 1)):
                    rz2 = rz_r.next()
                    gc2 = gco_r.next()
                    tmp2 = tmp_r.next()
                    p.recip(rz2[:], Ov[:, :, 64])
                    p.tt("dve", gc2[:], rz2[:], gates[:, i, gi:12:3], ALU.mult)
                    p.tt("dve", tmp2[:], Ov[:, :, 0:64], gc2[:].unsqueeze(2).to_broadcast([128, 4, 64]), ALU.mult)
                    p.tt("pool", onsa[:], onsa[:], tmp2[:], ALU.add)
                onb = onb_r.next()
                p.copy("act", onb[:], onsa[:].rearrange("p h d -> p (h d)"))

                def flush(i=i, onb=onb, oTs=stt_["oTs"]):
                    TPb = TP[:].bitcast(BF16)
                    for c2 in range(2):
                        p.tr(TPb[:, 256 + c2 * 128:256 + (c2 + 1) * 128], onb[:, c2 * 128:(c2 + 1) * 128], ident[:])
                    p.copy("dve", oTs[:, :, (i % 4) * 128:(i % 4 + 1) * 128],
                           TPb[:, 256:512].rearrange("p (c q) -> p c q", c=2))
                    if i % 4 == 3:
                        i0 = (i // 4) * 512
                        p.dma(oT_d[0:256, i0:i0 + 512].rearrange("(c f) q -> f c q", c=2), oTs[:], eng="pool")
                stt_["flush"] = flush
            su[-1]["post"] = post_sel
            units += cu + wu + su
        run_pipeline(units, 1)
        tile_state[NQ - 1].pop("flush")()

    if mid_hook is not None:
        mid_hook()
    if do_diff:
        Zr = Ring(ps[0:4])
        TP = ps[3]
        E2_r = p.ring("E2d", 3, [128, 1024], BF16)
        TPb = TP[:].bitcast(BF16)
        OD = [[ps[4], ps[5]], [ps[6], ps[7]]]

        def odv(m, sub):
            return OD[m][sub // 2][:, 0:258].rearrange("p (s c) -> p s c", s=2)[:, sub % 2, :]

        for j in range(4):
            p.dma(bigA[:, j, :], dqT_d[j])
            p.dma(bigB[:, j, :], dkT_d[j])
        dv1 = p.sbuf("dv1", [128, NQ, 2, 129], BF16)
        p.memset("pool", dv1[:, :, :, 128:129], 1.0)
        for q4 in range(4):
            a, b = q4 * NQ // 4, (q4 + 1) * NQ // 4
            p.dma(vstg[:, a:b, :], dv_d[:, a:b, :])
        p.copy("dve", dv1[:, :, :, 0:128], vstg[:].rearrange("p k (h d) -> p k h d", h=2))
        lqk = p.sbuf("lqk_sb", [128, 4, 64], F32)
        li = p.sbuf("li_sb", [128, 2], F32)
        sB = p.sbuf("sublnB", [128, 128], F32)
        p.dma(lqk[:].rearrange("p a d -> p (a d)"), lqk_d[0:1, :].to_broadcast([128, 256]))
        p.dma(li[:], li_d[0:1, :].to_broadcast([128, 2]))
        p.dma(sB[:], sub_d[0:1, :].to_broadcast([128, 128]))
        lpr = p.sbuf("lpr", [128, 2, 64], F32)
        lsum = p.sbuf("lsum", [128, 2], F32)
        nlam = p.sbuf("nlam", [128, 1], F32)
        p.tt("dve", lpr[:], lqk[:, 0:4:2, :], lqk[:, 1:4:2, :], ALU.mult)
        p.reduce("dve", lsum[:], lpr[:], ALU.add)
        p.act(lsum[:], lsum[:], AF.Exp)
        p.tt("dve", nlam[:], lsum[:, 1:2], lsum[:, 0:1], ALU.subtract)
        p.tt("dve", nlam[:], nlam[:], li[:, 0:1], ALU.subtract)
        p.ts("dve", sB[:], sB[:], li[:, 1:2], None, ALU.mult)
        od_r = p.ring("od", 2, [128, 128], F32)
        od2_r = p.ring("od2", 2, [128, 128], F32)
        odb_r = p.ring("odb", 2, [128, 128], BF16)
        rzd_r = p.ring("rzd", 4, [128, 1], F32)
        ssd_r = p.ring("ssd", 2, [128, 1], F32)
        jk = p.sbuf("jkd", [128, 128], F32)
        oTd_r = p.ring("oTd", 2, [128, 512], BF16)
        units = []
        for hd in range(2):
            for I in range(NI):
                nkb = 4 * I + 4
                for kb in range(nkb):
                    if True:
                        m = 1
                        u = {}
                        c0 = max(0, 128 * (kb - 4 * I))

                        def cvd(v, c0=c0):
                            return v.rearrange("p (h q) -> p h q", h=2)[:, :, c0:512]

                        def A(u=u, hd=hd, I=I, kb=kb, c0=c0):
                            u["Z"] = [Zr.next(), Zr.next()]
                            for m_ in range(2):
                                p.mm(u["Z"][m_][:, c0:512], bigB[:, hd * 2 + m_, kb * 128:(kb + 1) * 128],
                                     bigA[:, hd * 2 + m_, I * 512 + c0:(I + 1) * 512], True, True)

                        def B(u=u, I=I, kb=kb, c0=c0, cvd=cvd):
                            E2 = E2_r.next()
                            u["E"] = E2
                            Zp, Zo = p.pair(u["Z"][0])
                            p.act(cvd(E2[:]), cvd(Zp), AF.Exp, extra_r=[Zo])
                            if kb >= 4 * I:
                                p.asel(cvd(E2[:]), cvd(E2[:]), [[0, 2], [1, 512 - c0]], ALU.is_ge, 0.0, 0, -1)

                        def C(u=u, hd=hd, kb=kb, nkb=nkb, c0=c0):
                            for m_ in range(2):
                                for sub in range(c0 // 128, 4):
                                    p.mm(odv(m_, sub), u["E"][:, m_ * 512 + sub * 128:m_ * 512 + (sub + 1) * 128],
                                         dv1[:, kb, hd, :], kb == 0 and sub % 2 == 0,
                                         (sub == 1 and kb == nkb - 3) or (sub == 3 and kb == nkb - 1))
                        u["A"], u["B"], u["C"] = A, B, C

                        def post(hd=hd, I=I):
                            oTd = oTd_r.next()
                            for sub in range(4):
                                r1 = rzd_r.next()
                                r2 = rzd_r.next()
                                od = od_r.next()
                                od2 = od2_r.next()
                                odb = odb_r.next()
                                ss = ssd_r.next()
                                p.recip(r1[:], odv(0, sub)[:, 128:129])
                                p.recip(r2[:], odv(1, sub)[:, 128:129])
                                p.tt("dve", r2[:], r2[:], nlam[:], ALU.mult)
                                p.ts("dve", od[:], odv(0, sub)[:, 0:128], r1[:], None, ALU.mult)
                                p.stt("dve", od2[:], odv(1, sub)[:, 0:128], r2[:], od[:], ALU.mult, ALU.add)
                                p.act(jk[:], od2[:], AF.Square, accum_out=ss[:])
                                p.ts("dve", ss[:], ss[:], 1.0 / 128, NORM_EPS, ALU.mult, ALU.add)
                                p.act(ss[:], ss[:], AF.Ln)
                                p.act(ss[:], ss[:], AF.Exp, scale=-0.5)
                                p.stt("dve", odb[:], od2[:], ss[:], sB[:], ALU.mult, ALU.mult)
                                p.tr(TPb[:, sub * 128:(sub + 1) * 128], odb[:], ident[:])
                            p.copy("dve", oTd[:], TPb[:, 0:512])
                            p.dma(oT_d[256 + hd * 128:256 + (hd + 1) * 128, I * 512:(I + 1) * 512], oTd[:],
                                  eng="pool")
                        if kb == nkb - 1 and m == 1:
                            u["post"] = post
                        units.append(u)
        run_pipeline(units, 1)
    return p


_PROGS = {}


def _prog(key, fn):
    if key not in _PROGS:
        _PROGS[key] = fn().build()
    return _PROGS[key]


def make_ftab(S):
    NQ = S // 128
    NSL = S // 64
    t = np.arange(S)
    qblk = t // 64
    jb = np.arange(NSL)
    F = np.zeros((S, NSL), np.float32)
    F[jb[None, :] > qblk[:, None]] = -1e30
    F[np.arange(S), qblk] = 3e30
    m = qblk >= 1
    F[np.arange(S)[m], qblk[m] - 1] = 2e30
    F[:, 0] = 1e30
    return np.ascontiguousarray(F.reshape(NQ, 128, NSL).transpose(1, 0, 2))


def _pm(a):
    S = a.shape[0]
    return np.ascontiguousarray(a.reshape(S // 128, 128, -1).transpose(1, 0, 2))


def _run(nc, in_maps):
    res = run_bass_kernel_spmd(nc, in_maps, core_ids=list(range(8)))
    return res.results


def kernel_unfused(x, norm_mix, norm_ffn, norm_final, even_w_in, even_w_out,
           cmp_pos_k, cmp_w_k, cmp_pos_v, cmp_w_v,
           diff_lq1, diff_lk1, diff_lq2, diff_lk2, diff_subln,
           odd_w_in, odd_w_out, ffn_w_gate, ffn_w_up, ffn_w_down):
    f32 = lambda a: np.ascontiguousarray(np.asarray(a, dtype=np.float32))
    x = f32(x)
    xs = [np.ascontiguousarray(x[c // 2, (c % 2) * NT:(c % 2 + 1) * NT]) for c in range(8)]
    ropes = [rope_tables_T(np.arange((c % 2) * NT, (c % 2 + 1) * NT)) for c in range(8)]
    ftab = make_ftab(SEQ)
    oT_full = None
    out = None
    for layer in range(DEPTH + 1):
        has_prev = layer > 0
        nxt = None if layer == DEPTH else ("even" if layer % 2 == 0 else "odd")
        nc = _prog(("tok", has_prev, nxt), lambda: build_tok(has_prev, nxt))
        common = {}
        if has_prev:
            pl = layer - 1
            w_out = f32(even_w_out[pl // 2]) if pl % 2 == 0 else f32(odd_w_out[pl // 2])
            common["wo"] = lay_wo(w_out)
            common["g2"] = f32(norm_ffn[pl])[None, :]
            common["wgu"] = lay_wgu(f32(ffn_w_gate[pl]), f32(ffn_w_up[pl]))
            common["wd"] = lay_wd(f32(ffn_w_down[pl]))
        if nxt is None:
            common["g1"] = f32(norm_final)[None, :]
        else:
            common["g1"] = f32(norm_mix[layer])[None, :]
            common["win"] = lay_win_even(f32(even_w_in[layer // 2])) if nxt == "even" else lay_win_odd(f32(odd_w_in[layer // 2]))
        in_maps = []
        for c in range(8):
            m = dict(common)
            m["x"] = xs[c]
            if has_prev:
                b, hf = c // 2, c % 2
                m["oT"] = np.ascontiguousarray(oT_full[b][:, hf * NT:(hf + 1) * NT])
            if nxt == "even":
                m["cosT"], m["sinT"] = ropes[c]
            in_maps.append(m)
        r = _run(nc, in_maps)
        if nxt is None:
            out = np.stack([np.concatenate([r[2 * b]["out"], r[2 * b + 1]["out"]], 0) for b in range(BATCH)], 0)
            break
        xs = [np.ascontiguousarray(r[c]["xo"]) for c in range(8)]

        def cat(name, b, axis):
            return np.concatenate([r[2 * b][name], r[2 * b + 1][name]], axis)

        in_maps = []
        if nxt == "even":
            e = layer // 2
            lam_init = 0.8 - 0.6 * math.exp(-0.3 * layer)
            cw = {
                "wck": np.ascontiguousarray(f32(cmp_w_k[e]).reshape(32, 64, 64).transpose(1, 0, 2)),
                "wcv": np.ascontiguousarray(f32(cmp_w_v[e]).reshape(32, 64, 64).transpose(1, 0, 2)),
                "pkT": np.ascontiguousarray(f32(cmp_pos_k[e]).T),
                "pvT": np.ascontiguousarray(f32(cmp_pos_v[e]).T),
                "ftab": ftab,
                "lqk": np.ascontiguousarray(np.stack([f32(diff_lq1[e]), f32(diff_lk1[e]), f32(diff_lq2[e]),
                                                      f32(diff_lk2[e])], 0).reshape(1, 256)),
                "lami": np.array([[lam_init, 1.0 - lam_init]], np.float32),
                "subln": f32(diff_subln[e])[None, :],
            }
            nca = _prog(("att_even",), lambda: build_att_even(SEQ))
            for b in range(BATCH):
                nqT, kcT, vcT = cat("nqT", b, 2), cat("kcT", b, 2), cat("vcT", b, 2)
                ksT, kwT = cat("ksT", b, 2), cat("kwT", b, 2)
                vs, vw, gl = cat("vs", b, 0), cat("vw", b, 0), cat("gl", b, 0)
                dqT, dkT, dv = cat("dqT", b, 2), cat("dkT", b, 2), cat("dv", b, 0)
                for g in range(2):
                    m = dict(cw)
                    m["nqT"] = np.ascontiguousarray(nqT[4 * g:4 * g + 4])
                    m["kcT"] = np.ascontiguousarray(kcT[g])
                    m["vcT"] = np.ascontiguousarray(vcT[g])
                    m["ksT"] = np.ascontiguousarray(ksT[g])
                    m["kwT"] = np.ascontiguousarray(kwT[g])
                    m["vs"] = _pm(vs[:, 64 * g:64 * g + 64])
                    m["vw"] = _pm(vw[:, 64 * g:64 * g + 64])
                    m["gl"] = _pm(gl[:, 12 * g:12 * g + 12])
                    m["dqT"] = np.ascontiguousarray(dqT[4 * g:4 * g + 4])
                    m["dkT"] = np.ascontiguousarray(dkT[4 * g:4 * g + 4])
                    m["dv"] = _pm(dv[:, 256 * g:256 * g + 256])
                    in_maps.append(m)
            ra = _run(nca, in_maps)
            oT_full = []
            for b in range(BATCH):
                o0, o1 = ra[2 * b]["oT"], ra[2 * b + 1]["oT"]
                oT_full.append(np.concatenate([o0[0:256], o1[0:256], o0[256:512], o1[256:512]], 0))
        else:
            nca = _prog(("att_odd",), lambda: build_att_odd(SEQ, 2))
            for b in range(BATCH):
                qT, kT, v = cat("qT", b, 2), cat("kT", b, 2), cat("v", b, 0)
                for hh in range(2):
                    in_maps.append({
                        "qT": np.ascontiguousarray(qT[8 * hh:8 * hh + 8]),
                        "kT": np.ascontiguousarray(kT[8 * hh:8 * hh + 8]),
                        "v": _pm(v[:, 512 * hh:512 * hh + 512]),
                    })
            ra = _run(nca, in_maps)
            oT_full = [np.concatenate([ra[2 * b]["oT"], ra[2 * b + 1]["oT"]], 0) for b in range(BATCH)]
    return out.astype(np.float32)


def build_inproj(p, nxt, S=SEQ):
    GT = 512
    NGRP = S // GT
    NQ = S // 128
    halls = [p.dram(f"hall{i}", [256, 8 * 1024], BF16, "Internal") for i in range(NT // 1024)]
    hvs = [h_[:, :].rearrange("(r k) (c t) -> r k c t", r=2, c=8) for h_ in halls]
    if nxt == "even":
        NCW = 13
        T = {
            "nqT": p.dram("nqT", [4, 64, S], BF16, "Internal"), "kcT": p.dram("kcT", [64, S], BF16, "Internal"),
            "vcT": p.dram("vcT", [64, S], BF16, "Internal"), "ksT": p.dram("ksT", [64, S], BF16, "Internal"),
            "kwT": p.dram("kwT", [64, S], BF16, "Internal"), "vs": p.dram("vs", [128, NQ, 64], BF16, "Internal"),
            "vw": p.dram("vw", [128, NQ, 64], BF16, "Internal"), "gl": p.dram("gl", [128, NQ, 12], F32, "Internal"),
            "dqT": p.dram("dqT", [4, 64, S], BF16, "Internal"), "dkT": p.dram("dkT", [4, 64, S], BF16, "Internal"),
            "dv": p.dram("dv", [128, NQ, 256], BF16, "Internal"),
        }
        cos_d = p.dram("cosT", [128, S], F32, "ExternalInput")
        sin_d = p.dram("sinT", [128, S], F32, "ExternalInput")
        kinds = [("rope", 0.125), ("rope", 0.125), ("rope", 1.0), ("rope", 1.0), ("fm", 1.0), ("tm", 1.0),
                 ("tmf", 1.0), ("rope", 0.125), ("rope", 0.125), ("rope", 1.0), ("rope", 1.0), ("tm", 1.0),
                 ("tm", 1.0)]
    else:
        NCW = 12
        T = {"qT": p.dram("qT", [8, 64, S], BF16, "Internal"), "kT": p.dram("kT", [8, 64, S], BF16, "Internal"),
             "v": p.dram("v", [128, NQ, 512], BF16, "Internal")}
        kinds = [("fm", 0.125)] * 4 + [("fm", 1.0)] * 4 + [("tm", 1.0)] * 4
    win_d = p.dram("win", [NCW, 128, 8, 128], F32, "ExternalInput")
    ps = [p.psum(f"ps{i}", [128, 512], F32) for i in range(8)]
    psi = [0]

    def next_ps():
        t = ps[psi[0] % 8]
        psi[0] += 1
        return t

    st_r = p.ring("ipst", 2, [128, 8, 128], F32)
    wb = [p.sbuf(f"ipw{i}", [128, 8, 128], BF16) for i in range(NCW)]
    wr = {}
    for ci, (kind, scale) in enumerate(kinds):
        st = st_r.next()
        p.dma(st[:], win_d[ci])
        if scale == 1.0:
            p.copy("act" if ci % 2 == 0 else "dve", wb[ci][:], st[:])
        else:
            p.ts("dve", wb[ci][:], st[:], scale, None, ALU.mult)
        if kind == "rope":
            wr[ci] = p.sbuf(f"ipr{ci}", [128, 8, 128], BF16)
            sv = st[:].rearrange("p c (h f d) -> p (c h) f d", h=2, f=2, d=32)
            wv = wr[ci][:].rearrange("p c (h f d) -> p (c h) f d", h=2, f=2, d=32)
            p.ts("pool", wv[:, :, 0, :], sv[:, :, 1, :], -scale, None, ALU.mult)
            p.ts("pool", wv[:, :, 1, :], sv[:, :, 0, :], scale, None, ALU.mult)
    hT_r = p.ring("iphT", 2, [128, 8, GT], BF16)
    fo_r = p.ring("ipfo", 3, [128, GT], BF16)
    tmo_r = p.ring("iptmo", 2, [128, 4, 128], BF16)
    tmf_r = p.ring("iptmf", 2, [128, 4, 128], F32)
    if nxt == "even":
        cos_r = p.ring("ipcos", 2, [128, GT], F32)
        sin_r = p.ring("ipsin", 2, [128, GT], F32)
        t1_r = p.ring("ipt1", 2, [128, GT], F32)
        t2_r = p.ring("ipt2", 2, [128, GT], F32)
    for tg in range(NGRP):
        r, tl0 = divmod(tg * GT, NT)
        t0 = tg * GT
        kb0 = tg * 4
        hT = hT_r.next()
        pi, off = divmod(tl0, 1024)
        p.dma(hT[:], hvs[pi][r, :, :, off:off + GT])
        if nxt == "even":
            cosg = cos_r.next()
            sing = sin_r.next()
            p.dma(cosg[:], cos_d[:, t0:t0 + GT])
            p.dma(sing[:], sin_d[:, t0:t0 + GT])
        for ci, (kind, scale) in enumerate(kinds):
            w = wb[ci]
            if kind in ("fm", "rope"):
                A = next_ps()
                for c in range(8):
                    p.mm(A[:], w[:, c, :], hT[:, c, :], c == 0, c == 7)
                fo = fo_r.next()
                if kind == "rope":
                    Bp = next_ps()
                    for c in range(8):
                        p.mm(Bp[:], wr[ci][:, c, :], hT[:, c, :], c == 0, c == 7)
                    t1 = t1_r.next()
                    t2 = t2_r.next()
                    p.tt("dve", t1[:], A[:], cosg[:], ALU.mult)
                    p.tt("dve", t2[:], Bp[:], sing[:], ALU.mult)
                    p.tt("pool", fo[:], t1[:], t2[:], ALU.add)
                else:
                    p.copy("act", fo[:], A[:])
                ts_ = slice(t0, t0 + GT)
                if nxt == "even":
                    if ci in (0, 1):
                        dsts = [(T["nqT"][2 * ci:2 * ci + 2, :, ts_].rearrange("h d t -> (h d) t"), fo[:])]
                    elif ci == 2:
                        dsts = [(T["kcT"][:, ts_], fo[0:64, :]), (T["ksT"][:, ts_], fo[64:128, :])]
                    elif ci == 3:
                        dsts = [(T["kwT"][:, ts_], fo[0:64, :])]
                    elif ci == 4:
                        dsts = [(T["vcT"][:, ts_], fo[0:64, :])]
                    elif ci in (7, 8):
                        j = ci - 7
                        dsts = [(T["dqT"][2 * j:2 * j + 2, :, ts_].rearrange("h d t -> (h d) t"), fo[:])]
                    else:
                        j = ci - 9
                        dsts = [(T["dkT"][2 * j:2 * j + 2, :, ts_].rearrange("h d t -> (h d) t"), fo[:])]
                else:
                    nm = "qT" if ci < 4 else "kT"
                    j = ci % 4
                    dsts = [(T[nm][2 * j:2 * j + 2, :, ts_].rearrange("h d t -> (h d) t"), fo[:])]
                for (dv_, sv_) in dsts:
                    p.dma(dv_, sv_, eng="pool")
            else:
                A = next_ps()
                Av = A[:].rearrange("p (i n) -> p i n", i=4)
                for i in range(4):
                    for c in range(8):
                        p.mm(Av[:, i, :], hT[:, c, i * 128:(i + 1) * 128], w[:, c, :], c == 0, c == 7)
                to = tmo_r.next() if kind == "tm" else tmf_r.next()
                p.copy("act", to[:], Av)
                ks_ = slice(kb0, kb0 + 4)
                if nxt == "even":
                    if ci == 5:
                        dsts = [(T["vs"][:, ks_, :], to[:, :, 0:64]), (T["vw"][:, ks_, :], to[:, :, 64:128])]
                    elif ci == 6:
                        dsts = [(T["gl"][:, ks_, :], to[:, :, 0:12])]
                    else:
                        j = ci - 11
                        dsts = [(T["dv"][:, ks_, j * 128:(j + 1) * 128], to[:])]
                else:
                    j = ci - 8
                    dsts = [(T["v"][:, ks_, j * 128:(j + 1) * 128], to[:])]
                for (dv_, sv_) in dsts:
                    p.dma(dv_, sv_, eng="pool")
    return T


PAIRS = [[0, 1], [2, 3], [4, 5], [6, 7]]


class RowSplit:
    def __init__(self, a, b, n):
        self.a, self.b, self.n = a, b, n

    def __getitem__(self, idx):
        rs, cs = idx
        if rs.start >= self.n:
            return self.b[rs.start - self.n:rs.stop - self.n, cs]
        return self.a[rs, cs]


def build_fused():
    p = Prog()
    p.enable_arena(200 * 1024)
    ext = lambda name, shape, dt=F32: p.dram_real(name, shape, dt, "ExternalInput")
    x_in = ext("x", [NT, D_MODEL])
    out_d = p.dram_real("out", [NT, D_MODEL], F32, "ExternalOutput")
    xbuf = p.dram_real("xbuf", [NT, D_MODEL], F32, "Internal")
    hsend = [p.dram_real(f"hsend{i}", [128, 8, 1024], BF16, "Internal") for i in range(NT // 1024)]
    hall = [p.dram_real(f"hall{i}", [256, 8 * 1024], BF16, "Internal") for i in range(NT // 1024)]
    osend = p.dram_real("osend", [256, SEQ], BF16, "Internal")
    osendb = p.dram_real("osendb", [256, SEQ], BF16, "Internal")
    oall = p.dram_real("oall", [512, SEQ], BF16, "Internal")
    oallb = p.dram_real("oallb", [512, SEQ], BF16, "Internal")
    cosT = ext("cosT", [128, SEQ])
    sinT = ext("sinT", [128, SEQ])
    ftab = ext("ftab", [128, SEQ // 128, SEQ // 64])
    g1 = [ext(f"g1_{l}", [1, D_MODEL]) for l in range(DEPTH + 1)]
    win = [ext(f"win_{l}", [13 if l % 2 == 0 else 12, 128, 8, 128]) for l in range(DEPTH)]
    wo = [ext(f"wo_{l}", [8, 128, D_MODEL]) for l in range(DEPTH)]
    g2 = [ext(f"g2_{l}", [1, D_MODEL]) for l in range(DEPTH)]
    wgu = [ext(f"wgu_{l}", [NFC, 128, 2, 8, 128]) for l in range(DEPTH)]
    wd = [ext(f"wd_{l}", [NFC, 128, D_MODEL]) for l in range(DEPTH)]
    ev = {}
    for e in range(2):
        ev[e] = {"wck": ext(f"wck_{e}", [64, 32, 64]), "wcv": ext(f"wcv_{e}", [64, 32, 64]),
                 "pkT": ext(f"pkT_{e}", [64, 32]), "pvT": ext(f"pvT_{e}", [64, 32]),
                 "lqk": ext(f"lqk_{e}", [1, 256]), "lami": ext(f"lami_{e}", [1, 2]),
                 "subln": ext(f"subln_{e}", [1, 128])}
    scratch = {}
    for layer in range(DEPTH + 1):
        has_prev = layer > 0
        nxt = None if layer == DEPTH else ("even" if layer % 2 == 0 else "odd")
        ov = {"x": x_in if layer == 0 else xbuf, "xo": xbuf, "out": out_d, "g1": g1[layer],
              "hsend0": hsend[0], "hsend1": hsend[1], "oT": oall, "oTb": oallb}
        if has_prev:
            ov.update({"wo": wo[layer - 1], "g2": g2[layer - 1], "wgu": wgu[layer - 1], "wd": wd[layer - 1]})
        p.dram_override = ov
        gather_h = lambda i: p.collective("AllGather", hall[i].all(),
                                          hsend[i][:, :, :].rearrange("k c t -> k (c t)"), PAIRS)
        build_tok2(p, has_prev, nxt, pass_hook=gather_h if nxt is not None else None)
        if nxt is None:
            break
        p.phase_reset()
        ov = dict(scratch)
        ov.update({"hall0": hall[0], "hall1": hall[1], "win": win[layer], "cosT": cosT, "sinT": sinT})
        p.dram_override = ov
        T = build_inproj(p, nxt)
        scratch.update(T)
        p.phase_reset()
        ov = dict(scratch)
        ov["oT"] = RowSplit(osend, osendb, 256)
        if nxt == "even":
            ov.update(ev[layer // 2])
            ov["ftab"] = ftab
        first_ag = lambda: p.collective("AllGather", oall.all(), osend.all(), PAIRS)
        if nxt == "even":
            p.dram_override = ov
            build_att_even(SEQ, p=p, mid_hook=first_ag)
        else:
            p.dram_override = ov
            build_att_odd(SEQ, 2, p=p, mid_hook=first_ag)
        p.collective("AllGather", oallb.all(), osendb.all(), PAIRS)
        p.phase_reset()
    return p


def lay_win_even_core(w, g):
    z = lambda a: np.pad(a, ((0, 0), (0, 128 - a.shape[1])))
    c = [w[:, 256 * g:256 * g + 128], w[:, 256 * g + 128:256 * g + 256],
         np.concatenate([w[:, 512 + 64 * g:576 + 64 * g], w[:, 768 + 64 * g:832 + 64 * g]], 1),
         z(w[:, 1024 + 64 * g:1088 + 64 * g]), z(w[:, 640 + 64 * g:704 + 64 * g]),
         np.concatenate([w[:, 896 + 64 * g:960 + 64 * g], w[:, 1152 + 64 * g:1216 + 64 * g]], 1),
         z(w[:, 1280 + 12 * g:1292 + 12 * g]),
         w[:, 1304 + 256 * g:1304 + 256 * g + 128], w[:, 1304 + 256 * g + 128:1304 + 256 * g + 256],
         w[:, 1816 + 256 * g:1816 + 256 * g + 128], w[:, 1816 + 256 * g + 128:1816 + 256 * g + 256],
         w[:, 2328 + 256 * g:2328 + 256 * g + 128], w[:, 2328 + 256 * g + 128:2328 + 256 * g + 256]]
    wp = np.stack(c, 0)
    return np.ascontiguousarray(wp.reshape(13, 8, 128, 128).transpose(0, 2, 1, 3))


def lay_win_odd_core(w, hh):
    c = []
    for base in (0, 1024, 2048):
        for j in range(4):
            a = base + 512 * hh + 128 * j
            c.append(w[:, a:a + 128])
    wp = np.stack(c, 0)
    return np.ascontiguousarray(wp.reshape(12, 8, 128, 128).transpose(0, 2, 1, 3))


def kernel_fused(x, norm_mix, norm_ffn, norm_final, even_w_in, even_w_out,
                 cmp_pos_k, cmp_w_k, cmp_pos_v, cmp_w_v,
                 diff_lq1, diff_lk1, diff_lq2, diff_lk2, diff_subln,
                 odd_w_in, odd_w_out, ffn_w_gate, ffn_w_up, ffn_w_down):
    f32 = lambda a: np.ascontiguousarray(np.asarray(a, dtype=np.float32))
    x = f32(x)
    nc = _prog(("fused",), build_fused)
    cosT, sinT = rope_tables_T(np.arange(SEQ))
    common = {"cosT": cosT, "sinT": sinT, "ftab": make_ftab(SEQ)}
    for l in range(DEPTH):
        common[f"g1_{l}"] = f32(norm_mix[l])[None, :]
        w_out = f32(even_w_out[l // 2]) if l % 2 == 0 else f32(odd_w_out[l // 2])
        if l % 2 == 1:
            w_out = np.concatenate([w_out[0:256], w_out[512:768], w_out[256:512], w_out[768:1024]], 0)
        common[f"wo_{l}"] = lay_wo(w_out)
        common[f"g2_{l}"] = f32(norm_ffn[l])[None, :]
        common[f"wgu_{l}"] = lay_wgu(f32(ffn_w_gate[l]), f32(ffn_w_up[l]))
        common[f"wd_{l}"] = lay_wd(f32(ffn_w_down[l]))
    common[f"g1_{DEPTH}"] = f32(norm_final)[None, :]
    for e in range(2):
        lam_init = 0.8 - 0.6 * math.exp(-0.3 * (2 * e))
        common[f"wck_{e}"] = np.ascontiguousarray(f32(cmp_w_k[e]).reshape(32, 64, 64).transpose(1, 0, 2))
        common[f"wcv_{e}"] = np.ascontiguousarray(f32(cmp_w_v[e]).reshape(32, 64, 64).transpose(1, 0, 2))
        common[f"pkT_{e}"] = np.ascontiguousarray(f32(cmp_pos_k[e]).T)
        common[f"pvT_{e}"] = np.ascontiguousarray(f32(cmp_pos_v[e]).T)
        common[f"lqk_{e}"] = np.ascontiguousarray(np.stack([f32(diff_lq1[e]), f32(diff_lk1[e]), f32(diff_lq2[e]),
                                                            f32(diff_lk2[e])], 0).reshape(1, 256))
        common[f"lami_{e}"] = np.array([[lam_init, 1.0 - lam_init]], np.float32)
        common[f"subln_{e}"] = f32(diff_subln[e])[None, :]
    wins = {}
    for par in range(2):
        for l in range(DEPTH):
            wins[(par, l)] = (lay_win_even_core(f32(even_w_in[l // 2]), par) if l % 2 == 0
                              else lay_win_odd_core(f32(odd_w_in[l // 2]), par))
    in_maps = []
    for c in range(8):
        m = dict(common)
        m["x"] = np.ascontiguousarray(x[c // 2, (c % 2) * NT:(c % 2 + 1) * NT])
        for l in range(DEPTH):
            m[f"win_{l}"] = wins[(c % 2, l)]
        in_maps.append(m)
    r = _run(nc, in_maps)
    out = np.stack([np.concatenate([r[2 * b]["out"], r[2 * b + 1]["out"]], 0) for b in range(BATCH)], 0)
    return out.astype(np.float32)


def kernel(**inputs):
    return kernel_fused(**inputs)


def build_tok2(p, has_prev, nxt, pass_hook=None):
    PT = 1024
    NPASS = NT // PT
    TPP = PT // 128
    NH2 = NFC // 2
    x_d = p.dram("x", [NT, D_MODEL], F32, "ExternalInput")
    g1_d = p.dram("g1", [1, D_MODEL], F32, "ExternalInput")
    if getattr(p, "par", None) is None:
        p.par = p.nc.partition_id() % 2
    par = p.par
    if has_prev:
        oT_d = p.dram("oT", [512, 2 * NT], BF16, "Internal")
        oTb_d = p.dram("oTb", [512, 2 * NT], BF16, "Internal")
        wo_d = p.dram("wo", [8, 128, D_MODEL], F32, "ExternalInput")
        g2_d = p.dram("g2", [1, D_MODEL], F32, "ExternalInput")
        wgu_d = p.dram("wgu", [NFC, 128, 2, 8, 128], F32, "ExternalInput")
        wd_d = p.dram("wd", [NFC, 128, D_MODEL], F32, "ExternalInput")
    if nxt is not None:
        xo_d = p.dram("xo", [NT, D_MODEL], F32, "Internal")
        hs_d = [p.dram(f"hsend{i}", [128, 8, PT], BF16, "Internal") for i in range(NPASS)]
    else:
        out_d = p.dram("out", [NT, D_MODEL], F32, "ExternalOutput")
    ident = make_ident(p)
    xg = p.sbuf("xg", [128, TPP, D_MODEL], F32)
    hT = p.sbuf("hT", [128, 8, PT], BF16)
    hs_r = p.ring("hs", 2, [128, D_MODEL], BF16)
    junk = p.sbuf("junk", [128, D_MODEL], F32)
    ss_r = p.ring("ss", 2, [128, 1], F32)
    gB1 = p.sbuf("gB1", [128, D_MODEL], F32)
    p.dma(gB1[:], g1_d[0:1, :].to_broadcast([128, D_MODEL]))
    ps = [p.psum(f"ps{i}", [128, 512], F32) for i in range(8)]
    psi = [0]

    def next_ps():
        t = ps[psi[0] % 8]
        psi[0] += 1
        return t

    if has_prev:
        st_r = p.ring("st", 3, [128, 2, 8, 128], F32)
        wb_r = p.ring("wb", 3, [128, 2, 8, 128], BF16)
        gB2 = p.sbuf("gB2", [128, D_MODEL], F32)
        p.dma(gB2[:], g2_d[0:1, :].to_broadcast([128, D_MODEL]))
        wo = p.sbuf("wo_sb", [128, 8, D_MODEL], BF16)
        for c in range(8):
            st = st_r.next()
            stv = st[:].rearrange("p a b c -> p (a b c)")[:, 0:D_MODEL]
            p.dma(stv, wo_d[c])
            p.copy("dve", wo[:, c, :], stv)
        oTg_r = p.ring("oTg", 2, [128, 8, 512], BF16)
        aT = p.sbuf("aT", [128, NH2, PT], BF16)
        wdr = p.sbuf("wdr", [128, NH2, D_MODEL], BF16)
        sg_r = p.ring("sg", 2, [128, 512], F32)
    else:
        og_r = None
    og_r = p.ring("og", 2, [128, D_MODEL], F32) if nxt is None else None

    def rstd_of(i):
        ss = ss_r.next()
        p.act(junk[:], xg[:, i, :], AF.Square, accum_out=ss[:])
        p.ts("dve", ss[:], ss[:], 1.0 / D_MODEL, NORM_EPS, ALU.mult, ALU.add)
        p.act(ss[:], ss[:], AF.Ln)
        p.act(ss[:], ss[:], AF.Exp, scale=-0.5)
        return ss

    def rmsnorm_to_hT(gB):
        for i in range(TPP):
            rs = rstd_of(i)
            hs = hs_r.next()
            p.stt("dve", hs[:], xg[:, i, :], rs[:], gB[:], ALU.mult, ALU.mult)
            pt = next_ps()
            ptv = pt[:].bitcast(BF16).rearrange("p (c t) -> p c t", c=8)
            for c in range(8):
                p.tr(ptv[:, c, :], hs[:, c * 128:(c + 1) * 128], ident[:])
            p.copy("dve", hT[:, :, i * 128:(i + 1) * 128], ptv)

    for ps_ in range(NPASS):
        t0 = ps_ * PT
        p.dma(xg[:], x_d[t0:t0 + PT, :].rearrange("(i p) d -> p i d", p=128))
        if has_prev:
            for sub in range(PT // 512):
                oTg = oTg_r.next()
                tt0 = t0 + sub * 512
                p.dma(oTg[:, 0:4, :], oT_d[:, bass.ds(par * NT + tt0, 512)].rearrange("(c f) t -> f c t", f=128))
                p.dma(oTg[:, 4:8, :], oTb_d[:, bass.ds(par * NT + tt0, 512)].rearrange("(c f) t -> f c t", f=128))
                for i4 in range(4):
                    i = sub * 4 + i4
                    for hf in range(2):
                        acc = next_ps()
                        for c in range(8):
                            p.mm(acc[:], oTg[:, c, i4 * 128:(i4 + 1) * 128], wo[:, c, hf * 512:(hf + 1) * 512],
                                 c == 0, c == 7)
                        xs = xg[:, i, hf * 512:(hf + 1) * 512]
                        p.tt("dve", xs, xs, acc[:], ALU.add)
            rmsnorm_to_hT(gB2)
            for hh in range(2):
                for n in range(NH2):
                    nn = hh * NH2 + n
                    st = st_r.next()
                    wb = wb_r.next()
                    p.dma(st[:], wgu_d[nn])
                    p.copy("act", wb[:], st[:])
                    st2 = st_r.next()
                    stv = st2[:].rearrange("p a b c -> p (a b c)")[:, 0:D_MODEL]
                    p.dma(stv, wd_d[nn])
                    p.copy("dve", wdr[:, n, :], stv)
                    for sub in range(PT // 512):
                        G = next_ps()
                        Uu = next_ps()
                        for c in range(8):
                            p.mm(G[:], wb[:, 0, c, :], hT[:, c, sub * 512:(sub + 1) * 512], c == 0, c == 7)
                        for c in range(8):
                            p.mm(Uu[:], wb[:, 1, c, :], hT[:, c, sub * 512:(sub + 1) * 512], c == 0, c == 7)
                        sg = sg_r.next()
                        p.act(sg[:], G[:], AF.Silu)
                        p.tt("dve", aT[:, n, sub * 512:(sub + 1) * 512], sg[:], Uu[:], ALU.mult)
                for i in range(TPP):
                    for hf in range(2):
                        acc = next_ps()
                        for n in range(NH2):
                            p.mm(acc[:], aT[:, n, i * 128:(i + 1) * 128], wdr[:, n, hf * 512:(hf + 1) * 512],
                                 n == 0, n == NH2 - 1)
                        xs = xg[:, i, hf * 512:(hf + 1) * 512]
                        p.tt("dve", xs, xs, acc[:], ALU.add)
        if nxt is None:
            for i in range(TPP):
                rs = rstd_of(i)
                og = og_r.next()
                p.stt("dve", og[:], xg[:, i, :], rs[:], gB1[:], ALU.mult, ALU.mult)
                p.dma(out_d[t0 + i * 128:t0 + (i + 1) * 128, :], og[:], eng="pool")
            continue
        p.dma(xo_d[t0:t0 + PT, :].rearrange("(i p) d -> p i d", p=128), xg[:], eng="pool")
        rmsnorm_to_hT(gB1)
        p.dma(hs_d[ps_][:, :, :], hT[:], eng="pool")
        if pass_hook is not None:
            pass_hook(ps_)
    return p
```

```python
import math
import numpy as np
import ml_dtypes
from contextlib import ExitStack
import concourse.bass as bass
import concourse.mybir as mybir
from concourse.bass_utils import run_bass_kernel_spmd

F32 = mybir.dt.float32
BF16 = mybir.dt.bfloat16
AF = mybir.ActivationFunctionType
ALU = mybir.AluOpType
AX = mybir.AxisListType
NPBF16 = ml_dtypes.bfloat16

SEM_LIMIT = 30000
N_DMA_SEMS = {"sp": 14, "pool": 6, "act": 4}

D_MODEL = 1024
SEQ = 4096
BATCH = 4
DEPTH = 4
HD = 64
FFN_H = 2816
NFC = FFN_H // 128
EVEN_IN = 2840
NORM_EPS = 1e-6
NT = 2048


class Dep:
    __slots__ = ("last_w", "readers")

    def __init__(self):
        self.last_w = None
        self.readers = []


class V:
    __slots__ = ("tl", "ap")

    def __init__(self, tl, ap):
        self.tl = tl
        self.ap = ap

    def __getitem__(self, idx):
        return V(self.tl, self.ap[idx])

    def rearrange(self, pattern, **kw):
        return V(self.tl, self.ap.rearrange(pattern, **kw))

    def bitcast(self, dt):
        return V(self.tl, self.ap.bitcast(dt))

    def to_broadcast(self, shape):
        return V(self.tl, self.ap.to_broadcast(list(shape)))

    def unsqueeze(self, ax):
        return V(self.tl, self.ap.unsqueeze(ax))

    @property
    def shape(self):
        return self.ap.shape


class Tl:
    def __init__(self, h, name, is_ap=False):
        self.h = h
        self.name = name
        self.dep = Dep()
        self.is_ap = is_ap

    def __getitem__(self, idx):
        return V(self, self.h[idx])

    def all(self):
        return V(self, self.h[:] if not self.is_ap else self.h)

    def __hash__(self):
        return id(self)


class Op:
    __slots__ = ("eng", "fn", "deps", "signal", "event", "is_dma", "dma_prev", "idx", "inc")


def _ap(v):
    return v.ap if isinstance(v, V) else v


class Prog:
    ENGS = ("pe", "act", "dve", "pool", "sp")

    def __init__(self):
        self.nc = bass.Bass("TRN2", target_bir_lowering=False)
        self.es = ExitStack()
        self.ops = {e: [] for e in self.ENGS}
        self.nops = 0
        self.same_engine_sync = True
        self._uid = 0
        self.arena = None
        self.arena_off = 0
        self.arena_base = 0
        self.ps_pool = None
        self.ps_i = 0
        self.dram_override = {}
        self.dma_pending = []
        self.bar = None

    def dram(self, name, shape, dt, kind):
        if name in self.dram_override:
            return self.dram_override[name]
        return self.dram_real(name, shape, dt, kind)

    def dram_real(self, name, shape, dt, kind):
        h = self.nc.dram_tensor(name, list(shape), dt, kind=kind)
        return Tl(h.ap(), name, is_ap=True)

    def sbuf(self, name, shape, dt):
        if self.arena is not None:
            esz = {F32: 4, BF16: 2}[dt]
            free = 1
            for d in shape[1:]:
                free *= d
            nb = (free * esz + 63) // 64 * 64
            off = self.arena_off
            assert off + nb <= self.arena_size, ("arena overflow", name, off, nb)
            self.arena_off = off + nb
            self.arena_peak = max(getattr(self, "arena_peak", 0), self.arena_off)
            ap = self.arena[0:shape[0], off:off + free * esz].bitcast(dt)
            if len(shape) == 3:
                ap = ap.rearrange("p (a b) -> p a b", a=shape[1])
            elif len(shape) == 4:
                ap = ap.rearrange("p (a b c) -> p a b c", a=shape[1], b=shape[2])
            return Tl(ap, name, is_ap=True)
        h = self.es.enter_context(self.nc.sbuf_tensor(name, list(shape), dt))
        return Tl(h, name)

    def psum(self, name, shape, dt):
        if self.ps_pool is not None:
            assert list(shape) == [128, 512] and dt == F32
            t = self.ps_pool[self.ps_i % 8]
            self.ps_i += 1
            return t
        h = self.es.enter_context(self.nc.psum_tensor(name, list(shape), dt))
        return Tl(h, name)

    def enable_arena(self, nbytes):
        U8 = mybir.dt.uint8
        h = self.es.enter_context(self.nc.sbuf_tensor("arena", [128, nbytes], U8))
        self.arena_size = nbytes
        self.bar = {e: Tl(self.es.enter_context(self.nc.sbuf_tensor(f"bar_{e}", [128, 8], F32)), f"bar_{e}")
                    for e in ("act", "dve", "pool", "sp", "src", "w_act", "w_dve", "w_pool", "w_sp")}
        self.bar_lhs = Tl(self.es.enter_context(self.nc.sbuf_tensor("bar_lhs", [128, 8], BF16)), "bar_lhs")
        self.ps_pool = []
        self.ps_pair = {}
        for j in range(4):
            hh = self.es.enter_context(self.nc.psum_tensor(f"psd{j}", [128, 1024], F32))
            a = Tl(hh[:, 0:512], f"psb{2 * j}", is_ap=True)
            b_ = Tl(hh[:, 512:1024], f"psb{2 * j + 1}", is_ap=True)
            self.ps_pool += [a, b_]
            self.ps_pair[id(a)] = (hh[:, :], b_)
        self.arena = h[:]
        self.memset("pool", self.bar["src"][:], 0.0)
        self.memset("pool", self.bar_lhs[:], 0.0)

    def phase_reset(self):
        b = self.bar
        pend = list(self.dma_pending)
        self.dma_pending = []
        pst = self.ps_pool[7]
        m = {}
        m["act"] = self.copy("act", b["act"][:], b["src"][:])
        m["dve"] = self.copy("dve", b["dve"][:], b["src"][:])
        m["pool"] = self.copy("pool", b["pool"][:], b["src"][:])
        m["sp"] = self.dma(b["sp"][:], b["src"][:])
        m["pe"] = self.mm(pst[0:8, 0:8], self.bar_lhs[:], self.bar_lhs[:], True, True)
        allr = [b["act"][:], b["dve"][:], b["pool"][:], b["sp"][:], pst[0:8, 0:8]]
        self.add("act", lambda e, o=b["w_act"][:].ap, i=b["src"][:].ap: e.copy(o, i), allr, [b["w_act"][:]], extra=pend)
        self.add("dve", lambda e, o=b["w_dve"][:].ap, i=b["src"][:].ap: e.tensor_copy(o, i), allr, [b["w_dve"][:]], extra=pend)
        self.add("pool", lambda e, o=b["w_pool"][:].ap, i=b["src"][:].ap: e.tensor_copy(o, i), allr, [b["w_pool"][:]], extra=pend)
        self.add("sp", lambda e, o=b["w_sp"][:].ap, i=b["src"][:].ap: e.dma_start(out=o, in_=i), allr, [b["w_sp"][:]],
                 dma=True, extra=pend)
        self.add("pe", lambda e, o=pst[0:8, 8:16].ap, l=self.bar_lhs[:].ap: e.matmul(o, l, l, start=True, stop=True),
                 allr, [pst[0:8, 8:16]], extra=pend)
        self.arena_off = self.arena_base
        self.ps_i = 0

    def ring(self, name, n, shape, dt, space="sbuf"):
        f = self.sbuf if space == "sbuf" else self.psum
        return Ring([f(f"{name}{i}", shape, dt) for i in range(n)])

    def add(self, eng, fn, reads=(), writes=(), dma=False, extra=()):
        op = Op()
        op.eng = eng
        op.fn = fn
        op.is_dma = dma
        op.signal = False
        op.event = None
        op.dma_prev = None
        op.inc = 16
        op.idx = self.nops
        self.nops += 1
        deps = {}
        rt = []
        wt = []
        for r in reads:
            if isinstance(r, V):
                rt.append(r.tl)
        for w in writes:
            if isinstance(w, V):
                wt.append(w.tl)
        for r in rt:
            lw = r.dep.last_w
            if lw is not None:
                deps[lw.idx] = lw
        for w in wt:
            lw = w.dep.last_w
            if lw is not None:
                deps[lw.idx] = lw
            for rd in w.dep.readers:
                deps[rd.idx] = rd
        for d in extra:
            deps[d.idx] = d
        out = []
        for d in deps.values():
            if d.eng == eng and not d.is_dma and not dma:
                if eng == "pe" or not self.same_engine_sync:
                    continue
            d.signal = True
            out.append(d)
        op.deps = out
        for r in rt:
            r.dep.readers.append(op)
        for w in wt:
            w.dep.last_w = op
            w.dep.readers = []
        self.ops[eng].append(op)
        if dma:
            self.dma_pending.append(op)
        return op

    def dma(self, out, in_, eng="sp"):
        o, i = out.ap, in_.ap
        return self.add(eng, lambda e: e.dma_start(out=o, in_=i), [in_], [out], dma=True)

    def collective(self, kind, out, in_, groups):
        o, i = out.ap, in_.ap
        op = self.add("pool", lambda e: e.collective_compute(kind, ALU.bypass, replica_groups=groups,
                                                             ins=[i], outs=[o]), [in_], [out], dma=True)
        op.inc = 1
        return op

    def mm(self, out, lhsT, rhs, start, stop):
        o, l, r = out.ap, lhsT.ap, rhs.ap
        return self.add("pe", lambda e: e.matmul(o, l, r, start=start, stop=stop), [lhsT, rhs], [out])

    def tr(self, out, in_, ident):
        o, i, d = out.ap, in_.ap, ident.ap
        return self.add("pe", lambda e: e.transpose(o, i, d), [in_, ident], [out])

    def pair(self, tl):
        full, other = self.ps_pair[id(tl)]
        return V(tl, full), other.all()

    def act(self, out, in_, func, bias=None, scale=None, accum_out=None, extra_r=(), extra_w=()):
        kw = {}
        rd = [in_] + list(extra_r)
        wr = [out] + list(extra_w)
        if bias is not None:
            kw["bias"] = _ap(bias)
            rd.append(bias)
        if scale is not None:
            kw["scale"] = _ap(scale)
            rd.append(scale)
        if accum_out is not None:
            kw["accum_out"] = accum_out.ap
            wr.append(accum_out)
        o, i = out.ap, in_.ap
        return self.add("act", lambda e: e.activation(out=o, in_=i, func=func, **kw), rd, wr)

    def tt(self, eng, out, in0, in1, op):
        o, a, b = out.ap, in0.ap, in1.ap
        return self.add(eng, lambda e: e.tensor_tensor(o, a, b, op), [in0, in1], [out])

    def ts(self, eng, out, in0, s1, s2, op0, op1=None, accum_out=None):
        o, a = out.ap, in0.ap
        a1, a2 = _ap(s1), _ap(s2)
        wr = [out]
        kw = {}
        if accum_out is not None:
            kw["accum_out"] = accum_out.ap
            wr.append(accum_out)
        if op1 is None:
            return self.add(eng, lambda e: e.tensor_scalar(o, a, a1, None, op0, **kw), [in0, s1], wr)
        return self.add(eng, lambda e: e.tensor_scalar(o, a, a1, a2, op0, op1, **kw), [in0, s1, s2], wr)

    def stt(self, eng, out, in0, scalar, in1, op0, op1):
        o, a, s, b = out.ap, in0.ap, _ap(scalar), in1.ap
        return self.add(eng, lambda e: e.scalar_tensor_tensor(o, a, s, b, op0, op1), [in0, scalar, in1], [out])

    def copy(self, eng, out, in_):
        o, i = out.ap, in_.ap
        if eng == "act":
            return self.add(eng, lambda e: e.copy(o, i), [in_], [out])
        return self.add(eng, lambda e: e.tensor_copy(o, i), [in_], [out])

    def memset(self, eng, out, val):
        o = out.ap
        return self.add(eng, lambda e: e.memset(o, val), [], [out])

    def asel(self, out, in_, pattern, cmp, fill, base, cm):
        o, i = out.ap, in_.ap
        return self.add("pool", lambda e: e.affine_select(out=o, in_=i, pattern=pattern, compare_op=cmp,
                                                          fill=fill, base=base, channel_multiplier=cm),
                        [in_], [out])

    def recip(self, out, in_):
        o, i = out.ap, in_.ap
        return self.add("dve", lambda e: e.reciprocal(o, i), [in_], [out])

    def sqrt(self, out, in_):
        o, i = out.ap, in_.ap
        return self.add("act", lambda e: e.sqrt(o, i), [in_], [out])

    def max8(self, out, in_):
        o, i = out.ap, in_.ap
        return self.add("dve", lambda e: e.max(out=o, in_=i), [in_], [out])

    def match_replace(self, out, vals, in_, imm):
        o, v, i = out.ap, vals.ap, in_.ap
        return self.add("dve", lambda e: e.match_replace(out=o, in_to_replace=v, in_values=i, imm_value=imm),
                        [vals, in_], [out])

    def reduce(self, eng, out, in_, op, axis=AX.X):
        o, i = out.ap, in_.ap
        return self.add(eng, lambda e: e.tensor_reduce(o, i, axis, op), [in_], [out])

    def build(self):
        nc = self.nc
        es = self.es
        eng_sems = {}
        for e in self.ENGS:
            nsig = sum(1 for o in self.ops[e] if o.signal and not o.is_dma)
            nep = max(1, -(-nsig // SEM_LIMIT))
            eng_sems[e] = [es.enter_context(nc.semaphore(f"s_{e}_{i}")) for i in range(nep)]
        dma_sems = {}
        for e in self.ENGS:
            nd = sum(1 for o in self.ops[e] if o.is_dma)
            if nd:
                n = min(N_DMA_SEMS.get(e, 4), nd)
                dma_sems[e] = [es.enter_context(nc.semaphore(f"d_{e}_{i}")) for i in range(n)]
        cc_sem = es.enter_context(nc.semaphore("s_cc"))
        final_dma = {}
        for e in self.ENGS:
            cnt = 0
            dcnt = 0
            uses = {}
            for o in self.ops[e]:
                if o.is_dma:
                    pool = dma_sems[e]
                    if o.inc == 1:
                        s = cc_sem
                    else:
                        s = pool[dcnt % len(pool)]
                        dcnt += 1
                    k = uses.get(id(s), 0)
                    o.dma_prev = (s, k)
                    uses[id(s)] = k + o.inc
                    o.event = (s, k + o.inc)
                    final_dma[(e, id(s))] = (s, k + o.inc)
                elif o.signal:
                    ep = cnt // SEM_LIMIT
                    o.event = (eng_sems[e][ep], cnt % SEM_LIMIT + 1)
                    cnt += 1
        ops = self.ops
        nwaits = {e: 0 for e in self.ENGS}

        def emit(e, eng):
            seen = {}

            def wait(s, v):
                if v > 0 and seen.get(id(s), 0) < v:
                    eng.wait_ge(s, v)
                    seen[id(s)] = v
                    nwaits[e] += 1

            for o in ops[e]:
                need = {}
                for d in o.deps:
                    s_, v_ = d.event
                    if need.get(id(s_), (None, 0))[1] < v_:
                        need[id(s_)] = (s_, v_)
                for s_, v_ in need.values():
                    wait(s_, v_)
                if o.is_dma:
                    wait(*o.dma_prev)
                ins = o.fn(eng)
                if o.is_dma:
                    ins.then_inc(o.event[0], o.inc)
                elif o.signal:
                    ins.then_inc(o.event[0], 1)
            for (ee, _), (s, v) in final_dma.items():
                if ee == e:
                    wait(s, v)

        with nc.Block() as block:
            @block.tensor
            def _(eng):
                emit("pe", eng)

            @block.scalar
            def _(eng):
                emit("act", eng)

            @block.vector
            def _(eng):
                emit("dve", eng)

            @block.gpsimd
            def _(eng):
                emit("pool", eng)

            @block.sync
            def _(eng):
                emit("sp", eng)

        self.stats = {e: (len(ops[e]), nwaits[e]) for e in self.ENGS}
        es.close()
        return nc


class Ring:
    def __init__(self, tiles):
        self.tiles = tiles
        self.i = 0

    def next(self):
        t = self.tiles[self.i % len(self.tiles)]
        self.i += 1
        return t


def run_pipeline(units, depth=1):
    n = len(units)

    def doA(u):
        if "pre" in u:
            u["pre"]()
        if "A" in u:
            u["A"]()
    for j in range(min(depth, n)):
        doA(units[j])
    for k in range(n):
        if k + depth < n:
            doA(units[k + depth])
        u = units[k]
        for key in ("B", "C", "post"):
            if key in u:
                u[key]()


def make_ident(p, dt=BF16):
    ident = p.sbuf("ident", [128, 128], dt)
    p.memset("pool", ident[:], 1.0)
    p.asel(ident[:], ident[:], [[-1, 128]], ALU.is_equal, 0.0, 0, 1)
    return ident


def even_chunks():
    ch = []
    for j in range(4):
        ch.append(("rope", 0.125, "nqT", j))
    ch.append(("rope", 1.0, "kcT", 0))
    ch.append(("fm", 1.0, "vcT", 0))
    ch.append(("rope", 1.0, "ksT", 0))
    ch.append(("tm", 1.0, "vs", 0))
    ch.append(("rope", 1.0, "kwT", 0))
    ch.append(("tm", 1.0, "vw", 0))
    ch.append(("tmf", 1.0, "gl", 0))
    for j in range(4):
        ch.append(("rope", 0.125, "dqT", j))
    for j in range(4):
        ch.append(("rope", 1.0, "dkT", j))
    for j in range(4):
        ch.append(("tm", 1.0, "dv", j))
    return ch


def odd_chunks():
    ch = []
    for j in range(8):
        ch.append(("fm", 0.125, "qT", j))
    for j in range(8):
        ch.append(("fm", 1.0, "kT", j))
    for j in range(8):
        ch.append(("tm", 1.0, "v", j))
    return ch


def inproj_outputs(nxt):
    if nxt == "even":
        return {
            "nqT": ([8, 64, NT], BF16), "kcT": ([2, 64, NT], BF16), "vcT": ([2, 64, NT], BF16),
            "ksT": ([2, 64, NT], BF16), "kwT": ([2, 64, NT], BF16),
            "vs": ([NT, 128], BF16), "vw": ([NT, 128], BF16), "gl": ([NT, 128], F32),
            "dqT": ([8, 64, NT], BF16), "dkT": ([8, 64, NT], BF16), "dv": ([NT, 512], BF16),
        }
    return {"qT": ([16, 64, NT], BF16), "kT": ([16, 64, NT], BF16), "v": ([NT, 1024], BF16)}


def build_tok(has_prev, nxt, p=None, fused=False):
    p = p or Prog()
    GT = 512
    NG = NT // GT
    TPG = GT // 128
    x_d = p.dram("x", [NT, D_MODEL], F32, "ExternalInput")
    if has_prev:
        oT_d = p.dram("oT", [512 if fused else D_MODEL, 2 * NT if fused else NT], BF16, "ExternalInput")
        oTb_d = p.dram("oTb", [512, 2 * NT], BF16, "ExternalInput") if fused else None
        par = (p.nc.partition_id() % 2) if fused else None
        wo_d = p.dram("wo", [8, 128, D_MODEL], F32, "ExternalInput")
        g2_d = p.dram("g2", [1, D_MODEL], F32, "ExternalInput")
        wgu_d = p.dram("wgu", [NFC, 128, 2, 8, 128], F32, "ExternalInput")
        wd_d = p.dram("wd", [NFC, 128, D_MODEL], F32, "ExternalInput")
    g1_d = p.dram("g1", [1, D_MODEL], F32, "ExternalInput")
    if nxt is not None and fused:
        xo_d = p.dram("xo", [NT, D_MODEL], F32, "ExternalOutput")
        hs_d = p.dram("hsend", [128, 4, NT], BF16, "Internal")
        hsb_d = p.dram("hsendb", [128, 4, NT], BF16, "Internal")
    elif nxt is not None:
        chunks = even_chunks() if nxt == "even" else odd_chunks()
        NCW = len(chunks)
        win_d = p.dram("win", [NCW, 128, 8, 128], F32, "ExternalInput")
        outs = {k: p.dram(k, sh, dt, "ExternalOutput") for k, (sh, dt) in inproj_outputs(nxt).items()}
        xo_d = p.dram("xo", [NT, D_MODEL], F32, "ExternalOutput")
        if nxt == "even":
            cos_d = p.dram("cosT", [128, NT], F32, "ExternalInput")
            sin_d = p.dram("sinT", [128, NT], F32, "ExternalInput")
    else:
        out_d = p.dram("out", [NT, D_MODEL], F32, "ExternalOutput")

    ident = make_ident(p)
    xg = p.sbuf("xg", [128, TPG, D_MODEL], F32)
    hT = p.sbuf("hT", [128, 8, GT], BF16)
    hs_r = p.ring("hs", 2, [128, D_MODEL], BF16)
    junk = p.sbuf("junk", [128, D_MODEL], F32)
    ss_r = p.ring("ss", 2, [128, 1], F32)
    rs_r = p.ring("rs", 2, [128, 1], F32)
    gB1 = p.sbuf("gB1", [128, D_MODEL], F32)
    p.dma(gB1[:], g1_d[0:1, :].to_broadcast([128, D_MODEL]))
    ps = [p.psum(f"ps{i}", [128, 512], F32) for i in range(8)]
    st_r = p.ring("st", 3, [128, 2, 8, 128], F32)
    wb_r = p.ring("wb", 3, [128, 2, 8, 128], BF16)
    if has_prev:
        gB2 = p.sbuf("gB2", [128, D_MODEL], F32)
        p.dma(gB2[:], g2_d[0:1, :].to_broadcast([128, D_MODEL]))
        wo = p.sbuf("wo_sb", [128, 8, D_MODEL], BF16)
        for c in range(8):
            st = st_r.next()
            stv = st[:].rearrange("p a b c -> p (a b c)")[:, 0:D_MODEL]
            p.dma(stv, wo_d[c])
            p.copy("pool", wo[:, c, :], stv)
        oTg = p.sbuf("oTg", [128, 8, GT], BF16)
        aT = p.sbuf("aT", [128, NFC, GT], BF16)
        sg_r = p.ring("sg", 2, [128, GT], F32)
        wdb_r = p.ring("wdb", 3, [128, D_MODEL], BF16)
    if nxt == "even" and not fused:
        cosg = p.sbuf("cosg", [128, GT], F32)
        sing = p.sbuf("sing", [128, GT], F32)
        wrot_r = p.ring("wrot", 2, [128, 8, 128], BF16)
        t1_r = p.ring("t1", 2, [128, GT], F32)
        t2_r = p.ring("t2", 2, [128, GT], F32)
    if nxt is not None and fused:
        pass
    elif nxt is not None:
        fo_r = p.ring("fo", 3, [128, GT], BF16)
        tmo_r = p.ring("tmo", 2, [128, TPG, 128], BF16)
        tmf_r = p.ring("tmf", 2, [128, TPG, 128], F32)
    else:
        og_r = p.ring("og", 2, [128, D_MODEL], F32)
    psi = [0]

    def next_ps():
        t = ps[psi[0] % 8]
        psi[0] += 1
        return t

    def rmsnorm_to_hT(gB):
        for i in range(TPG):
            ss = ss_r.next()
            rs = rs_r.next()
            hs = hs_r.next()
            p.act(junk[:], xg[:, i, :], AF.Square, accum_out=ss[:])
            p.ts("dve", rs[:], ss[:], 1.0 / D_MODEL, NORM_EPS, ALU.mult, ALU.add)
            p.sqrt(rs[:], rs[:])
            p.recip(rs[:], rs[:])
            p.stt("dve", hs[:], xg[:, i, :], rs[:], gB[:], ALU.mult, ALU.mult)
            pt = next_ps()
            ptv = pt[:].bitcast(BF16).rearrange("p (c t) -> p c t", c=8)
            for c in range(8):
                p.tr(ptv[:, c, :], hs[:, c * 128:(c + 1) * 128], ident[:])
            p.copy("dve", hT[:, :, i * 128:(i + 1) * 128], ptv)

    for g in range(NG):
        t0 = g * GT
        p.dma(xg[:], x_d[t0:t0 + GT, :].rearrange("(i p) d -> p i d", p=128))
        if has_prev:
            if fused:
                p.dma(oTg[:, 0:4, :], oT_d[:, bass.ds(par * NT + t0, GT)].rearrange("(c f) t -> f c t", f=128))
                p.dma(oTg[:, 4:8, :], oTb_d[:, bass.ds(par * NT + t0, GT)].rearrange("(c f) t -> f c t", f=128))
            else:
                p.dma(oTg[:], oT_d[:, t0:t0 + GT].rearrange("(c f) t -> f c t", f=128))
            for i in range(TPG):
                for hf in range(2):
                    acc = next_ps()
                    for c in range(8):
                        p.mm(acc[:], oTg[:, c, i * 128:(i + 1) * 128], wo[:, c, hf * 512:(hf + 1) * 512],
                             c == 0, c == 7)
                    xs = xg[:, i, hf * 512:(hf + 1) * 512]
                    p.tt("dve", xs, xs, acc[:], ALU.add)
            rmsnorm_to_hT(gB2)
            for n in range(NFC):
                st = st_r.next()
                wb = wb_r.next()
                p.dma(st[:], wgu_d[n])
                p.copy("pool", wb[:], st[:])
                G = next_ps()
                U = next_ps()
                for c in range(8):
                    p.mm(G[:], wb[:, 0, c, :], hT[:, c, :], c == 0, c == 7)
                for c in range(8):
                    p.mm(U[:], wb[:, 1, c, :], hT[:, c, :], c == 0, c == 7)
                sg = sg_r.next()
                p.act(sg[:], G[:], AF.Silu)
                p.tt("dve", aT[:, n, :], sg[:], U[:], ALU.mult)
            accs = [next_ps() for _ in range(8)]
            for n in range(NFC):
                st = st_r.next()
                wdb = wdb_r.next()
                stv = st[:].rearrange("p a b c -> p (a b c)")[:, 0:D_MODEL]
                p.dma(stv, wd_d[n])
                p.copy("pool", wdb[:], stv)
                for i in range(TPG):
                    for hf in range(2):
                        p.mm(accs[i * 2 + hf][:], aT[:, n, i * 128:(i + 1) * 128],
                             wdb[:, hf * 512:(hf + 1) * 512], n == 0, n == NFC - 1)
            for i in range(TPG):
                for hf in range(2):
                    xs = xg[:, i, hf * 512:(hf + 1) * 512]
                    p.tt("dve", xs, xs, accs[i * 2 + hf][:], ALU.add)
        if nxt is None:
            for i in range(TPG):
                ss = ss_r.next()
                rs = rs_r.next()
                og = og_r.next()
                p.act(junk[:], xg[:, i, :], AF.Square, accum_out=ss[:])
                p.ts("dve", rs[:], ss[:], 1.0 / D_MODEL, NORM_EPS, ALU.mult, ALU.add)
                p.sqrt(rs[:], rs[:])
                p.recip(rs[:], rs[:])
                p.stt("dve", og[:], xg[:, i, :], rs[:], gB1[:], ALU.mult, ALU.mult)
                p.dma(out_d[t0 + i * 128:t0 + (i + 1) * 128, :], og[:], eng="pool")
            continue
        p.dma(xo_d[t0:t0 + GT, :].rearrange("(i p) d -> p i d", p=128), xg[:], eng="pool")
        rmsnorm_to_hT(gB1)
        if fused:
            p.dma(hs_d[:, :, t0:t0 + GT], hT[:, 0:4, :], eng="pool")
            p.dma(hsb_d[:, :, t0:t0 + GT], hT[:, 4:8, :], eng="pool")
            continue
        if nxt == "even":
            p.dma(cosg[:], cos_d[:, t0:t0 + GT])
            p.dma(sing[:], sin_d[:, t0:t0 + GT])
        for ci, (kind, scale, dname, dj) in enumerate(chunks):
            st = st_r.next()
            wb = wb_r.next()
            stv = st[:, 0]
            wbv = wb[:, 0]
            p.dma(stv, win_d[ci])
            if scale == 1.0:
                p.copy("pool", wbv, stv)
            else:
                p.ts("pool", wbv, stv, scale, None, ALU.mult)
            dst = outs[dname]
            if kind in ("fm", "rope"):
                A = next_ps()
                for c in range(8):
                    p.mm(A[:], wbv[:, c, :], hT[:, c, :], c == 0, c == 7)
                fo = fo_r.next()
                if kind == "rope":
                    wr = wrot_r.next()
                    sv = stv.rearrange("p c (h f d) -> p (c h) f d", h=2, f=2, d=32)
                    wv = wr[:].rearrange("p c (h f d) -> p (c h) f d", h=2, f=2, d=32)
                    p.ts("pool", wv[:, :, 0, :], sv[:, :, 1, :], -scale, None, ALU.mult)
                    p.ts("pool", wv[:, :, 1, :], sv[:, :, 0, :], scale, None, ALU.mult)
                    Bp = next_ps()
                    for c in range(8):
                        p.mm(Bp[:], wr[:, c, :], hT[:, c, :], c == 0, c == 7)
                    t1 = t1_r.next()
                    t2 = t2_r.next()
                    p.tt("dve", t1[:], A[:], cosg[:], ALU.mult)
                    p.tt("dve", t2[:], Bp[:], sing[:], ALU.mult)
                    p.tt("pool", fo[:], t1[:], t2[:], ALU.add)
                else:
                    p.copy("act", fo[:], A[:])
                dv_ = dst[2 * dj:2 * dj + 2, :, t0:t0 + GT].rearrange("h d t -> (h d) t")
                p.dma(dv_, fo[:], eng="pool")
            else:
                A = next_ps()
                Av = A[:].rearrange("p (i n) -> p i n", i=TPG)
                for i in range(TPG):
                    for c in range(8):
                        p.mm(Av[:, i, :], hT[:, c, i * 128:(i + 1) * 128], wbv[:, c, :], c == 0, c == 7)
                if kind == "tm":
                    to = tmo_r.next()
                else:
                    to = tmf_r.next()
                p.copy("act", to[:], Av)
                dv_ = dst[t0:t0 + GT, dj * 128:(dj + 1) * 128].rearrange("(i p) n -> p i n", p=128)
                p.dma(dv_, to[:], eng="pool")
    return p


def lay_wo(w):
    return np.ascontiguousarray(w.reshape(8, 128, D_MODEL))


def lay_wgu(wg, wu):
    a = np.stack([wg, wu], 0).reshape(2, 8, 128, NFC, 128)
    return np.ascontiguousarray(a.transpose(3, 2, 0, 1, 4))


def lay_wd(wd):
    return np.ascontiguousarray(wd.reshape(NFC, 128, D_MODEL))


def lay_win_even(w):
    cols = [w[:, 0:512], w[:, 512:640], w[:, 640:768], w[:, 768:896], w[:, 896:1024], w[:, 1024:1152],
            w[:, 1152:1280], np.pad(w[:, 1280:1304], ((0, 0), (0, 104))), w[:, 1304:1816], w[:, 1816:2328],
            w[:, 2328:2840]]
    wp = np.concatenate(cols, 1)
    n = wp.shape[1] // 128
    return np.ascontiguousarray(wp.reshape(8, 128, n, 128).transpose(2, 1, 0, 3))


def lay_win_odd(w):
    return np.ascontiguousarray(w.reshape(8, 128, 24, 128).transpose(2, 1, 0, 3))


def rope_tables_T(pos):
    inv = 1.0 / (10000.0 ** (np.arange(0, HD, 2, dtype=np.float32) / HD))
    ang = pos.astype(np.float32)[None, :] * inv.astype(np.float32)[:, None]
    c = np.cos(ang).astype(np.float32)
    s = np.sin(ang).astype(np.float32)
    return np.ascontiguousarray(np.tile(c, (4, 1))), np.ascontiguousarray(np.tile(s, (4, 1)))


def build_att_odd(S=SEQ, n_hg=2, p=None, mid_hook=None):
    if p is None:
        p = Prog()
        p.enable_arena(200 * 1024)
    NH = n_hg * 4
    NI = S // 512
    NKB = S // 128
    qT_d = p.dram("qT", [NH, 64, S], BF16, "ExternalInput")
    kT_d = p.dram("kT", [NH, 64, S], BF16, "ExternalInput")
    v_d = p.dram("v", [128, NKB, NH * 64], BF16, "ExternalInput")
    oT_d = p.dram("oT", [NH * 64, S], BF16, "ExternalOutput")
    ident = make_ident(p)
    U = p.sbuf("Uincl", [128, 128], BF16)
    L = p.sbuf("Lstr", [128, 128], BF16)
    p.memset("pool", U[:], 1.0)
    p.asel(U[:], U[:], [[-1, 128]], ALU.is_ge, 0.0, 0, 1)
    p.memset("pool", L[:], 1.0)
    p.asel(L[:], L[:], [[1, 128]], ALU.is_ge, 0.0, -1, -1)
    qT = p.sbuf("qT_sb", [64, 4, S], BF16)
    kT = p.sbuf("kT_sb", [64, 4, S], BF16)
    v_sb = p.sbuf("v_sb", [128, NKB, NH * 64], BF16)
    for q4 in range(4):
        a, b = q4 * NKB // 4, (q4 + 1) * NKB // 4
        p.dma(v_sb[:, a:b, :], v_d[:, a:b, :])
    Z_r = p.ring("Z", 4, [128, 512], F32, "psum")
    A_r = p.ring("A", 2, [128, 512], F32, "psum")
    O_r = p.ring("O", 2, [128, 512], F32, "psum")
    E_r = p.ring("E", 3, [128, 1024], F32)
    SP_r = p.ring("SP", 3, [128, 1024], BF16)
    G_r = p.ring("G", 3, [128, 1024], F32)
    P_r = p.ring("P", 3, [128, 1024], BF16)
    oT_r = p.ring("oTs", 4, [64, 512], BF16)
    units = []
    for hg in range(n_hg):
        for I in range(NI):
            kb_hi = 4 * I + 3
            for hp in range(2):
                st = {}
                for kb in range(kb_hi, -1, -1):
                    units.append({"st": st, "hg": hg, "I": I, "hp": hp, "kb": kb, "kb_hi": kb_hi,
                                  "heads": (2 * hp, 2 * hp + 1),
                                  "load": (I == 0 and hp == 0 and kb == kb_hi)})

    def c0_of(u):
        return max(0, 128 * (u["kb"] - 4 * u["I"]))

    def cv(v, c0):
        return v.rearrange("p (h q) -> p h q", h=2)[:, :, c0:512]

    def fA(u):
        if u["load"]:
            for j in range(4):
                p.dma(qT[:, j, :], qT_d[u["hg"] * 4 + j])
                p.dma(kT[:, j, :], kT_d[u["hg"] * 4 + j])
        kb, I = u["kb"], u["I"]
        c0 = c0_of(u)
        u["Z"] = {}
        for h in u["heads"]:
            u["Z"][h] = Z_r.next()
            p.mm(u["Z"][h][:, c0:512], kT[:, h, kb * 128:(kb + 1) * 128], qT[:, h, I * 512 + c0:(I + 1) * 512],
                 True, True)

    def fE(u):
        c0 = c0_of(u)
        u["E2"] = E_r.next()
        Zp, Zo = p.pair(u["Z"][u["heads"][0]])
        p.act(cv(u["E2"][:], c0), cv(Zp, c0), AF.Exp, extra_r=[Zo])

    def fSPU(u):
        kb, I, kb_hi, heads = u["kb"], u["I"], u["kb_hi"], u["heads"]
        c0 = c0_of(u)
        if kb == kb_hi:
            u["st"]["A"] = {h: A_r.next() for h in heads}
        Ah = u["st"]["A"]
        u["SP2"] = SP_r.next()
        SP2 = u["SP2"]
        p.act(cv(SP2[:], c0), cv(u["E2"][:], c0), AF.Ln, bias=1.0)
        if kb >= 4 * I:
            p.asel(cv(SP2[:], c0), cv(SP2[:], c0), [[0, 2], [1, 512 - c0]], ALU.is_ge, 0.0, -1, -1)
        for j, h in enumerate(heads):
            p.mm(Ah[h][:, c0:512], U[:], SP2[:, j * 512 + c0:(j + 1) * 512], kb == kb_hi, False)

    def fRest(u):
        kb, I, kb_hi, heads, hg, hp = u["kb"], u["I"], u["kb_hi"], u["heads"], u["hg"], u["hp"]
        c0 = c0_of(u)
        Ah = u["st"]["A"]
        SP2, E2 = u["SP2"], u["E2"]
        G2, P2 = G_r.next(), P_r.next()
        Ap, Ao = p.pair(Ah[heads[0]])
        p.act(cv(G2[:], c0), cv(Ap, c0), AF.Exp, scale=-1.0, extra_r=[Ao])
        for j, h in enumerate(heads):
            p.mm(Ah[h][:, c0:512], L[:], SP2[:, j * 512 + c0:(j + 1) * 512], False, kb == 0)
        p.tt("dve", cv(P2[:], c0), cv(E2[:], c0), cv(G2[:], c0), ALU.mult)
        if kb >= 4 * I:
            p.asel(cv(P2[:], c0), cv(P2[:], c0), [[0, 2], [1, 512 - c0]], ALU.is_ge, 0.0, -1, -1)
        u["P2"] = P2

    def fPV(u):
        kb, I, kb_hi, heads, hg, hp = u["kb"], u["I"], u["kb_hi"], u["heads"], u["hg"], u["hp"]
        c0 = c0_of(u)
        P2 = u["P2"]
        if kb == kb_hi:
            u["st"]["O"] = {h: O_r.next() for h in heads}
        Oh = u["st"]["O"]
        for j, h in enumerate(heads):
            p.mm(Oh[h][0:64, c0:512], v_sb[:, kb, (hg * 4 + h) * 64:(hg * 4 + h + 1) * 64],
                 P2[:, j * 512 + c0:(j + 1) * 512], kb == kb_hi, kb == 0)
        if kb == 0:
            for h in heads:
                ot = oT_r.next()
                p.copy("dve", ot[:], Oh[h][0:64, :])
                r0 = (hg * 4 + h) * 64
                p.dma(oT_d[r0:r0 + 64, I * 512:(I + 1) * 512], ot[:], eng="pool")
            if mid_hook is not None and hg == 0 and hp == 1 and I == NI - 1 and n_hg > 1:
                mid_hook()

    n = len(units)
    fA(units[0])
    if n > 1:
        fA(units[1])
    fE(units[0])
    for k in range(n):
        fSPU(units[k])
        if k + 2 < n:
            fA(units[k + 2])
        if k + 1 < n:
            fE(units[k + 1])
        fRest(units[k])
        if k >= 1:
            fPV(units[k - 1])
    fPV(units[n - 1])
    return p


NEGM = 30000.0


def build_att_even(S=SEQ, do_nsa=True, do_diff=True, p=None, mid_hook=None):
    if p is None:
        p = Prog()
        p.enable_arena(200 * 1024)
    NQ = S // 128
    NI = S // 512
    NSL = S // 64
    NCMP = (S - 32) // 16 + 1
    NCC = -(-NCMP // 128)
    nqT_d = p.dram("nqT", [4, 64, S], BF16, "ExternalInput")
    kcT_d = p.dram("kcT", [64, S], BF16, "ExternalInput")
    vcT_d = p.dram("vcT", [64, S], BF16, "ExternalInput")
    ksT_d = p.dram("ksT", [64, S], BF16, "ExternalInput")
    kwT_d = p.dram("kwT", [64, S], BF16, "ExternalInput")
    vs_d = p.dram("vs", [128, NQ, 64], BF16, "ExternalInput")
    vw_d = p.dram("vw", [128, NQ, 64], BF16, "ExternalInput")
    gl_d = p.dram("gl", [128, NQ, 12], F32, "ExternalInput")
    wck_d = p.dram("wck", [64, 32, 64], F32, "ExternalInput")
    wcv_d = p.dram("wcv", [64, 32, 64], F32, "ExternalInput")
    pk_d = p.dram("pkT", [64, 32], F32, "ExternalInput")
    pv_d = p.dram("pvT", [64, 32], F32, "ExternalInput")
    ftab_d = p.dram("ftab", [128, NQ, NSL], F32, "ExternalInput")
    dqT_d = p.dram("dqT", [4, 64, S], BF16, "ExternalInput")
    dkT_d = p.dram("dkT", [4, 64, S], BF16, "ExternalInput")
    dv_d = p.dram("dv", [128, NQ, 256], BF16, "ExternalInput")
    lqk_d = p.dram("lqk", [1, 256], F32, "ExternalInput")
    li_d = p.dram("lami", [1, 2], F32, "ExternalInput")
    sub_d = p.dram("subln", [1, 128], F32, "ExternalInput")
    oT_d = p.dram("oT", [512, S], BF16, "ExternalOutput")

    ident = make_ident(p)
    ps = [p.psum(f"ps{i}", [128, 512], F32) for i in range(8)]
    bigA = p.sbuf("bigA", [64, 4, S], BF16)
    bigB = p.sbuf("bigB", [64, 4, S], BF16)
    E_r = p.ring("E", 4, [128, 512], BF16)
    oTs_r = p.ring("oTs", 2, [128, 2, 512], BF16)
    vstg = p.sbuf("vstg", [128, NQ, 256], BF16)

    if do_nsa:
        Zr = Ring(ps[0:4])
        OC, OS, OW, TP = ps[4], ps[5], ps[6], ps[7]
        OCv = OC[:].rearrange("p (h c) -> p h c", h=4)
        OSv = OS[:, 0:260].rearrange("p (h c) -> p h c", h=4)
        OWv = OW[:, 0:260].rearrange("p (h c) -> p h c", h=4)
        for j in range(4):
            p.dma(bigA[:, j, :], nqT_d[j])
        p.dma(bigB[:, 0, :], kcT_d[:, :])
        p.dma(bigB[:, 1, :], vcT_d[:, :])
        p.dma(bigB[:, 2, :], ksT_d[:, :])
        p.dma(bigB[:, 3, :], kwT_d[:, :])
        kcT, vcT, ksT, kwT = (bigB[:, j, :] for j in range(4))
        vs1 = p.sbuf("vs1", [128, NQ, 65], BF16)
        vw1 = p.sbuf("vw1", [128, NQ, 65], BF16)
        p.memset("pool", vs1[:, :, 64:65], 1.0)
        p.memset("pool", vw1[:, :, 64:65], 1.0)
        p.dma(vstg[:, :, 0:64], vs_d[:, :, :])
        p.dma(vstg[:, :, 64:128], vw_d[:, :, :])
        p.copy("act", vs1[:, :, 0:64], vstg[:, :, 0:64])
        p.copy("act", vw1[:, :, 0:64], vstg[:, :, 64:128])
        gates = p.sbuf("gates", [128, NQ, 12], F32)
        p.dma(gates[:], gl_d[:, :, :])
        p.act(gates[:], gates[:], AF.Exp, scale=-1.0)
        p.ts("dve", gates[:], gates[:], 1.0, None, ALU.add)
        p.recip(gates[:], gates[:])
        ftab = p.sbuf("ftab_sb", [128, NQ, NSL], F32)
        p.dma(ftab[:], ftab_d[:, :, :])
        EE = p.sbuf("EE", [64, S], BF16)
        p.memset("pool", EE[:], 1.0)
        p.asel(EE[:], EE[:], [[1, S]], ALU.is_ge, 0.0, 0, -64)
        p.asel(EE[:], EE[:], [[-1, S]], ALU.is_ge, 0.0, 63, 64)
        wst = p.sbuf("wst", [64, 32, 64], F32)
        wck = p.sbuf("wck_sb", [64, 32, 64], BF16)
        wcv = p.sbuf("wcv_sb", [64, 32, 64], BF16)
        pst = p.sbuf("pst", [64, 32], F32)
        pkT = p.sbuf("pkT_sb", [64, 32], BF16)
        pvTb = p.sbuf("pvTb", [64, 32, 128], BF16)
        p.dma(wst[:], wck_d[:, :, :])
        p.copy("dve", wck[:], wst[:])
        p.dma(wst[:], wcv_d[:, :, :])
        p.copy("dve", wcv[:], wst[:])
        p.dma(pst[:], pk_d[:, :])
        p.copy("dve", pkT[:], pst[:])
        p.dma(pst[:], pv_d[:, :])
        p.copy("dve", pvTb[:], pst[:].unsqueeze(2).to_broadcast([64, 32, 128]))
        kcmpT = p.sbuf("kcmpT", [64, NCC * 128], BF16)
        VC = p.sbuf("VC", [128, NCC, 128], BF16)
        ck = p.sbuf("ck", [64, 1], F32)
        p.memset("pool", kcmpT[:], 0.0)
        zk = ps[0]
        zc = ps[1]
        for j in range(32):
            p.mm(zk[0:64, 0:NCMP], wck[:, j, :], kcT[:, j:j + 16 * (NCMP - 1) + 1:16], j == 0, j == 31)
        for j in range(32):
            p.mm(zc[0:64, 0:1], wck[:, j, :], pkT[:, j:j + 1], j == 0, j == 31)
        p.copy("dve", ck[:], zc[0:64, 0:1])
        p.ts("dve", kcmpT[:, 0:NCMP], zk[0:64, 0:NCMP], ck[:], None, ALU.add)
        cmask = p.sbuf("cmask", [128, NCC, 64], F32)
        p.memset("pool", VC[:], 0.0)
        p.memset("pool", cmask[:], 1.0)
        for c in range(NCC):
            m = min(128, NCMP - 128 * c)
            zv = ps[2 + c % 2]
            for j in range(32):
                n0 = 16 * 128 * c + j
                p.mm(zv[0:m, 0:64], vcT[:, n0:n0 + 16 * (m - 1) + 1:16], wcv[:, j, :], j == 0, False)
            for j in range(32):
                p.mm(zv[0:m, 0:64], pvTb[:, j, 0:m], wcv[:, j, :], False, j == 31)
            p.copy("dve", VC[0:m, c, 0:64], zv[0:m, 0:64])
            p.asel(cmask[:, c, :], cmask[:, c, :], [[-4, 64]], ALU.is_ge, 0.0, 128 * c + 1, 1)
            p.asel(cmask[:, c, :], cmask[:, c, :], [[4, 64]], ALU.is_ge, 0.0, 3 - 128 * c, -1)
            p.copy("pool", VC[0:m, c, 64:128], cmask[0:m, c, :])
            p.memset("pool", VC[0:m, c, 64:65], 1.0)
        onsa_r = p.ring("onsa", 2, [128, 4, 64], F32)
        onb_r = p.ring("onb", 2, [128, 256], BF16)
        rz_r = p.ring("rz", 3, [128, 4], F32)
        gco_r = p.ring("gco", 3, [128, 4], F32)
        tmp_r = p.ring("tmpn", 2, [128, 4, 64], F32)
        imp_r = p.ring("imp", 2, [128, 64], F32)
        wk_r = p.ring("impw", 2, [128, 64], F32)
        m8_r = p.ring("m8", 2, [128, 16], F32)
        thr_r = p.ring("thr", 2, [128, 1], F32)
        neg_r = p.ring("neg", 2, [128, 64], BF16)
        negT_r = p.ring("negT", 2, [64, 128], BF16)
        units = []
        tile_state = {}
        E2n_r = p.ring("E2n", 3, [128, 1024], BF16)

        def pre_sel_fn(stt_):
            negT = negT_r.next()
            stt_["negT"] = negT
            TPb = TP[:].bitcast(BF16)
            p.tr(TPb[0:64, 0:128], stt_["neg"][:], ident[:])
            p.copy("dve", negT[:], TPb[0:64, 0:128])

        for i in range(NQ):
            qv = bigA[:, :, i * 128:(i + 1) * 128]
            stt_ = {"onsa": None}
            tile_state[i] = stt_
            if i % 4 == 0:
                oTs_cur = oTs_r.next()
            stt_["oTs"] = oTs_cur

            def mk_unit(kind, kb, i=i, qv=qv, stt_=stt_, first=False, last=False, c=None):
                u = {}

                def A(u=u):
                    Z = Zr.next()
                    u["Z"] = Z
                    if kind == "cmp":
                        p.mm(Z[:], kcmpT[:, c * 128:(c + 1) * 128], qv, True, True)
                    elif kind == "win":
                        p.mm(Z[:], kwT[:, kb * 128:(kb + 1) * 128], qv, True, True)
                    else:
                        if first:
                            pre_sel_fn(stt_)
                        p.mm(Z[:], ksT[:, kb * 128:(kb + 1) * 128], qv, True, False)
                        p.mm(Z[:], EE[:, kb * 128:(kb + 1) * 128],
                             stt_["negT"][:].unsqueeze(1).to_broadcast([64, 4, 128]), False, True)

                def B(u=u):
                    E = E_r.next()
                    u["E"] = E
                    p.act(E[:], u["Z"][:], AF.Exp)
                    if kind == "cmp":
                        if not (128 * c + 127 <= 8 * i - 2):
                            p.asel(E[:], E[:], [[0, 4], [1, 128]], ALU.is_ge, 0.0, 128 * i - 31 - 2048 * c, -16)
                    else:
                        if kb == i:
                            p.asel(E[:], E[:], [[0, 4], [1, 128]], ALU.is_ge, 0.0, 0, -1)
                        if kind == "win" and kb == i - 4:
                            p.asel(E[:], E[:], [[0, 4], [-1, 128]], ALU.is_ge, 0.0, -1, 1)

                def C(u=u):
                    E = u["E"]
                    for h in range(4):
                        if kind == "cmp":
                            p.mm(OCv[:, h, :], E[:, h * 128:(h + 1) * 128], VC[:, c, :], first and h == 0,
                                 last and h == 3)
                        elif kind == "win":
                            p.mm(OWv[:, h, :], E[:, h * 128:(h + 1) * 128], vw1[:, kb, :], first and h == 0,
                                 last and h == 3)
                        else:
                            p.mm(OSv[:, h, :], E[:, h * 128:(h + 1) * 128], vs1[:, kb, :], first and h == 0,
                                 last and h == 3)
                u["A"], u["B"], u["C"] = A, B, C
                return u

            ccs = [c for c in range(NCC) if 2048 * c + 31 <= 128 * i + 127]
            cu = [mk_unit("cmp", None, first=(c == ccs[0]), last=(c == ccs[-1]), c=c) for c in ccs]

            def post_cmp(i=i, stt_=stt_):
                prev = tile_state.get(i - 1)
                if prev is not None and "flush" in prev:
                    prev.pop("flush")()
                onsa = onsa_r.next()
                stt_["onsa"] = onsa
                rz = rz_r.next()
                gco = gco_r.next()
                p.ts("dve", rz[:], OCv[:, :, 64], 1e-30, None, ALU.max)
                p.recip(rz[:], rz[:])
                p.tt("dve", gco[:], rz[:], gates[:, i, 0:12:3], ALU.mult)
                p.tt("dve", onsa[:], OCv[:, :, 0:64], gco[:].unsqueeze(2).to_broadcast([128, 4, 64]), ALU.mult)
                tmp = tmp_r.next()
                imp = imp_r.next()
                wk = wk_r.next()
                m8 = m8_r.next()
                thr = thr_r.next()
                neg = neg_r.next()
                stt_["neg"] = neg
                p.tt("dve", tmp[:], OCv[:, :, 64:128], rz[:].unsqueeze(2).to_broadcast([128, 4, 64]), ALU.mult)
                p.reduce("dve", imp[:, 0:64], tmp[:].rearrange("p h j -> p j h"), ALU.add)
                p.tt("dve", imp[:, 0:NSL], imp[:, 0:NSL], ftab[:, i, :], ALU.add)
                p.max8(m8[:, 0:8], imp[:, 0:NSL])
                p.match_replace(wk[:, 0:NSL], m8[:, 0:8], imp[:, 0:NSL], -3.0e38)
                p.max8(m8[:, 8:16], wk[:, 0:NSL])
                p.reduce("dve", thr[:], m8[:, 8:16], ALU.min)
                p.memset("pool", neg[:], 0.0)
                p.ts("dve", wk[:, 0:NSL], imp[:, 0:NSL], thr[:], None, ALU.is_ge)
                p.ts("dve", neg[:, 0:NSL], wk[:, 0:NSL], NEGM, -NEGM, ALU.mult, ALU.add)
            cu[-1]["post"] = post_cmp
            kbs = list(range(max(0, i - 4), i + 1))
            def mk_win_pair(kb0, i=i, qv=qv, kbs=kbs):
                u = {}

                def A(u=u):
                    if Zr.i % 2:
                        Zr.next()
                    u["Z"] = [Zr.next(), Zr.next()]
                    for b_ in range(2):
                        kb = kb0 + b_
                        p.mm(u["Z"][b_][:], kwT[:, kb * 128:(kb + 1) * 128], qv, True, True)

                def B(u=u):
                    E2 = E2n_r.next()
                    u["E"] = E2
                    Zp, Zo = p.pair(u["Z"][0])
                    p.act(E2[:], Zp, AF.Exp, extra_r=[Zo])
                    for b_ in range(2):
                        kb = kb0 + b_
                        Eh = E2[:, b_ * 512:(b_ + 1) * 512]
                        if kb == i:
                            p.asel(Eh, Eh, [[0, 4], [1, 128]], ALU.is_ge, 0.0, 0, -1)
                        if kb == i - 4:
                            p.asel(Eh, Eh, [[0, 4], [-1, 128]], ALU.is_ge, 0.0, -1, 1)

                def C(u=u):
                    for b_ in range(2):
                        kb = kb0 + b_
                        for h in range(4):
                            p.mm(OWv[:, h, :], u["E"][:, b_ * 512 + h * 128:b_ * 512 + (h + 1) * 128],
                                 vw1[:, kb, :], kb == kbs[0] and h == 0, kb == i and h == 3)
                u["A"], u["B"], u["C"] = A, B, C
                return u

            wu = [mk_win_pair(kbs[j]) for j in range(0, len(kbs) - 1, 2)]
            if len(kbs) % 2 == 1:
                wu.append(mk_unit("win", kbs[-1], first=(len(kbs) == 1), last=True))
            def mk_sel_pair(kb0, i=i, qv=qv, stt_=stt_):
                u = {}
                first, last = (kb0 == 0), (kb0 + 1 == i)

                def A(u=u):
                    if first:
                        pre_sel_fn(stt_)
                    if Zr.i % 2:
                        Zr.next()
                    u["Z"] = [Zr.next(), Zr.next()]
                    for b_ in range(2):
                        kb = kb0 + b_
                        p.mm(u["Z"][b_][:], ksT[:, kb * 128:(kb + 1) * 128], qv, True, False)
                        p.mm(u["Z"][b_][:], EE[:, kb * 128:(kb + 1) * 128],
                             stt_["negT"][:].unsqueeze(1).to_broadcast([64, 4, 128]), False, True)

                def B(u=u):
                    E2 = E2n_r.next()
                    u["E"] = E2
                    Zp, Zo = p.pair(u["Z"][0])
                    p.act(E2[:], Zp, AF.Exp, extra_r=[Zo])
                    if last:
                        p.asel(E2[:, 512:1024], E2[:, 512:1024], [[0, 4], [1, 128]], ALU.is_ge, 0.0, 0, -1)

                def C(u=u):
                    for b_ in range(2):
                        kb = kb0 + b_
                        for h in range(4):
                            p.mm(OSv[:, h, :], u["E"][:, b_ * 512 + h * 128:b_ * 512 + (h + 1) * 128],
                                 vs1[:, kb, :], first and b_ == 0 and h == 0, last and b_ == 1 and h == 3)
                u["A"], u["B"], u["C"] = A, B, C
                return u

            su = [mk_sel_pair(kb0) for kb0 in range(0, i, 2)]
            if (i + 1) % 2 == 1:
                su.append(mk_unit("sel", i, first=(i == 0), last=True))


            def post_sel(i=i, stt_=stt_):
                onsa = stt_["onsa"]
                for (Ov, gi) in ((OWv, 2), (OSv, 1)):
                    rz2 = rz_r.next()
                    gc2 = gco_r.next()
                    tmp2 = tmp_r.next()
                    p.recip(rz2[:], Ov[:, :, 64])
                    p.tt("dve", gc2[:], rz2[:], gates[:, i, gi:12:3], ALU.mult)
                    p.tt("dve", tmp2[:], Ov[:, :, 0:64], gc2[:].unsqueeze(2).to_broadcast([128, 4, 64]), ALU.mult)
                    p.tt("pool", onsa[:], onsa[:], tmp2[:], ALU.add)
                onb = onb_r.next()
                p.copy("act", onb[:], onsa[:].rearrange("p h d -> p (h d)"))

                def flush(i=i, onb=onb, oTs=stt_["oTs"]):
                    TPb = TP[:].bitcast(BF16)
                    for c2 in range(2):
                        p.tr(TPb[:, 256 + c2 * 128:256 + (c2 + 1) * 128], onb[:, c2 * 128:(c2 + 1) * 128], ident[:])
                    p.copy("dve", oTs[:, :, (i % 4) * 128:(i % 4 + 1) * 128],
                           TPb[:, 256:512].rearrange("p (c q) -> p c q", c=2))
                    if i % 4 == 3:
                        i0 = (i // 4) * 512
                        p.dma(oT_d[0:256, i0:i0 + 512].rearrange("(c f) q -> f c q", c=2), oTs[:], eng="pool")
                stt_["flush"] = flush
            su[-1]["post"] = post_sel
            units += cu + wu + su
        run_pipeline(units, 1)
        tile_state[NQ - 1].pop("flush")()

    if mid_hook is not None:
        mid_hook()
    if do_diff:
        Zr = Ring(ps[0:4])
        TP = ps[3]
        E2_r = p.ring("E2d", 3, [128, 1024], BF16)
        TPb = TP[:].bitcast(BF16)
        OD = [[ps[4], ps[5]], [ps[6], ps[7]]]

        def odv(m, sub):
            return OD[m][sub // 2][:, 0:258].rearrange("p (s c) -> p s c", s=2)[:, sub % 2, :]

        for j in range(4):
            p.dma(bigA[:, j, :], dqT_d[j])
            p.dma(bigB[:, j, :], dkT_d[j])
        dv1 = p.sbuf("dv1", [128, NQ, 2, 129], BF16)
        p.memset("pool", dv1[:, :, :, 128:129], 1.0)
        for q4 in range(4):
            a, b = q4 * NQ // 4, (q4 + 1) * NQ // 4
            p.dma(vstg[:, a:b, :], dv_d[:, a:b, :])
        p.copy("dve", dv1[:, :, :, 0:128], vstg[:].rearrange("p k (h d) -> p k h d", h=2))
        lqk = p.sbuf("lqk_sb", [128, 4, 64], F32)
        li = p.sbuf("li_sb", [128, 2], F32)
        sB = p.sbuf("sublnB", [128, 128], F32)
        p.dma(lqk[:].rearrange("p a d -> p (a d)"), lqk_d[0:1, :].to_broadcast([128, 256]))
        p.dma(li[:], li_d[0:1, :].to_broadcast([128, 2]))
        p.dma(sB[:], sub_d[0:1, :].to_broadcast([128, 128]))
        lpr = p.sbuf("lpr", [128, 2, 64], F32)
        lsum = p.sbuf("lsum", [128, 2], F32)
        nlam = p.sbuf("nlam", [128, 1], F32)
        p.tt("dve", lpr[:], lqk[:, 0:4:2, :], lqk[:, 1:4:2, :], ALU.mult)
        p.reduce("dve", lsum[:], lpr[:], ALU.add)
        p.act(lsum[:], lsum[:], AF.Exp)
        p.tt("dve", nlam[:], lsum[:, 1:2], lsum[:, 0:1], ALU.subtract)
        p.tt("dve", nlam[:], nlam[:], li[:, 0:1], ALU.subtract)
        p.ts("dve", sB[:], sB[:], li[:, 1:2], None, ALU.mult)
        od_r = p.ring("od", 2, [128, 128], F32)
        od2_r = p.ring("od2", 2, [128, 128], F32)
        odb_r = p.ring("odb", 2, [128, 128], BF16)
        rzd_r = p.ring("rzd", 4, [128, 1], F32)
        ssd_r = p.ring("ssd", 2, [128, 1], F32)
        jk = p.sbuf("jkd", [128, 128], F32)
        oTd_r = p.ring("oTd", 2, [128, 512], BF16)
        units = []
        for hd in range(2):
            for I in range(NI):
                nkb = 4 * I + 4
                for kb in range(nkb):
                    if True:
                        m = 1
                        u = {}
                        c0 = max(0, 128 * (kb - 4 * I))

                        def cvd(v, c0=c0):
                            return v.rearrange("p (h q) -> p h q", h=2)[:, :, c0:512]

                        def A(u=u, hd=hd, I=I, kb=kb, c0=c0):
                            u["Z"] = [Zr.next(), Zr.next()]
                            for m_ in range(2):
                                p.mm(u["Z"][m_][:, c0:512], bigB[:, hd * 2 + m_, kb * 128:(kb + 1) * 128],
                                     bigA[:, hd * 2 + m_, I * 512 + c0:(I + 1) * 512], True, True)

                        def B(u=u, I=I, kb=kb, c0=c0, cvd=cvd):
                            E2 = E2_r.next()
                            u["E"] = E2
                            Zp, Zo = p.pair(u["Z"][0])
                            p.act(cvd(E2[:]), cvd(Zp), AF.Exp, extra_r=[Zo])
                            if kb >= 4 * I:
                                p.asel(cvd(E2[:]), cvd(E2[:]), [[0, 2], [1, 512 - c0]], ALU.is_ge, 0.0, 0, -1)

                        def C(u=u, hd=hd, kb=kb, nkb=nkb, c0=c0):
                            for m_ in range(2):
                                for sub in range(c0 // 128, 4):
                                    p.mm(odv(m_, sub), u["E"][:, m_ * 512 + sub * 128:m_ * 512 + (sub + 1) * 128],
                                         dv1[:, kb, hd, :], kb == 0 and sub % 2 == 0,
                                         (sub == 1 and kb == nkb - 3) or (sub == 3 and kb == nkb - 1))
                        u["A"], u["B"], u["C"] = A, B, C

                        def post(hd=hd, I=I):
                            oTd = oTd_r.next()
                            for sub in range(4):
                                r1 = rzd_r.next()
                                r2 = rzd_r.next()
                                od = od_r.next()
                                od2 = od2_r.next()
                                odb = odb_r.next()
                                ss = ssd_r.next()
                                p.recip(r1[:], odv(0, sub)[:, 128:129])
                                p.recip(r2[:], odv(1, sub)[:, 128:129])
                                p.tt("dve", r2[:], r2[:], nlam[:], ALU.mult)
                                p.ts("dve", od[:], odv(0, sub)[:, 0:128], r1[:], None, ALU.mult)
                                p.stt("dve", od2[:], odv(1, sub)[:, 0:128], r2[:], od[:], ALU.mult, ALU.add)
                                p.act(jk[:], od2[:], AF.Square, accum_out=ss[:])
                                p.ts("dve", ss[:], ss[:], 1.0 / 128, NORM_EPS, ALU.mult, ALU.add)
                                p.act(ss[:], ss[:], AF.Ln)
                                p.act(ss[:], ss[:], AF.Exp, scale=-0.5)
                                p.stt("dve", odb[:], od2[:], ss[:], sB[:], ALU.mult, ALU.mult)
                                p.tr(TPb[:, sub * 128:(sub + 1) * 128], odb[:], ident[:])
                            p.copy("dve", oTd[:], TPb[:, 0:512])
                            p.dma(oT_d[256 + hd * 128:256 + (hd + 1) * 128, I * 512:(I + 1) * 512], oTd[:],
                                  eng="pool")
                        if kb == nkb - 1 and m == 1:
                            u["post"] = post
                        units.append(u)
        run_pipeline(units, 1)
    return p


_PROGS = {}


def _prog(key, fn):
    if key not in _PROGS:
        _PROGS[key] = fn().build()
    return _PROGS[key]


def make_ftab(S):
    NQ = S // 128
    NSL = S // 64
    t = np.arange(S)
    qblk = t // 64
    jb = np.arange(NSL)
    F = np.zeros((S, NSL), np.float32)
    F[jb[None, :] > qblk[:, None]] = -1e30
    F[np.arange(S), qblk] = 3e30
    m = qblk >= 1
    F[np.arange(S)[m], qblk[m] - 1] = 2e30
    F[:, 0] = 1e30
    return np.ascontiguousarray(F.reshape(NQ, 128, NSL).transpose(1, 0, 2))


def _pm(a):
    S = a.shape[0]
    return np.ascontiguousarray(a.reshape(S // 128, 128, -1).transpose(1, 0, 2))


def _run(nc, in_maps):
    res = run_bass_kernel_spmd(nc, in_maps, core_ids=list(range(8)))
    return res.results


def kernel_unfused(x, norm_mix, norm_ffn, norm_final, even_w_in, even_w_out,
           cmp_pos_k, cmp_w_k, cmp_pos_v, cmp_w_v,
           diff_lq1, diff_lk1, diff_lq2, diff_lk2, diff_subln,
           odd_w_in, odd_w_out, ffn_w_gate, ffn_w_up, ffn_w_down):
    f32 = lambda a: np.ascontiguousarray(np.asarray(a, dtype=np.float32))
    x = f32(x)
    xs = [np.ascontiguousarray(x[c // 2, (c % 2) * NT:(c % 2 + 1) * NT]) for c in range(8)]
    ropes = [rope_tables_T(np.arange((c % 2) * NT, (c % 2 + 1) * NT)) for c in range(8)]
    ftab = make_ftab(SEQ)
    oT_full = None
    out = None
    for layer in range(DEPTH + 1):
        has_prev = layer > 0
        nxt = None if layer == DEPTH else ("even" if layer % 2 == 0 else "odd")
        nc = _prog(("tok", has_prev, nxt), lambda: build_tok(has_prev, nxt))
        common = {}
        if has_prev:
            pl = layer - 1
            w_out = f32(even_w_out[pl // 2]) if pl % 2 == 0 else f32(odd_w_out[pl // 2])
            common["wo"] = lay_wo(w_out)
            common["g2"] = f32(norm_ffn[pl])[None, :]
            common["wgu"] = lay_wgu(f32(ffn_w_gate[pl]), f32(ffn_w_up[pl]))
            common["wd"] = lay_wd(f32(ffn_w_down[pl]))
        if nxt is None:
            common["g1"] = f32(norm_final)[None, :]
        else:
            common["g1"] = f32(norm_mix[layer])[None, :]
            common["win"] = lay_win_even(f32(even_w_in[layer // 2])) if nxt == "even" else lay_win_odd(f32(odd_w_in[layer // 2]))
        in_maps = []
        for c in range(8):
            m = dict(common)
            m["x"] = xs[c]
            if has_prev:
                b, hf = c // 2, c % 2
                m["oT"] = np.ascontiguousarray(oT_full[b][:, hf * NT:(hf + 1) * NT])
            if nxt == "even":
                m["cosT"], m["sinT"] = ropes[c]
            in_maps.append(m)
        r = _run(nc, in_maps)
        if nxt is None:
            out = np.stack([np.concatenate([r[2 * b]["out"], r[2 * b + 1]["out"]], 0) for b in range(BATCH)], 0)
            break
        xs = [np.ascontiguousarray(r[c]["xo"]) for c in range(8)]

        def cat(name, b, axis):
            return np.concatenate([r[2 * b][name], r[2 * b + 1][name]], axis)

        in_maps = []
        if nxt == "even":
            e = layer // 2
            lam_init = 0.8 - 0.6 * math.exp(-0.3 * layer)
            cw = {
                "wck": np.ascontiguousarray(f32(cmp_w_k[e]).reshape(32, 64, 64).transpose(1, 0, 2)),
                "wcv": np.ascontiguousarray(f32(cmp_w_v[e]).reshape(32, 64, 64).transpose(1, 0, 2)),
                "pkT": np.ascontiguousarray(f32(cmp_pos_k[e]).T),
                "pvT": np.ascontiguousarray(f32(cmp_pos_v[e]).T),
                "ftab": ftab,
                "lqk": np.ascontiguousarray(np.stack([f32(diff_lq1[e]), f32(diff_lk1[e]), f32(diff_lq2[e]),
                                                      f32(diff_lk2[e])], 0).reshape(1, 256)),
                "lami": np.array([[lam_init, 1.0 - lam_init]], np.float32),
                "subln": f32(diff_subln[e])[None, :],
            }
            nca = _prog(("att_even",), lambda: build_att_even(SEQ))
            for b in range(BATCH):
                nqT, kcT, vcT = cat("nqT", b, 2), cat("kcT", b, 2), cat("vcT", b, 2)
                ksT, kwT = cat("ksT", b, 2), cat("kwT", b, 2)
                vs, vw, gl = cat("vs", b, 0), cat("vw", b, 0), cat("gl", b, 0)
                dqT, dkT, dv = cat("dqT", b, 2), cat("dkT", b, 2), cat("dv", b, 0)
                for g in range(2):
                    m = dict(cw)
                    m["nqT"] = np.ascontiguousarray(nqT[4 * g:4 * g + 4])
                    m["kcT"] = np.ascontiguousarray(kcT[g])
                    m["vcT"] = np.ascontiguousarray(vcT[g])
                    m["ksT"] = np.ascontiguousarray(ksT[g])
                    m["kwT"] = np.ascontiguousarray(kwT[g])
                    m["vs"] = _pm(vs[:, 64 * g:64 * g + 64])
                    m["vw"] = _pm(vw[:, 64 * g:64 * g + 64])
                    m["gl"] = _pm(gl[:, 12 * g:12 * g + 12])
                    m["dqT"] = np.ascontiguousarray(dqT[4 * g:4 * g + 4])
                    m["dkT"] = np.ascontiguousarray(dkT[4 * g:4 * g + 4])
                    m["dv"] = _pm(dv[:, 256 * g:256 * g + 256])
                    in_maps.append(m)
            ra = _run(nca, in_maps)
            oT_full = []
            for b in range(BATCH):
                o0, o1 = ra[2 * b]["oT"], ra[2 * b + 1]["oT"]
                oT_full.append(np.concatenate([o0[0:256], o1[0:256], o0[256:512], o1[256:512]], 0))
        else:
            nca = _prog(("att_odd",), lambda: build_att_odd(SEQ, 2))
            for b in range(BATCH):
                qT, kT, v = cat("qT", b, 2), cat("kT", b, 2), cat("v", b, 0)
                for hh in range(2):
                    in_maps.append({
                        "qT": np.ascontiguousarray(qT[8 * hh:8 * hh + 8]),
                        "kT": np.ascontiguousarray(kT[8 * hh:8 * hh + 8]),
                        "v": _pm(v[:, 512 * hh:512 * hh + 512]),
                    })
            ra = _run(nca, in_maps)
            oT_full = [np.concatenate([ra[2 * b]["oT"], ra[2 * b + 1]["oT"]], 0) for b in range(BATCH)]
    return out.astype(np.float32)


def build_inproj(p, nxt, S=SEQ):
    GT = 512
    NGRP = S // GT
    NQ = S // 128
    halls = [p.dram(f"hall{i}", [256, 8 * 1024], BF16, "Internal") for i in range(NT // 1024)]
    hvs = [h_[:, :].rearrange("(r k) (c t) -> r k c t", r=2, c=8) for h_ in halls]
    if nxt == "even":
        NCW = 13
        T = {
            "nqT": p.dram("nqT", [4, 64, S], BF16, "Internal"), "kcT": p.dram("kcT", [64, S], BF16, "Internal"),
            "vcT": p.dram("vcT", [64, S], BF16, "Internal"), "ksT": p.dram("ksT", [64, S], BF16, "Internal"),
            "kwT": p.dram("kwT", [64, S], BF16, "Internal"), "vs": p.dram("vs", [128, NQ, 64], BF16, "Internal"),
            "vw": p.dram("vw", [128, NQ, 64], BF16, "Internal"), "gl": p.dram("gl", [128, NQ, 12], F32, "Internal"),
            "dqT": p.dram("dqT", [4, 64, S], BF16, "Internal"), "dkT": p.dram("dkT", [4, 64, S], BF16, "Internal"),
            "dv": p.dram("dv", [128, NQ, 256], BF16, "Internal"),
        }
        cos_d = p.dram("cosT", [128, S], F32, "ExternalInput")
        sin_d = p.dram("sinT", [128, S], F32, "ExternalInput")
        kinds = [("rope", 0.125), ("rope", 0.125), ("rope", 1.0), ("rope", 1.0), ("fm", 1.0), ("tm", 1.0),
                 ("tmf", 1.0), ("rope", 0.125), ("rope", 0.125), ("rope", 1.0), ("rope", 1.0), ("tm", 1.0),
                 ("tm", 1.0)]
    else:
        NCW = 12
        T = {"qT": p.dram("qT", [8, 64, S], BF16, "Internal"), "kT": p.dram("kT", [8, 64, S], BF16, "Internal"),
             "v": p.dram("v", [128, NQ, 512], BF16, "Internal")}
        kinds = [("fm", 0.125)] * 4 + [("fm", 1.0)] * 4 + [("tm", 1.0)] * 4
    win_d = p.dram("win", [NCW, 128, 8, 128], F32, "ExternalInput")
    ps = [p.psum(f"ps{i}", [128, 512], F32) for i in range(8)]
    psi = [0]

    def next_ps():
        t = ps[psi[0] % 8]
        psi[0] += 1
        return t

    st_r = p.ring("ipst", 2, [128, 8, 128], F32)
    wb = [p.sbuf(f"ipw{i}", [128, 8, 128], BF16) for i in range(NCW)]
    wr = {}
    for ci, (kind, scale) in enumerate(kinds):
        st = st_r.next()
        p.dma(st[:], win_d[ci])
        if scale == 1.0:
            p.copy("act" if ci % 2 == 0 else "dve", wb[ci][:], st[:])
        else:
            p.ts("dve", wb[ci][:], st[:], scale, None, ALU.mult)
        if kind == "rope":
            wr[ci] = p.sbuf(f"ipr{ci}", [128, 8, 128], BF16)
            sv = st[:].rearrange("p c (h f d) -> p (c h) f d", h=2, f=2, d=32)
            wv = wr[ci][:].rearrange("p c (h f d) -> p (c h) f d", h=2, f=2, d=32)
            p.ts("pool", wv[:, :, 0, :], sv[:, :, 1, :], -scale, None, ALU.mult)
            p.ts("pool", wv[:, :, 1, :], sv[:, :, 0, :], scale, None, ALU.mult)
    hT_r = p.ring("iphT", 2, [128, 8, GT], BF16)
    fo_r = p.ring("ipfo", 3, [128, GT], BF16)
    tmo_r = p.ring("iptmo", 2, [128, 4, 128], BF16)
    tmf_r = p.ring("iptmf", 2, [128, 4, 128], F32)
    if nxt == "even":
        cos_r = p.ring("ipcos", 2, [128, GT], F32)
        sin_r = p.ring("ipsin", 2, [128, GT], F32)
        t1_r = p.ring("ipt1", 2, [128, GT], F32)
        t2_r = p.ring("ipt2", 2, [128, GT], F32)
    for tg in range(NGRP):
        r, tl0 = divmod(tg * GT, NT)
        t0 = tg * GT
        kb0 = tg * 4
        hT = hT_r.next()
        pi, off = divmod(tl0, 1024)
        p.dma(hT[:], hvs[pi][r, :, :, off:off + GT])
        if nxt == "even":
            cosg = cos_r.next()
            sing = sin_r.next()
            p.dma(cosg[:], cos_d[:, t0:t0 + GT])
            p.dma(sing[:], sin_d[:, t0:t0 + GT])
        for ci, (kind, scale) in enumerate(kinds):
            w = wb[ci]
            if kind in ("fm", "rope"):
                A = next_ps()
                for c in range(8):
                    p.mm(A[:], w[:, c, :], hT[:, c, :], c == 0, c == 7)
                fo = fo_r.next()
                if kind == "rope":
                    Bp = next_ps()
                    for c in range(8):
                        p.mm(Bp[:], wr[ci][:, c, :], hT[:, c, :], c == 0, c == 7)
                    t1 = t1_r.next()
                    t2 = t2_r.next()
                    p.tt("dve", t1[:], A[:], cosg[:], ALU.mult)
                    p.tt("dve", t2[:], Bp[:], sing[:], ALU.mult)
                    p.tt("dve", fo[:], t1[:], t2[:], ALU.add)
                else:
                    p.copy("act", fo[:], A[:])
                ts_ = slice(t0, t0 + GT)
                if nxt == "even":
                    if ci in (0, 1):
                        dsts = [(T["nqT"][2 * ci:2 * ci + 2, :, ts_].rearrange("h d t -> (h d) t"), fo[:])]
                    elif ci == 2:
                        dsts = [(T["kcT"][:, ts_], fo[0:64, :]), (T["ksT"][:, ts_], fo[64:128, :])]
                    elif ci == 3:
                        dsts = [(T["kwT"][:, ts_], fo[0:64, :])]
                    elif ci == 4:
                        dsts = [(T["vcT"][:, ts_], fo[0:64, :])]
                    elif ci in (7, 8):
                        j = ci - 7
                        dsts = [(T["dqT"][2 * j:2 * j + 2, :, ts_].rearrange("h d t -> (h d) t"), fo[:])]
                    else:
                        j = ci - 9
                        dsts = [(T["dkT"][2 * j:2 * j + 2, :, ts_].rearrange("h d t -> (h d) t"), fo[:])]
                else:
                    nm = "qT" if ci < 4 else "kT"
                    j = ci % 4
                    dsts = [(T[nm][2 * j:2 * j + 2, :, ts_].rearrange("h d t -> (h d) t"), fo[:])]
                for (dv_, sv_) in dsts:
                    p.dma(dv_, sv_, eng="pool")
            else:
                A = next_ps()
                Av = A[:].rearrange("p (i n) -> p i n", i=4)
                for i in range(4):
                    for c in range(8):
                        p.mm(Av[:, i, :], hT[:, c, i * 128:(i + 1) * 128], w[:, c, :], c == 0, c == 7)
                to = tmo_r.next() if kind == "tm" else tmf_r.next()
                p.copy("act", to[:], Av)
                ks_ = slice(kb0, kb0 + 4)
                if nxt == "even":
                    if ci == 5:
                        dsts = [(T["vs"][:, ks_, :], to[:, :, 0:64]), (T["vw"][:, ks_, :], to[:, :, 64:128])]
                    elif ci == 6:
                        dsts = [(T["gl"][:, ks_, :], to[:, :, 0:12])]
                    else:
                        j = ci - 11
                        dsts = [(T["dv"][:, ks_, j * 128:(j + 1) * 128], to[:])]
                else:
                    j = ci - 8
                    dsts = [(T["v"][:, ks_, j * 128:(j + 1) * 128], to[:])]
                for (dv_, sv_) in dsts:
                    p.dma(dv_, sv_, eng="pool")
    return T


PAIRS = [[0, 1], [2, 3], [4, 5], [6, 7]]


class RowSplit:
    def __init__(self, a, b, n):
        self.a, self.b, self.n = a, b, n

    def __getitem__(self, idx):
        rs, cs = idx
        if rs.start >= self.n:
            return self.b[rs.start - self.n:rs.stop - self.n, cs]
        return self.a[rs, cs]


def build_fused():
    p = Prog()
    p.enable_arena(200 * 1024)
    ext = lambda name, shape, dt=F32: p.dram_real(name, shape, dt, "ExternalInput")
    x_in = ext("x", [NT, D_MODEL])
    out_d = p.dram_real("out", [NT, D_MODEL], F32, "ExternalOutput")
    xbuf = p.dram_real("xbuf", [NT, D_MODEL], F32, "Internal")
    hsend = [p.dram_real(f"hsend{i}", [128, 8, 1024], BF16, "Internal") for i in range(NT // 1024)]
    hall = [p.dram_real(f"hall{i}", [256, 8 * 1024], BF16, "Internal") for i in range(NT // 1024)]
    osend = p.dram_real("osend", [256, SEQ], BF16, "Internal")
    osendb = p.dram_real("osendb", [256, SEQ], BF16, "Internal")
    oall = p.dram_real("oall", [512, SEQ], BF16, "Internal")
    oallb = p.dram_real("oallb", [512, SEQ], BF16, "Internal")
    cosT = ext("cosT", [128, SEQ])
    sinT = ext("sinT", [128, SEQ])
    ftab = ext("ftab", [128, SEQ // 128, SEQ // 64])
    g1 = [ext(f"g1_{l}", [1, D_MODEL]) for l in range(DEPTH + 1)]
    win = [ext(f"win_{l}", [13 if l % 2 == 0 else 12, 128, 8, 128]) for l in range(DEPTH)]
    wo = [ext(f"wo_{l}", [8, 128, D_MODEL]) for l in range(DEPTH)]
    g2 = [ext(f"g2_{l}", [1, D_MODEL]) for l in range(DEPTH)]
    wgu = [ext(f"wgu_{l}", [NFC, 128, 2, 8, 128]) for l in range(DEPTH)]
    wd = [ext(f"wd_{l}", [NFC, 128, D_MODEL]) for l in range(DEPTH)]
    ev = {}
    for e in range(2):
        ev[e] = {"wck": ext(f"wck_{e}", [64, 32, 64]), "wcv": ext(f"wcv_{e}", [64, 32, 64]),
                 "pkT": ext(f"pkT_{e}", [64, 32]), "pvT": ext(f"pvT_{e}", [64, 32]),
                 "lqk": ext(f"lqk_{e}", [1, 256]), "lami": ext(f"lami_{e}", [1, 2]),
                 "subln": ext(f"subln_{e}", [1, 128])}
    scratch = {}
    for layer in range(DEPTH + 1):
        has_prev = layer > 0
        nxt = None if layer == DEPTH else ("even" if layer % 2 == 0 else "odd")
        ov = {"x": x_in if layer == 0 else xbuf, "xo": xbuf, "out": out_d, "g1": g1[layer],
              "hsend0": hsend[0], "hsend1": hsend[1], "oT": oall, "oTb": oallb}
        if has_prev:
            ov.update({"wo": wo[layer - 1], "g2": g2[layer - 1], "wgu": wgu[layer - 1], "wd": wd[layer - 1]})
        p.dram_override = ov
        gather_h = lambda i: p.collective("AllGather", hall[i].all(),
                                          hsend[i][:, :, :].rearrange("k c t -> k (c t)"), PAIRS)
        build_tok2(p, has_prev, nxt, pass_hook=gather_h if nxt is not None else None)
        if nxt is None:
            break
        p.phase_reset()
        ov = dict(scratch)
        ov.update({"hall0": hall[0], "hall1": hall[1], "win": win[layer], "cosT": cosT, "sinT": sinT})
        p.dram_override = ov
        T = build_inproj(p, nxt)
        scratch.update(T)
        p.phase_reset()
        ov = dict(scratch)
        ov["oT"] = RowSplit(osend, osendb, 256)
        if nxt == "even":
            ov.update(ev[layer // 2])
            ov["ftab"] = ftab
        first_ag = lambda: p.collective("AllGather", oall.all(), osend.all(), PAIRS)
        if nxt == "even":
            p.dram_override = ov
            build_att_even(SEQ, p=p, mid_hook=first_ag)
        else:
            p.dram_override = ov
            build_att_odd(SEQ, 2, p=p, mid_hook=first_ag)
        p.collective("AllGather", oallb.all(), osendb.all(), PAIRS)
        p.phase_reset()
    return p


def lay_win_even_core(w, g):
    z = lambda a: np.pad(a, ((0, 0), (0, 128 - a.shape[1])))
    c = [w[:, 256 * g:256 * g + 128], w[:, 256 * g + 128:256 * g + 256],
         np.concatenate([w[:, 512 + 64 * g:576 + 64 * g], w[:, 768 + 64 * g:832 + 64 * g]], 1),
         z(w[:, 1024 + 64 * g:1088 + 64 * g]), z(w[:, 640 + 64 * g:704 + 64 * g]),
         np.concatenate([w[:, 896 + 64 * g:960 + 64 * g], w[:, 1152 + 64 * g:1216 + 64 * g]], 1),
         z(w[:, 1280 + 12 * g:1292 + 12 * g]),
         w[:, 1304 + 256 * g:1304 + 256 * g + 128], w[:, 1304 + 256 * g + 128:1304 + 256 * g + 256],
         w[:, 1816 + 256 * g:1816 + 256 * g + 128], w[:, 1816 + 256 * g + 128:1816 + 256 * g + 256],
         w[:, 2328 + 256 * g:2328 + 256 * g + 128], w[:, 2328 + 256 * g + 128:2328 + 256 * g + 256]]
    wp = np.stack(c, 0)
    return np.ascontiguousarray(wp.reshape(13, 8, 128, 128).transpose(0, 2, 1, 3))


def lay_win_odd_core(w, hh):
    c = []
    for base in (0, 1024, 2048):
        for j in range(4):
            a = base + 512 * hh + 128 * j
            c.append(w[:, a:a + 128])
    wp = np.stack(c, 0)
    return np.ascontiguousarray(wp.reshape(12, 8, 128, 128).transpose(0, 2, 1, 3))


def kernel_fused(x, norm_mix, norm_ffn, norm_final, even_w_in, even_w_out,
                 cmp_pos_k, cmp_w_k, cmp_pos_v, cmp_w_v,
                 diff_lq1, diff_lk1, diff_lq2, diff_lk2, diff_subln,
                 odd_w_in, odd_w_out, ffn_w_gate, ffn_w_up, ffn_w_down):
    f32 = lambda a: np.ascontiguousarray(np.asarray(a, dtype=np.float32))
    x = f32(x)
    nc = _prog(("fused",), build_fused)
    cosT, sinT = rope_tables_T(np.arange(SEQ))
    common = {"cosT": cosT, "sinT": sinT, "ftab": make_ftab(SEQ)}
    for l in range(DEPTH):
        common[f"g1_{l}"] = f32(norm_mix[l])[None, :]
        w_out = f32(even_w_out[l // 2]) if l % 2 == 0 else f32(odd_w_out[l // 2])
        if l % 2 == 1:
            w_out = np.concatenate([w_out[0:256], w_out[512:768], w_out[256:512], w_out[768:1024]], 0)
        common[f"wo_{l}"] = lay_wo(w_out)
        common[f"g2_{l}"] = f32(norm_ffn[l])[None, :]
        common[f"wgu_{l}"] = lay_wgu(f32(ffn_w_gate[l]), f32(ffn_w_up[l]))
        common[f"wd_{l}"] = lay_wd(f32(ffn_w_down[l]))
    common[f"g1_{DEPTH}"] = f32(norm_final)[None, :]
    for e in range(2):
        lam_init = 0.8 - 0.6 * math.exp(-0.3 * (2 * e))
        common[f"wck_{e}"] = np.ascontiguousarray(f32(cmp_w_k[e]).reshape(32, 64, 64).transpose(1, 0, 2))
        common[f"wcv_{e}"] = np.ascontiguousarray(f32(cmp_w_v[e]).reshape(32, 64, 64).transpose(1, 0, 2))
        common[f"pkT_{e}"] = np.ascontiguousarray(f32(cmp_pos_k[e]).T)
        common[f"pvT_{e}"] = np.ascontiguousarray(f32(cmp_pos_v[e]).T)
        common[f"lqk_{e}"] = np.ascontiguousarray(np.stack([f32(diff_lq1[e]), f32(diff_lk1[e]), f32(diff_lq2[e]),
                                                            f32(diff_lk2[e])], 0).reshape(1, 256))
        common[f"lami_{e}"] = np.array([[lam_init, 1.0 - lam_init]], np.float32)
        common[f"subln_{e}"] = f32(diff_subln[e])[None, :]
    wins = {}
    for par in range(2):
        for l in range(DEPTH):
            wins[(par, l)] = (lay_win_even_core(f32(even_w_in[l // 2]), par) if l % 2 == 0
                              else lay_win_odd_core(f32(odd_w_in[l // 2]), par))
    in_maps = []
    for c in range(8):
        m = dict(common)
        m["x"] = np.ascontiguousarray(x[c // 2, (c % 2) * NT:(c % 2 + 1) * NT])
        for l in range(DEPTH):
            m[f"win_{l}"] = wins[(c % 2, l)]
        in_maps.append(m)
    r = _run(nc, in_maps)
    out = np.stack([np.concatenate([r[2 * b]["out"], r[2 * b + 1]["out"]], 0) for b in range(BATCH)], 0)
    return out.astype(np.float32)


def kernel(**inputs):
    return kernel_fused(**inputs)


def build_tok2(p, has_prev, nxt, pass_hook=None):
    PT = 1024
    NPASS = NT // PT
    TPP = PT // 128
    NH2 = NFC // 2
    x_d = p.dram("x", [NT, D_MODEL], F32, "ExternalInput")
    g1_d = p.dram("g1", [1, D_MODEL], F32, "ExternalInput")
    if getattr(p, "par", None) is None:
        p.par = p.nc.partition_id() % 2
    par = p.par
    if has_prev:
        oT_d = p.dram("oT", [512, 2 * NT], BF16, "Internal")
        oTb_d = p.dram("oTb", [512, 2 * NT], BF16, "Internal")
        wo_d = p.dram("wo", [8, 128, D_MODEL], F32, "ExternalInput")
        g2_d = p.dram("g2", [1, D_MODEL], F32, "ExternalInput")
        wgu_d = p.dram("wgu", [NFC, 128, 2, 8, 128], F32, "ExternalInput")
        wd_d = p.dram("wd", [NFC, 128, D_MODEL], F32, "ExternalInput")
    if nxt is not None:
        xo_d = p.dram("xo", [NT, D_MODEL], F32, "Internal")
        hs_d = [p.dram(f"hsend{i}", [128, 8, PT], BF16, "Internal") for i in range(NPASS)]
    else:
        out_d = p.dram("out", [NT, D_MODEL], F32, "ExternalOutput")
    ident = make_ident(p)
    xg = p.sbuf("xg", [128, TPP, D_MODEL], F32)
    hT = p.sbuf("hT", [128, 8, PT], BF16)
    hs_r = p.ring("hs", 2, [128, D_MODEL], BF16)
    junk = p.sbuf("junk", [128, D_MODEL], F32)
    ss_r = p.ring("ss", 2, [128, 1], F32)
    gB1 = p.sbuf("gB1", [128, D_MODEL], F32)
    p.dma(gB1[:], g1_d[0:1, :].to_broadcast([128, D_MODEL]))
    ps = [p.psum(f"ps{i}", [128, 512], F32) for i in range(8)]
    psi = [0]

    def next_ps():
        t = ps[psi[0] % 8]
        psi[0] += 1
        return t

    if has_prev:
        st_r = p.ring("st", 3, [128, 2, 8, 128], F32)
        wb_r = p.ring("wb", 3, [128, 2, 8, 128], BF16)
        gB2 = p.sbuf("gB2", [128, D_MODEL], F32)
        p.dma(gB2[:], g2_d[0:1, :].to_broadcast([128, D_MODEL]))
        wo = p.sbuf("wo_sb", [128, 8, D_MODEL], BF16)
        for c in range(8):
            st = st_r.next()
            stv = st[:].rearrange("p a b c -> p (a b c)")[:, 0:D_MODEL]
            p.dma(stv, wo_d[c])
            p.copy("dve", wo[:, c, :], stv)
        oTg_r = p.ring("oTg", 2, [128, 8, 512], BF16)
        aT = p.sbuf("aT", [128, NH2, PT], BF16)
        wdr = p.sbuf("wdr", [128, NH2, D_MODEL], BF16)
        sg_r = p.ring("sg", 2, [128, 512], F32)
    else:
        og_r = None
    og_r = p.ring("og", 2, [128, D_MODEL], F32) if nxt is None else None

    def rstd_of(i):
        ss = ss_r.next()
        p.act(junk[:], xg[:, i, :], AF.Square, accum_out=ss[:])
        p.ts("dve", ss[:], ss[:], 1.0 / D_MODEL, NORM_EPS, ALU.mult, ALU.add)
        p.act(ss[:], ss[:], AF.Ln)
        p.act(ss[:], ss[:], AF.Exp, scale=-0.5)
        return ss

    def rmsnorm_to_hT(gB):
        for i in range(TPP):
            rs = rstd_of(i)
            hs = hs_r.next()
            p.stt("dve", hs[:], xg[:, i, :], rs[:], gB[:], ALU.mult, ALU.mult)
            pt = next_ps()
            ptv = pt[:].bitcast(BF16).rearrange("p (c t) -> p c t", c=8)
            for c in range(8):
                p.tr(ptv[:, c, :], hs[:, c * 128:(c + 1) * 128], ident[:])
            p.copy("dve", hT[:, :, i * 128:(i + 1) * 128], ptv)

    for ps_ in range(NPASS):
        t0 = ps_ * PT
        p.dma(xg[:], x_d[t0:t0 + PT, :].rearrange("(i p) d -> p i d", p=128))
        if has_prev:
            for sub in range(PT // 512):
                oTg = oTg_r.next()
                tt0 = t0 + sub * 512
                p.dma(oTg[:, 0:4, :], oT_d[:, bass.ds(par * NT + tt0, 512)].rearrange("(c f) t -> f c t", f=128))
                p.dma(oTg[:, 4:8, :], oTb_d[:, bass.ds(par * NT + tt0, 512)].rearrange("(c f) t -> f c t", f=128))
                for i4 in range(4):
                    i = sub * 4 + i4
                    for hf in range(2):
                        acc = next_ps()
                        for c in range(8):
                            p.mm(acc[:], oTg[:, c, i4 * 128:(i4 + 1) * 128], wo[:, c, hf * 512:(hf + 1) * 512],
                                 c == 0, c == 7)
                        xs = xg[:, i, hf * 512:(hf + 1) * 512]
                        p.tt("dve", xs, xs, acc[:], ALU.add)
            rmsnorm_to_hT(gB2)
            for hh in range(2):
                for n in range(NH2):
                    nn = hh * NH2 + n
                    st = st_r.next()
                    wb = wb_r.next()
                    p.dma(st[:], wgu_d[nn])
                    p.copy("act", wb[:], st[:])
                    st2 = st_r.next()
                    stv = st2[:].rearrange("p a b c -> p (a b c)")[:, 0:D_MODEL]
                    p.dma(stv, wd_d[nn])
                    p.copy("dve", wdr[:, n, :], stv)
                    for sub in range(PT // 512):
                        G = next_ps()
                        Uu = next_ps()
                        for c in range(8):
                            p.mm(G[:], wb[:, 0, c, :], hT[:, c, sub * 512:(sub + 1) * 512], c == 0, c == 7)
                        for c in range(8):
                            p.mm(Uu[:], wb[:, 1, c, :], hT[:, c, sub * 512:(sub + 1) * 512], c == 0, c == 7)
                        sg = sg_r.next()
                        p.act(sg[:], G[:], AF.Silu)
                        p.tt("dve", aT[:, n, sub * 512:(sub + 1) * 512], sg[:], Uu[:], ALU.mult)
                for i in range(TPP):
                    for hf in range(2):
                        acc = next_ps()
                        for n in range(NH2):
                            p.mm(acc[:], aT[:, n, i * 128:(i + 1) * 128], wdr[:, n, hf * 512:(hf + 1) * 512],
                                 n == 0, n == NH2 - 1)
                        xs = xg[:, i, hf * 512:(hf + 1) * 512]
                        p.tt("dve", xs, xs, acc[:], ALU.add)
        if nxt is None:
            for i in range(TPP):
                rs = rstd_of(i)
                og = og_r.next()
                p.stt("dve", og[:], xg[:, i, :], rs[:], gB1[:], ALU.mult, ALU.mult)
                p.dma(out_d[t0 + i * 128:t0 + (i + 1) * 128, :], og[:], eng="pool")
            continue
        p.dma(xo_d[t0:t0 + PT, :].rearrange("(i p) d -> p i d", p=128), xg[:], eng="pool")
        rmsnorm_to_hT(gB1)
        p.dma(hs_d[ps_][:, :, :], hT[:], eng="pool")
        if pass_hook is not None:
            pass_hook(ps_)
    return p
```
